# Optimizing a Trainium2 kernel written in Bass

```python
import math
import jax, jax.numpy as jnp
from jax import lax
import numpy as np

D_MODEL = 2048
BATCH = 4
SEQ = 4096
DEPTH = 4

CHUNK = 64
Q_BLOCK = 128
HEAD_DIM = 128
FOX_HEADS = 8
FOX_WIDTH = FOX_HEADS * HEAD_DIM
SGU_GROUPS = 8
SGU_WIDTH = SGU_GROUPS * HEAD_DIM
SGU_GROUP_DIM = SGU_WIDTH // SGU_GROUPS
SGU_SPAN = 128
GDN_HEADS = 8
GDN_WIDTH = GDN_HEADS * HEAD_DIM
GDN_CONV = 4
N_BRANCH = 3
D_FF = 5504
FFN_CONV = 3
DEEPNORM_ALPHA = (2 * DEPTH) ** 0.25
DEEPNORM_BETA = (8 * DEPTH) ** -0.25
LN_EPS = 1e-5
RMS_EPS = 1e-6

IN_SIZES = (3 * FOX_WIDTH,
            FOX_HEADS,
            2 * SGU_WIDTH,
            3 * GDN_WIDTH,
            GDN_HEADS,
            GDN_HEADS,
            GDN_WIDTH,
            N_BRANCH * D_MODEL)
IN_OFFSETS = tuple(sum(IN_SIZES[:i]) for i in range(1, len(IN_SIZES)))
N_IN = sum(IN_SIZES)
FOX_F_OFFSET = 3 * FOX_WIDTH

kernel_name = "hybrid_fox_sgu_gdn_convffn_block"


def layer_norm(x, g, b):
    xf = x.astype(jnp.float32)
    mu = jnp.mean(xf, axis=-1, keepdims=True)
    var = jnp.mean(jnp.square(xf - mu), axis=-1, keepdims=True)
    return ((xf - mu) * lax.rsqrt(var + LN_EPS) * g + b).astype(x.dtype)


def causal_depthwise_conv(x, w):
    k = w.shape[0]
    c = x.shape[-1]
    return lax.conv_general_dilated(
        x, w[:, None, :].astype(x.dtype), window_strides=(1,),
        padding=[(k - 1, 0)], dimension_numbers=('NWC', 'WIO', 'NWC'),
        feature_group_count=c)


def forgetting_attention(q, k, v, log_f):
    b, s, h, dh = q.shape
    nb = s // Q_BLOCK
    c = jnp.cumsum(log_f, axis=1).transpose(0, 2, 1)
    qb = q.reshape(b, nb, Q_BLOCK, h, dh).transpose(1, 0, 3, 2, 4)
    cqb = c.reshape(b, h, nb, Q_BLOCK).transpose(2, 0, 1, 3)
    kpos = jnp.arange(s)
    scale = dh ** -0.5

    def block(args):
        qi, cqi, i = args
        logits = jnp.einsum('bhqd,bkhd->bhqk', qi, k).astype(jnp.float32) * scale
        logits = logits + cqi[..., None] - c[:, :, None, :]
        qpos = i * Q_BLOCK + jnp.arange(Q_BLOCK)
        mask = qpos[:, None] >= kpos[None, :]
        p = jax.nn.softmax(jnp.where(mask, logits, -jnp.inf), axis=-1)
        return jnp.einsum('bhqk,bkhd->bqhd', p.astype(v.dtype), v)

    out = lax.map(block, (qb, cqb, jnp.arange(nb)))
    return out.transpose(1, 0, 2, 3, 4).reshape(b, s, h * dh)


def spatial_gating(u, v, ln_g, ln_b, w_s, b_s):
    b, s, w = v.shape
    n = s // SGU_SPAN
    shape5 = (b, n, SGU_SPAN, SGU_GROUPS, SGU_GROUP_DIM)
    vg = layer_norm(v.reshape(shape5), ln_g.reshape(SGU_GROUPS, SGU_GROUP_DIM),
                    ln_b.reshape(SGU_GROUPS, SGU_GROUP_DIM))
    pos = jnp.arange(SGU_SPAN) // CHUNK
    mask = pos[:, None] >= pos[None, :]
    mixed = jnp.einsum('gts,bnsgc->bntgc', jnp.where(mask, w_s, 0.0).astype(vg.dtype), vg)
    mixed = mixed + b_s.T[:, :, None]
    return (u.reshape(shape5) * mixed).reshape(b, s, w)


def l2_normalize(t):
    return t * lax.rsqrt(jnp.sum(jnp.square(t), axis=-1, keepdims=True) + RMS_EPS)


def gated_delta_rule(q, k, v, g, beta):
    b, s, h, dh = q.shape
    n = s // CHUNK
    to_c = lambda t: t.transpose(0, 2, 1, 3).reshape(b, h, n, CHUNK, t.shape[-1])
    q, k, v = to_c(q) * dh ** -0.5, to_c(k), to_c(v)
    g = g.transpose(0, 2, 1).reshape(b, h, n, CHUNK)
    beta = beta.transpose(0, 2, 1).reshape(b, h, n, CHUNK)
    gc = jnp.cumsum(g, axis=-1)
    causal = jnp.tril(jnp.ones((CHUNK, CHUNK), bool))
    strict = jnp.tril(jnp.ones((CHUNK, CHUNK), bool), -1)
    decay = jnp.exp(jnp.where(causal, gc[..., :, None] - gc[..., None, :], -jnp.inf))
    k_beta = k * beta[..., None]
    a_kk = jnp.where(strict, jnp.einsum('bhnid,bhnjd->bhnij', k_beta, k) * decay, 0.0)
    eye = jnp.eye(CHUNK, dtype=q.dtype)
    rhs = jnp.concatenate([v * beta[..., None], k_beta * jnp.exp(gc)[..., None]], axis=-1)
    sol = lax.linalg.triangular_solve(eye + a_kk, rhs, left_side=True, lower=True)
    u, w = sol[..., :dh], sol[..., dh:]
    qk = jnp.where(causal, jnp.einsum('bhnid,bhnjd->bhnij', q, k) * decay, 0.0)
    g_last = gc[..., -1]
    k_dec = k * jnp.exp(g_last[..., None] - gc)[..., None]
    q_dec = q * jnp.exp(gc)[..., None]

    def step(state, xs):
        qd, kd, qk_i, u_i, w_i, gl = xs
        v_new = u_i - jnp.einsum('bhcd,bhde->bhce', w_i, state)
        o = jnp.einsum('bhcd,bhde->bhce', qd, state) + jnp.einsum('bhij,bhje->bhie', qk_i, v_new)
        state = state * jnp.exp(gl)[..., None, None] + jnp.einsum('bhcd,bhce->bhde', kd, v_new)
        return state, o

    xs = tuple(jnp.moveaxis(t, 2, 0) for t in (q_dec, k_dec, qk, u, w, g_last))
    state0 = jnp.zeros((b, h, dh, dh), jnp.float32)
    _, o = lax.scan(step, state0, xs)
    return jnp.moveaxis(o, 0, 2).reshape(b, h, s, dh).transpose(0, 2, 1, 3)


def gdn_mixer(qkv, a, beta_logit, gate, conv_w, a_log, dt_bias, norm_g):
    b, s, _ = qkv.shape
    qkv = jax.nn.silu(causal_depthwise_conv(qkv, conv_w)).astype(jnp.float32)
    q, k, v = [t.reshape(b, s, GDN_HEADS, HEAD_DIM) for t in jnp.split(qkv, 3, axis=-1)]
    q, k = l2_normalize(q), l2_normalize(k)
    g = -jnp.exp(a_log.astype(jnp.float32)) * jax.nn.softplus((a + dt_bias).astype(jnp.float32))
    beta = jax.nn.sigmoid(beta_logit.astype(jnp.float32))
    o = gated_delta_rule(q, k, v, g, beta)
    o = o * lax.rsqrt(jnp.mean(jnp.square(o), axis=-1, keepdims=True) + RMS_EPS) * norm_g
    o = o * jax.nn.silu(gate.reshape(b, s, GDN_HEADS, HEAD_DIM).astype(jnp.float32))
    return o.astype(gate.dtype).reshape(b, s, GDN_WIDTH)


def token_mixing(x, w_in, b_in, sgu_ln_g, sgu_ln_b, sgu_w, sgu_b, gdn_conv_w,
                 gdn_a_log, gdn_dt_bias, gdn_norm_g, w_proj_a, w_proj_b, w_proj_c, w_out):
    bsz, s, _ = x.shape
    proj = x @ w_in + b_in
    fox_qkv, fox_f, sgu_uv, gdn_qkv, gdn_a, gdn_b, gdn_gate, gates = jnp.split(
        proj, list(IN_OFFSETS), axis=-1)
    fq, fk, fv = [t.reshape(bsz, s, FOX_HEADS, HEAD_DIM) for t in jnp.split(fox_qkv, 3, axis=-1)]
    y_a = forgetting_attention(fq, fk, fv, jax.nn.log_sigmoid(fox_f.astype(jnp.float32)))
    su, sv = jnp.split(sgu_uv, 2, axis=-1)
    y_b = spatial_gating(su, sv, sgu_ln_g, sgu_ln_b, sgu_w, sgu_b)
    y_c = gdn_mixer(gdn_qkv, gdn_a, gdn_b, gdn_gate, gdn_conv_w, gdn_a_log, gdn_dt_bias, gdn_norm_g)
    gt = jax.nn.sigmoid(gates).reshape(bsz, s, N_BRANCH, D_MODEL)
    merged = (gt[:, :, 0] * (y_a @ w_proj_a) + gt[:, :, 1] * (y_b @ w_proj_b)
              + gt[:, :, 2] * (y_c @ w_proj_c))
    return merged @ w_out


def conv_ffn(x, w_up, conv_w, conv_b, w_down):
    h = causal_depthwise_conv(x @ w_up, conv_w) + conv_b
    h_gate, h_val = jnp.split(h, 2, axis=-1)
    return (jax.nn.silu(h_gate) * h_val) @ w_down


def setup_inputs(seed: int = 0) -> dict:
    key = jax.random.key(seed)
    ks = jax.random.split(key, 24)
    L, D = DEPTH, D_MODEL
    f32 = jnp.float32
    nrm = lambda k, shape, scale: jax.random.normal(k, shape, f32) * scale
    x = nrm(ks[0], (BATCH, SEQ, D), 1.0)
    w_in = nrm(ks[1], (L, D, N_IN), D ** -0.5)
    b_in = nrm(ks[2], (L, N_IN), 0.01)
    fox_fb = jax.random.uniform(ks[3], (L, FOX_HEADS), f32, 1.0, 5.0)
    b_in = b_in.at[:, FOX_F_OFFSET:FOX_F_OFFSET + FOX_HEADS].add(fox_fb)
    sgu_ln_g = 1.0 + nrm(ks[4], (L, SGU_WIDTH), 0.01)
    sgu_ln_b = nrm(ks[5], (L, SGU_WIDTH), 0.01)
    sgu_w = nrm(ks[6], (L, SGU_GROUPS, SGU_SPAN, SGU_SPAN), SGU_SPAN ** -0.5)
    sgu_b = 1.0 + nrm(ks[7], (L, SGU_GROUPS, SGU_SPAN), 0.01)
    gdn_conv_w = nrm(ks[8], (L, GDN_CONV, 3 * GDN_WIDTH), GDN_CONV ** -0.5)
    gdn_a_log = jnp.log(jax.random.uniform(ks[9], (L, GDN_HEADS), f32, 1.0, 16.0))
    dt = jnp.exp(jax.random.uniform(ks[10], (L, GDN_HEADS), f32, math.log(1e-3), math.log(1e-1)))
    gdn_dt_bias = dt + jnp.log(-jnp.expm1(-dt))
    gdn_norm_g = 1.0 + nrm(ks[11], (L, HEAD_DIM), 0.01)
    w_proj_a = nrm(ks[12], (L, FOX_WIDTH, D), FOX_WIDTH ** -0.5 * DEEPNORM_BETA)
    w_proj_b = nrm(ks[13], (L, SGU_WIDTH, D), SGU_WIDTH ** -0.5 * DEEPNORM_BETA)
    w_proj_c = nrm(ks[14], (L, GDN_WIDTH, D), GDN_WIDTH ** -0.5 * DEEPNORM_BETA)
    w_out = nrm(ks[15], (L, D, D), D ** -0.5 * DEEPNORM_BETA)
    ln1_g = 1.0 + nrm(ks[16], (L, D), 0.01)
    ln1_b = nrm(ks[17], (L, D), 0.01)
    ffn_w_up = nrm(ks[18], (L, D, 2 * D_FF), D ** -0.5)
    ffn_conv_w = nrm(ks[19], (L, FFN_CONV, 2 * D_FF), FFN_CONV ** -0.5)
    ffn_conv_b = nrm(ks[20], (L, 2 * D_FF), 0.01)
    ffn_w_down = nrm(ks[21], (L, D_FF, D), D_FF ** -0.5 * DEEPNORM_BETA)
    ln2_g = 1.0 + nrm(ks[22], (L, D), 0.01)
    ln2_b = nrm(ks[23], (L, D), 0.01)
    return {"x": x, "w_in": w_in, "b_in": b_in, "sgu_ln_g": sgu_ln_g, "sgu_ln_b": sgu_ln_b,
            "sgu_w": sgu_w, "sgu_b": sgu_b, "gdn_conv_w": gdn_conv_w, "gdn_a_log": gdn_a_log,
            "gdn_dt_bias": gdn_dt_bias, "gdn_norm_g": gdn_norm_g, "w_proj_a": w_proj_a,
            "w_proj_b": w_proj_b, "w_proj_c": w_proj_c, "w_out": w_out, "ln1_g": ln1_g,
            "ln1_b": ln1_b, "ffn_w_up": ffn_w_up, "ffn_conv_w": ffn_conv_w,
            "ffn_conv_b": ffn_conv_b, "ffn_w_down": ffn_w_down, "ln2_g": ln2_g, "ln2_b": ln2_b}


def reference(x, w_in, b_in, sgu_ln_g, sgu_ln_b, sgu_w, sgu_b, gdn_conv_w, gdn_a_log,
              gdn_dt_bias, gdn_norm_g, w_proj_a, w_proj_b, w_proj_c, w_out, ln1_g, ln1_b,
              ffn_w_up, ffn_conv_w, ffn_conv_b, ffn_w_down, ln2_g, ln2_b):
    for l in range(DEPTH):
        mix = token_mixing(x, w_in[l], b_in[l], sgu_ln_g[l], sgu_ln_b[l], sgu_w[l], sgu_b[l],
                           gdn_conv_w[l], gdn_a_log[l], gdn_dt_bias[l], gdn_norm_g[l],
                           w_proj_a[l], w_proj_b[l], w_proj_c[l], w_out[l])
        x = layer_norm(DEEPNORM_ALPHA * x + mix, ln1_g[l], ln1_b[l])
        ffn = conv_ffn(x, ffn_w_up[l], ffn_conv_w[l], ffn_conv_b[l], ffn_w_down[l])
        x = layer_norm(DEEPNORM_ALPHA * x + ffn, ln2_g[l], ln2_b[l])
    return x
```

```python
import numpy as np
from contextlib import ExitStack
import concourse.bass as bass
import concourse.mybir as mybir
from concourse.bass_utils import run_bass_kernel_spmd

F32 = mybir.dt.float32
BF16 = mybir.dt.bfloat16
AF = mybir.ActivationFunctionType
ALU = mybir.AluOpType
AX = mybir.AxisListType

D = 2048
NIN = 15384
DFF = 5504
SEQ = 4096
DEPTH = 4
ALPHA = (2 * DEPTH) ** 0.25
LN_EPS = 1e-5
RMS_EPS = 1e-6
OFF = dict(fq=0, fk=1024, fv=2048, ff=3072, su=3080, sv=4104, gq=5128, gk=6152, gv=7176,
           ga=8200, gb=8208, gg=8216, mg=9240)


class Buf:
    __slots__ = ("name", "w", "r", "excl")

    def __init__(self, name="b"):
        self.name = name
        self.w = {}
        self.r = {}
        self.excl = False


class Tl:
    def __init__(self, t, name):
        self.t = t
        self.b = Buf(name)

    def __getitem__(self, k):
        return self.t[k]


class Prog:
    def __init__(self, nc, es):
        self.nc = nc
        self.es = es
        self.ges = es
        self.engs = {"pe": nc.tensor, "act": nc.scalar, "dve": nc.vector,
                     "pool": nc.gpsimd, "sp": nc.sync}
        self.sems = {}
        self.cnt = {}
        self.waited = {e: {} for e in self.engs}
        self.nwait = 0
        self.nops = 0
        self.uid = 0
        self.ekey = {e: e for e in self.engs}

    def sem(self, key):
        if key not in self.sems:
            self.sems[key] = self.ges.enter_context(self.nc.semaphore("s_" + key))
            self.cnt[key] = 0
        return self.sems[key]

    def _wait(self, eng, deps, own=None):
        for k, c in deps.items():
            if c <= 0:
                continue
            if k == own and eng == "pe":
                continue
            if self.waited[eng].get(k, 0) >= c:
                continue
            assert c <= self.cnt[k], f"forward wait {eng} on {k}: {c} > {self.cnt[k]}"
            self.engs[eng].wait_ge(self.sem(k), c)
            self.waited[eng][k] = c
            self.nwait += 1

    @staticmethod
    def _collect(reads, writes):
        deps = {}
        for t in reads:
            for k, c in t.w.items():
                if deps.get(k, 0) < c:
                    deps[k] = c
        for t in writes:
            for k, c in t.w.items():
                if deps.get(k, 0) < c:
                    deps[k] = c
            for k, c in t.r.items():
                if deps.get(k, 0) < c:
                    deps[k] = c
        return deps

    @staticmethod
    def _bufs(xs):
        return [x.b if hasattr(x, "b") else x for x in xs]

    def op(self, eng, fn, reads=(), writes=(), signal=True):
        reads = self._bufs(reads)
        writes = self._bufs(writes)
        writes = writes + [t for t in reads if t.excl and t not in writes]
        key = self.ekey[eng]
        self.sem(key)
        assert signal or eng == "pe"
        self._wait(eng, self._collect(reads, writes), own=key)
        ins = fn(self.engs[eng])
        self.nops += 1
        if signal:
            self.cnt[key] += 1
            ins.then_inc(self.sems[key], 1)
            o = self.cnt[key]
        else:
            o = self.cnt[key] + 1
        for t in reads:
            t.r[key] = o
        for t in writes:
            t.w[key] = o
        return ins

    def epoch(self, tag):
        self.barrier()
        self.ekey = {e: f"{e}_{tag}" for e in self.engs}

    def dma(self, q, out, in_, reads=(), writes=(), sem=None, **kw):
        reads = self._bufs(reads)
        writes = self._bufs(writes)
        self.sem(sem)
        deps = self._collect(reads, writes)
        deps[sem] = max(deps.get(sem, 0), self.cnt[sem])
        self._wait(q, deps)
        ins = self.engs[q].dma_start(out=out, in_=in_, **kw)
        ins.then_inc(self.sems[sem], 16)
        self.nops += 1
        self.cnt[sem] += 16
        o = self.cnt[sem]
        for t in reads:
            t.r[sem] = o
        for t in writes:
            t.w[sem] = o
        return ins

    def barrier(self):
        for e in self.engs:
            self._wait(e, dict(self.cnt), own=self.ekey[e])

    def sb(self, name, shape, dt):
        self.uid += 1
        nm = f"{name}_{self.uid}"
        return Tl(self.es.enter_context(self.nc.sbuf_tensor(nm, list(shape), dt)), nm)

    def mm(self, out, lhsT, rhs, start, stop, reads, writes, signal=None):
        return self.op("pe", lambda e: e.matmul(out, lhsT=lhsT, rhs=rhs, start=start, stop=stop),
                       reads=reads, writes=writes, signal=(stop if signal is None else signal))

    def tr(self, out, in_, ident, reads, writes, signal=True):
        return self.op("pe", lambda e: e.transpose(out, in_, ident), reads=reads, writes=writes, signal=signal)

    def act(self, out, in_, func, reads, writes, bias=0.0, scale=1.0, accum_out=None):
        if accum_out is None:
            return self.op("act", lambda e: e.activation(out=out, in_=in_, func=func, bias=bias, scale=scale),
                           reads=reads, writes=writes)
        return self.op("act", lambda e: e.activation(out=out, in_=in_, func=func, bias=bias, scale=scale,
                                                      accum_out=accum_out), reads=reads, writes=writes)

    def tt(self, eng, out, in0, in1, op, reads, writes):
        return self.op(eng, lambda e: e.tensor_tensor(out=out, in0=in0, in1=in1, op=op), reads=reads, writes=writes)

    def ts(self, eng, out, in0, s1, s2, op0, op1, reads, writes):
        if op1 is None:
            return self.op(eng, lambda e: e.tensor_scalar(out=out, in0=in0, scalar1=s1, scalar2=None, op0=op0),
                           reads=reads, writes=writes)
        return self.op(eng, lambda e: e.tensor_scalar(out=out, in0=in0, scalar1=s1, scalar2=s2, op0=op0, op1=op1),
                       reads=reads, writes=writes)

    def stt(self, eng, out, in0, scalar, in1, op0, op1, reads, writes):
        return self.op(eng, lambda e: e.scalar_tensor_tensor(out=out, in0=in0, scalar=scalar, in1=in1, op0=op0, op1=op1),
                       reads=reads, writes=writes)

    def cp(self, eng, out, in_, reads, writes):
        if eng == "act":
            return self.op("act", lambda e: e.activation(out=out, in_=in_, func=AF.Identity), reads=reads, writes=writes)
        return self.op(eng, lambda e: e.tensor_copy(out=out, in_=in_), reads=reads, writes=writes)


class Scope:
    def __init__(self, P):
        self.P = P

    def __enter__(self):
        self.old = self.P.es
        self.st = ExitStack()
        self.st.__enter__()
        self.P.es = self.st
        return self

    def __exit__(self, *a):
        self.P.barrier()
        self.P.es = self.old
        return self.st.__exit__(*a)


CN = ["ident", "ones", "triKQ", "triU", "triBD", "half0", "half1", "blk", "lm", "sm", "smT", "inclT", "sguT"]


def make_consts():
    i = np.arange(128)
    ch = i // 64
    same = ch[:, None] == ch[None, :]
    c = {}
    c["ident"] = np.eye(128)
    c["ones"] = np.ones((128, 128))
    c["triKQ"] = (i[None, :] >= i[:, None])
    c["triU"] = (i[:, None] <= i[None, :])
    c["triBD"] = same & (i[:, None] <= i[None, :])
    c["half0"] = np.broadcast_to((i < 64)[:, None], (128, 128))
    c["half1"] = np.broadcast_to((i >= 64)[:, None], (128, 128))
    c["blk"] = same
    c["lm"] = same & (i[:, None] > i[None, :])
    c["sm"] = same & (i[:, None] > i[None, :])
    c["smT"] = same & (i[None, :] > i[:, None])
    c["inclT"] = same & (i[None, :] >= i[:, None])
    c["sguT"] = (ch[None, :] >= ch[:, None])
    return np.ascontiguousarray(np.stack([c[n].astype(np.float32) for n in CN], axis=1))


FM_RANGES = [("fq", 0, 1024), ("fk", 1024, 1024), ("su", 3080, 1024), ("g3", 5128, 3072), ("mg", 9240, 6144)]
TM_RANGES = [("fv", 2048, 1024), ("sv", 4104, 1024), ("gg", 8216, 1024)]
SM_COLS = list(range(3072, 3080)) + list(range(8200, 8216))
NFM = sum(r[2] for r in FM_RANGES) // 128

BC = {}
_o = 0
for _n, _w in [("btm", 3072), ("bsm", 24), ("ln1g", D), ("ln1b", D), ("ln2g", D), ("ln2b", D),
               ("slg", 1024), ("slb", 1024), ("gng", 128), ("alog", 8), ("dtb", 8), ("sgub", 1024)]:
    BC[_n] = (_o, _w)
    _o += _w
BCW = _o
PC = {}
_o = 0
for _n, _w in [("bfm", NFM), ("gcw", 24 * 4), ("fcw", 86 * 3), ("fcb", 86)]:
    PC[_n] = (_o, _w)
    _o += _w
PCW = _o


def layer_small(inp, l):
    bin_ = inp["b_in"][l]
    rows = [np.concatenate([bin_[c0:c0 + w] for _, c0, w in TM_RANGES]), bin_[SM_COLS],
            inp["ln1_g"][l], inp["ln1_b"][l], inp["ln2_g"][l], inp["ln2_b"][l],
            inp["sgu_ln_g"][l], inp["sgu_ln_b"][l], inp["gdn_norm_g"][l], inp["gdn_a_log"][l],
            inp["gdn_dt_bias"][l], inp["sgu_b"][l].reshape(-1)]
    row = np.concatenate(rows).astype(np.float32)
    assert row.shape[0] == BCW
    bc = np.ascontiguousarray(np.broadcast_to(row[None, :], (128, BCW)))
    bfm = np.concatenate([bin_[c0:c0 + w] for _, c0, w in FM_RANGES]).reshape(NFM, 128).T
    gcw = inp["gdn_conv_w"][l].reshape(4, 24, 128).transpose(2, 1, 0).reshape(128, 96)
    fcw = inp["ffn_conv_w"][l].reshape(3, 86, 128).transpose(2, 1, 0).reshape(128, 258)
    fcb = inp["ffn_conv_b"][l].reshape(86, 128).T
    pc = np.ascontiguousarray(np.concatenate([bfm, gcw, fcw, fcb], axis=1).astype(np.float32))
    assert pc.shape[1] == PCW
    sguwT = np.ascontiguousarray(inp["sgu_w"][l].transpose(2, 0, 1))
    return bc, pc, sguwT


class Ctx:
    pass


def build(nl, T, dbg_in=(), dbg_out=(), stages="0ABCDEF"):
    nc = bass.Bass("TRN2", target_bir_lowering=False)
    NB = T // 128
    C = Ctx()
    C.nc, C.T, C.NB = nc, T, NB

    def ext_in(name, shape, dt=F32):
        return nc.dram_tensor(name, list(shape), dt, kind="ExternalInput").ap()

    def scr(name, shape, dt):
        kind = "ExternalInput" if name in dbg_in else ("ExternalOutput" if name in dbg_out else "Internal")
        return nc.dram_tensor(name, list(shape), dt, kind=kind).ap()

    x_in = ext_in("x", [T, D])
    consts = ext_in("consts", [128, len(CN), 128])
    W = []
    for l in range(nl):
        w = Ctx()
        w.w_in = ext_in(f"w_in{l}", [D, NIN])
        w.bc = ext_in(f"bc{l}", [128, BCW])
        w.pc = ext_in(f"pc{l}", [128, PCW])
        w.sguwT = ext_in(f"sguwT{l}", [128, 8, 128])
        w.wpa = ext_in(f"wpa{l}", [1024, D])
        w.wpb = ext_in(f"wpb{l}", [1024, D])
        w.wpc = ext_in(f"wpc{l}", [1024, D])
        w.wout = ext_in(f"wout{l}", [D, D])
        w.wup = ext_in(f"wup{l}", [D, 2 * DFF])
        w.wdn = ext_in(f"wdn{l}", [DFF, D])
        W.append(w)
    y_out = nc.dram_tensor("y", [T, D], F32, kind="ExternalOutput").ap()

    S = Ctx()
    S.xr1 = scr("xr1", [T, D], F32)
    S.xr0 = scr("xr0", [T, D], F32)
    S.xT = scr("xT", [D, T], BF16)
    S.x1T = scr("x1T", [D, T], BF16)
    S.fqT = scr("fqT", [1024, T], BF16)
    S.fkT = scr("fkT", [1024, T], BF16)
    S.suT = scr("suT", [1024, T], BF16)
    S.g3T = scr("g3T", [3072, T], BF16)
    S.mgT = scr("mgT", [6144, T], BF16)
    S.fv = scr("fv", [T, 1024], BF16)
    S.sv = scr("sv", [T, 1024], BF16)
    S.gg = scr("gg", [T, 1024], BF16)
    S.sm = scr("sm", [T, 24], F32)
    S.yaT = scr("yaT", [1024, T], BF16)
    S.ybT = scr("ybT", [1024, T], BF16)
    S.ycT = scr("ycT", [1024, T], BF16)
    S.mT = scr("mT", [D, T], BF16)
    SB = {k: Buf(k) for k in vars(S)}
    SB["y"] = Buf("y")
    SB["x"] = Buf("x")
    C.S, C.SB = S, SB

    with ExitStack() as es:
        P = Prog(nc, es)
        C.P = P
        C.ps = [Tl(es.enter_context(nc.psum_tensor(f"ps{i}", [128, 512], F32)), f"ps{i}") for i in range(6)]
        C.pb = [Tl(es.enter_context(nc.psum_tensor(f"pb{i}", [128, 1024], BF16)), f"pb{i}") for i in range(2)]
        for t_ in C.ps + C.pb:
            t_.b.excl = True
        C.c32 = P.sb("c32", [128, len(CN), 128], F32)
        C.cbf = P.sb("cbf", [128, len(CN), 128], BF16)
        P.dma("sp", C.c32[:], consts, writes=[C.c32], sem="ld_c")
        P.cp("dve", C.cbf[:], C.c32[:], [C.c32], [C.cbf])
        C.k32 = lambda n: C.c32[:, CN.index(n), :]
        C.kbf = lambda n: C.cbf[:, CN.index(n), :]

        for l in range(nl):
            w = W[l]
            xin, xin_b = (x_in, SB["x"]) if l == 0 else (S.xr0, SB["xr0"])
            last = (l == nl - 1)
            if l > 0:
                P.epoch(l)
            xout, xout_b = (y_out, SB["y"]) if last else (S.xr0, SB["xr0"])
            if "0" in stages:
                stage0(C, xin, xin_b)
            if "A" in stages:
                stageA(C, w)
            if "B" in stages:
                stageB(C, w)
            if "C" in stages:
                stageC(C, w)
            if "D" in stages:
                stageD(C, w)
            if "E" in stages:
                stageE1(C, w)
                stageE2(C, w, xin, xin_b)
            if "F" in stages:
                stageF(C, w, xout, xout_b)
        P.barrier()
        C.nops, C.nwait = P.nops, P.nwait
    return nc, C


class TStore:
    def __init__(self, C, dst, dst_buf, name):
        P = C.P
        self.C, self.dst, self.dst_buf = C, dst, dst_buf
        self.tiles = [P.sb(f"{name}_xTt{i}", [128, 16, 512], BF16) for i in range(2)]
        self.name = name
        self.n = 0

    def push(self, src, tb):
        C, P = self.C, self.C.P
        g, bi = tb // 4, tb % 4
        xt = self.tiles[g % 2]
        for q in range(4):
            pb = C.pb[self.n % 2]
            self.n += 1
            for j in range(4):
                kc = q * 4 + j
                P.tr(pb[:, j * 128:(j + 1) * 128], src[:, kc * 128:(kc + 1) * 128], C.kbf("ident"),
                     [src, C.cbf], [pb], signal=(j == 3))
            eng = "act" if (q % 2) else "dve"
            P.cp(eng, xt[:, q * 4:(q + 1) * 4, bi * 128:(bi + 1) * 128],
                 pb[:, 0:512].rearrange("p (j t) -> p j t", j=4), [pb], [xt])
        if bi == 3:
            P.dma("pool", self.dst.rearrange("(kc p) t -> p kc t", p=128)[:, :, g * 512:(g + 1) * 512], xt[:],
                  reads=[xt], writes=[self.dst_buf], sem=f"st_{self.name}{g % 2}")


def stage0(C, xin, xin_b):
    P, S, SB = C.P, C.S, C.SB
    with Scope(P):
        ts = TStore(C, S.xT, SB["xT"], "s0")
        x32 = [P.sb(f"x32_{i}", [128, D], F32) for i in range(2)]
        xbf = [P.sb(f"xbf_{i}", [128, D], BF16) for i in range(2)]
        for tb in range(C.NB):
            a, b = x32[tb % 2], xbf[tb % 2]
            P.dma("sp", a[:], xin[tb * 128:(tb + 1) * 128, :], reads=[xin_b], writes=[a], sem=f"ld_x{tb % 2}")
            P.cp("pool", b[:, 0:1024], a[:, 0:1024], [a], [b])
            P.cp("act", b[:, 1024:2048], a[:, 1024:2048], [a], [b])
            ts.push(b, tb)


def stageA(C, w):
    P, S, SB, T = C.P, C.S, C.SB, C.T
    TT = min(T, 2048)
    fm_dst = {"fq": (S.fqT, "fqT"), "fk": (S.fkT, "fkT"), "su": (S.suT, "suT"), "g3": (S.g3T, "g3T"), "mg": (S.mgT, "mgT")}
    tm_dst = {"fv": (S.fv, "fv"), "sv": (S.sv, "sv"), "gg": (S.gg, "gg")}
    with Scope(P):
        xs = P.sb("xs", [128, 16, TT], BF16)
        wt = [P.sb(f"wt{i}", [128, 16, 512], BF16) for i in range(2)]
        wsm = P.sb("wsm", [128, 16, 24], BF16)
        bfm = P.sb("bfm", [128, NFM], F32)
        btm = P.sb("btm", [128, 3072 + 24], F32)
        ofm = [P.sb(f"ofm{i}", [128, TT], BF16) for i in range(2)]
        otm = [P.sb(f"otm{i}", [128, 512], BF16) for i in range(2)]
        osm = [P.sb(f"osm{i}", [128, 24], F32) for i in range(2)]
        P.dma("sp", bfm[:], w.pc[:, PC["bfm"][0]:PC["bfm"][0] + NFM], writes=[bfm], sem="ld_s0")
        P.dma("sp", btm[:], w.bc[:, 0:3096], writes=[btm], sem="ld_s1")
        wi = 0
        pi = 0
        for ts_ in range(T // TT):
            t0 = ts_ * TT
            P.dma("sp", xs[:], S.xT.rearrange("(kc p) t -> p kc t", p=128)[:, :, t0:t0 + TT],
                  reads=[SB["xT"]], writes=[xs], sem="ld_xs")
            ci = 0
            oi = 0
            for name, c0, width in FM_RANGES:
                dst, dname = fm_dst[name]
                func = AF.Sigmoid if name == "mg" else AF.Identity
                for g in range(width // 512):
                    wb = wt[wi % 2]
                    P.dma("pool", wb[:], w.w_in.rearrange("(kc p) n -> p kc n", p=128)[:, :, c0 + g * 512:c0 + (g + 1) * 512],
                          writes=[wb], sem=f"ld_w{wi % 2}")
                    wi += 1
                    for c in range(4):
                        ob = ofm[oi % 2]
                        oi += 1
                        for tt in range(TT // 512):
                            ps = C.ps[pi % 4]
                            pi += 1
                            for kc in range(16):
                                P.mm(ps[:], wb[:, kc, c * 128:(c + 1) * 128], xs[:, kc, tt * 512:(tt + 1) * 512],
                                     kc == 0, kc == 15, [wb, xs], [ps])
                            P.act(ob[:, tt * 512:(tt + 1) * 512], ps[:], func, [ps, bfm], [ob],
                                  bias=bfm[:, ci:ci + 1])
                        r0 = g * 512 + c * 128
                        P.dma("sp", dst[r0:r0 + 128, t0:t0 + TT], ob[:], reads=[ob], writes=[SB[dname]],
                              sem=f"st_ofm{(oi - 1) % 2}")
                        ci += 1
            bo = 0
            oi = 0
            for name, c0, width in TM_RANGES:
                dst, dname = tm_dst[name]
                for g in range(width // 512):
                    wb = wt[wi % 2]
                    P.dma("pool", wb[:], w.w_in.rearrange("(kc p) n -> p kc n", p=128)[:, :, c0 + g * 512:c0 + (g + 1) * 512],
                          writes=[wb], sem=f"ld_w{wi % 2}")
                    wi += 1
                    for tb in range(TT // 128):
                        ps = C.ps[pi % 4]
                        pi += 1
                        for kc in range(16):
                            P.mm(ps[:], xs[:, kc, tb * 128:(tb + 1) * 128], wb[:, kc, :], kc == 0, kc == 15, [wb, xs], [ps])
                        ob = otm[oi % 2]
                        oi += 1
                        P.tt("dve", ob[:], ps[:], btm[:, bo:bo + 512], ALU.add, [ps, btm], [ob])
                        P.dma("sp", dst[t0 + tb * 128:t0 + (tb + 1) * 128, g * 512:(g + 1) * 512], ob[:], reads=[ob],
                              writes=[SB[dname]], sem=f"st_otm{(oi - 1) % 2}")
                    bo += 512
            if ts_ == 0:
                for i, c0 in enumerate((3072, 8200, 8208)):
                    P.dma("pool", wsm[:, :, i * 8:(i + 1) * 8], w.w_in.rearrange("(kc p) n -> p kc n", p=128)[:, :, c0:c0 + 8],
                          writes=[wsm], sem="ld_wsm")
            oi = 0
            for tb in range(TT // 128):
                ps = C.ps[pi % 4]
                pi += 1
                for kc in range(16):
                    P.mm(ps[:, 0:24], xs[:, kc, tb * 128:(tb + 1) * 128], wsm[:, kc, :], kc == 0, kc == 15, [wsm, xs], [ps])
                ob = osm[oi % 2]
                oi += 1
                P.tt("dve", ob[:], ps[:, 0:24], btm[:, 3072:3096], ALU.add, [ps, btm], [ob])
                P.dma("sp", S.sm[t0 + tb * 128:t0 + (tb + 1) * 128, :], ob[:], reads=[ob], writes=[SB["sm"]],
                      sem=f"st_osm{(oi - 1) % 2}")


def stageB(C, w):
    P, S, SB, T, NB = C.P, C.S, C.SB, C.T, C.NB
    scale = 128 ** -0.5
    with Scope(P):
        smf = P.sb("smf", [128, NB, 8], F32)
        Lp = P.sb("Lp", [128, NB, 8], F32)
        Lc = P.sb("Lc", [128, NB, 8], F32)
        tot = P.sb("tot", [128, NB, 8], F32)
        carry = P.sb("carry", [128, NB + 1, 8], F32)
        BT = [P.sb(f"BT{i}", [128, NB, NB], F32) for i in range(2)]
        kT = [P.sb(f"kT{i}", [128, T], BF16) for i in range(2)]
        qT = [P.sb(f"qT{i}", [128, T], BF16) for i in range(2)]
        vv = [P.sb(f"vv{i}", [128, NB, 128], BF16) for i in range(2)]
        pt = [P.sb(f"pt{i}", [128, 512], BF16) for i in range(3)]
        rl = P.sb("rl", [128, 512], F32)
        yat = [P.sb(f"yat{i}", [128, 512], BF16) for i in range(2)]
        P.dma("sp", smf[:], S.sm.rearrange("(b p) c -> p b c", p=128)[:, :, 0:8], reads=[SB["sm"]], writes=[smf], sem="ld_s0")
        P.act(Lp[:], smf[:], AF.Exp, [smf], [Lp], scale=-1.0)
        P.act(Lp[:], Lp[:], AF.Ln, [Lp], [Lp], bias=1.0)
        Lp2 = Lp[:].rearrange("p b h -> p (b h)")
        psW, psT = C.ps[4], C.ps[5]
        P.mm(psW[:, 0:NB * 8], C.k32("triU"), Lp2, True, True, [C.c32, Lp], [psW])
        P.mm(psT[:, 0:NB * 8], C.k32("ones"), Lp2, True, True, [C.c32, Lp], [psT])
        P.cp("dve", tot[:].rearrange("p b h -> p (b h)"), psT[:, 0:NB * 8], [psT], [tot])
        P.op("dve", lambda e: e.memset(carry[:, 0, :], 0.0), writes=[carry])
        for b in range(NB):
            P.tt("dve", carry[:, b + 1, :], carry[:, b, :], tot[:, b, :], ALU.add, [carry, tot], [carry])
        P.tt("dve", Lc[:].rearrange("p b h -> p (b h)"), psW[:, 0:NB * 8],
             carry[:, 0:NB, :].rearrange("p b h -> p (b h)"), ALU.add, [psW, carry], [Lc])
        pi = 0
        pti = 0
        yi = 0
        for h in range(8):
            k_, q_, v_, bt = kT[h % 2], qT[h % 2], vv[h % 2], BT[h % 2]
            P.dma("sp", k_[:], S.fkT[h * 128:(h + 1) * 128, :], reads=[SB["fkT"]], writes=[k_], sem=f"ld_k{h % 2}")
            P.dma("sp", q_[:], S.fqT[h * 128:(h + 1) * 128, :], reads=[SB["fqT"]], writes=[q_], sem=f"ld_q{h % 2}")
            P.dma("sp", v_[:], S.fv.rearrange("(b p) d -> p b d", p=128)[:, :, h * 128:(h + 1) * 128],
                  reads=[SB["fv"]], writes=[v_], sem=f"ld_v{h % 2}")
            P.tt("dve", bt[:], Lc[:, :, h].unsqueeze(2).to_broadcast([128, NB, NB]),
                 carry[:, 1:NB + 1, h].unsqueeze(1).to_broadcast([128, NB, NB]), ALU.subtract, [Lc, carry], [bt])
            for qt in range(T // 512):
                pso = C.ps[4 + (qt % 2)]
                psl = C.ps[2 + (qt % 2)]
                nk = 4 * qt + 4
                for i in range(nk):
                    jmin = max(0, i - 4 * qt)
                    c0 = jmin * 128
                    pss = C.ps[pi % 2]
                    pi += 1
                    P.mm(pss[:, c0:512], k_[:, i * 128:(i + 1) * 128], q_[:, qt * 512 + c0:(qt + 1) * 512], True, True,
                         [k_, q_], [pss])
                    p_ = pt[pti % 3]
                    pti += 1
                    for j in range(jmin, 4):
                        P.act(p_[:, j * 128:(j + 1) * 128], pss[:, j * 128:(j + 1) * 128], AF.Exp, [pss, bt], [p_],
                              bias=bt[:, i, 4 * qt + j:4 * qt + j + 1], scale=scale)
                    if i >= 4 * qt:
                        P.tt("pool", p_[:, c0:c0 + 128], p_[:, c0:c0 + 128], C.kbf("triKQ"), ALU.mult, [p_, C.cbf], [p_])
                    P.mm(pso[:, c0:512], v_[:, i, :], p_[:, c0:512], i == 0, i == nk - 1, [v_, p_], [pso])
                    P.mm(psl[:, c0:512], C.kbf("ones"), p_[:, c0:512], i == 0, i == nk - 1, [C.cbf, p_], [psl])
                P.op("dve", lambda e: e.reciprocal(out=rl[:], in_=psl[:]), reads=[psl], writes=[rl])
                y_ = yat[yi % 2]
                yi += 1
                P.tt("dve", y_[:], pso[:], rl[:], ALU.mult, [pso, rl], [y_])
                P.dma("sp", S.yaT[h * 128:(h + 1) * 128, qt * 512:(qt + 1) * 512], y_[:], reads=[y_], writes=[SB["yaT"]],
                      sem=f"st_ya{(yi - 1) % 2}")


def stageC(C, w):
    P, S, SB, T, NB = C.P, C.S, C.SB, C.T, C.NB
    with Scope(P):
        wT32 = P.sb("wT32", [128, 8, 128], F32)
        wTm = P.sb("wTm", [128, 8, 128], BF16)
        lg = P.sb("lg", [128, 1024], F32)
        lb = P.sb("lb", [128, 1024], F32)
        bsb = P.sb("bsb", [128, 8, 128], F32)
        sv = [P.sb(f"sv{i}", [128, 1024], BF16) for i in range(2)]
        su = [P.sb(f"su{i}", [128, 8, 512], BF16) for i in range(2)]
        yb = [P.sb(f"yb{i}", [128, 8, 512], BF16) for i in range(2)]
        sq = P.sb("sq", [128, 1024], F32)
        vn = P.sb("vn", [128, 1024], F32)
        vg = [P.sb(f"vg{i}", [128, 1024], BF16) for i in range(2)]
        tmp = P.sb("tmp", [128, 8, 128], F32)
        st = P.sb("st", [128, 4, 8], F32)
        P.dma("sp", wT32[:], w.sguwT, writes=[wT32], sem="ld_s0")
        P.dma("sp", lg[:], w.bc[:, BC["slg"][0]:BC["slg"][0] + 1024], writes=[lg], sem="ld_s1")
        P.dma("sp", lb[:], w.bc[:, BC["slb"][0]:BC["slb"][0] + 1024], writes=[lb], sem="ld_s2")
        P.dma("sp", bsb[:].rearrange("p g t -> p (g t)"), w.bc[:, BC["sgub"][0]:BC["sgub"][0] + 1024], writes=[bsb], sem="ld_s3")
        P.tt("dve", wTm[:], wT32[:], C.k32("sguT").unsqueeze(1).to_broadcast([128, 8, 128]), ALU.mult, [wT32, C.c32], [wTm])
        for g4 in range(NB // 4):
            u_, y_ = su[g4 % 2], yb[g4 % 2]
            P.dma("sp", u_[:], S.suT.rearrange("(g c) t -> c g t", c=128)[:, :, g4 * 512:(g4 + 1) * 512],
                  reads=[SB["suT"]], writes=[u_], sem=f"ld_su{g4 % 2}")
            for bi in range(4):
                sp_ = g4 * 4 + bi
                v_ = sv[sp_ % 2]
                P.dma("sp", v_[:], S.sv[sp_ * 128:(sp_ + 1) * 128, :], reads=[SB["sv"]], writes=[v_], sem=f"ld_sv{sp_ % 2}")
                v3 = v_[:].rearrange("p (g c) -> p g c", g=8)
                P.op("dve", lambda e: e.tensor_reduce(out=st[:, 0, :], in_=v3, axis=AX.X, op=ALU.add), reads=[v_], writes=[st])
                P.act(sq[:], v_[:], AF.Square, [v_], [sq])
                P.op("dve", lambda e: e.tensor_reduce(out=st[:, 1, :], in_=sq[:].rearrange("p (g c) -> p g c", g=8),
                                                      axis=AX.X, op=ALU.add), reads=[sq], writes=[st])
                P.ts("dve", st[:, 2, :], st[:, 0, :], 1.0 / 128, None, ALU.mult, None, [st], [st])
                P.tt("dve", st[:, 0, :], st[:, 2, :], st[:, 2, :], ALU.mult, [st], [st])
                P.stt("dve", st[:, 1, :], st[:, 1, :], 1.0 / 128, st[:, 0, :], ALU.mult, ALU.subtract, [st], [st])
                P.act(st[:, 3, :], st[:, 1, :], AF.Ln, [st], [st], bias=LN_EPS)
                P.act(st[:, 3, :], st[:, 3, :], AF.Exp, [st], [st], scale=-0.5)
                vn3 = vn[:].rearrange("p (g c) -> p g c", g=8)
                P.tt("dve", vn3, v3, st[:, 2, :].unsqueeze(2).to_broadcast([128, 8, 128]), ALU.subtract, [v_, st], [vn])
                P.tt("pool", vn3, vn3, st[:, 3, :].unsqueeze(2).to_broadcast([128, 8, 128]), ALU.mult, [vn, st], [vn])
                P.tt("pool", vn[:], vn[:], lg[:], ALU.mult, [vn, lg], [vn])
                vg_ = vg[sp_ % 2]
                P.tt("dve", vg_[:], vn[:], lb[:], ALU.add, [vn, lb], [vg_])
                for half in range(2):
                    ps = C.ps[(sp_ * 2 + half) % 4]
                    for gg in range(4):
                        g = half * 4 + gg
                        P.mm(ps[:, gg * 128:(gg + 1) * 128], vg_[:, g * 128:(g + 1) * 128], wTm[:, g, :], True, True,
                             [vg_, wTm], [ps], signal=(gg == 3))
                    t3 = tmp[:, half * 4:(half + 1) * 4, :]
                    P.tt("dve", t3, ps[:].rearrange("p (g t) -> p g t", g=4), bsb[:, half * 4:(half + 1) * 4, :], ALU.add,
                         [ps, bsb], [tmp])
                    P.tt("pool", y_[:, half * 4:(half + 1) * 4, bi * 128:(bi + 1) * 128], t3,
                         u_[:, half * 4:(half + 1) * 4, bi * 128:(bi + 1) * 128], ALU.mult, [tmp, u_], [y_])
            P.dma("sp", S.ybT.rearrange("(g c) t -> c g t", c=128)[:, :, g4 * 512:(g4 + 1) * 512], y_[:], reads=[y_],
                  writes=[SB["ybT"]], sem=f"st_yb{g4 % 2}")


def stageE1(C, w):
    P, S, SB, T = C.P, C.S, C.SB, C.T
    with Scope(P):
        wp = [P.sb(f"wp{i}", [128, 8, D], BF16) for i in range(3)]
        ys = [[P.sb(f"ys{i}_{j}", [128, 8, 512], BF16) for j in range(3)] for i in range(2)]
        gt = [P.sb(f"gt{i}", [128, 3, 512], BF16) for i in range(2)]
        acc = P.sb("acc", [128, 512], F32)
        t1 = P.sb("t1", [128, 512], F32)
        t2 = P.sb("t2", [128, 512], F32)
        mt = [P.sb(f"mt{i}", [128, 512], BF16) for i in range(2)]
        for i, wsrc in enumerate((w.wpa, w.wpb, w.wpc)):
            for hh in range(2):
                P.dma("pool", wp[i][:, :, hh * 1024:(hh + 1) * 1024],
                      wsrc.rearrange("(kc p) n -> p kc n", p=128)[:, :, hh * 1024:(hh + 1) * 1024], writes=[wp[i]], sem=f"ld_wp{i}")
        srcs = [(S.yaT, "yaT"), (S.ybT, "ybT"), (S.ycT, "ycT")]
        gi = 0
        for tt in range(T // 512):
            yt = ys[tt % 2]
            for i, (src, nm) in enumerate(srcs):
                P.dma("sp", yt[i][:], src.rearrange("(kc p) t -> p kc t", p=128)[:, :, tt * 512:(tt + 1) * 512],
                      reads=[SB[nm]], writes=[yt[i]], sem=f"ld_ys{tt % 2}_{i}")
            for fc in range(16):
                g_ = gt[gi % 2]
                m_ = mt[gi % 2]
                P.dma("sp", g_[:], S.mgT.rearrange("(b fc p) t -> p b fc t", b=3, p=128)[:, :, fc, tt * 512:(tt + 1) * 512],
                      reads=[SB["mgT"]], writes=[g_], sem=f"ld_gt{gi % 2}")
                pss = [C.ps[(gi % 2) * 3 + i] for i in range(3)]
                gi += 1
                for i in range(3):
                    for kc in range(8):
                        P.mm(pss[i][:], wp[i][:, kc, fc * 128:(fc + 1) * 128], yt[i][:, kc, :], kc == 0, kc == 7,
                             [wp[i], yt[i]], [pss[i]])
                P.tt("dve", acc[:], pss[0][:], g_[:, 0, :], ALU.mult, [pss[0], g_], [acc])
                P.tt("dve", t1[:], pss[1][:], g_[:, 1, :], ALU.mult, [pss[1], g_], [t1])
                P.tt("dve", t2[:], pss[2][:], g_[:, 2, :], ALU.mult, [pss[2], g_], [t2])
                P.tt("pool", acc[:], acc[:], t1[:], ALU.add, [acc, t1], [acc])
                P.tt("pool", m_[:], acc[:], t2[:], ALU.add, [acc, t2], [m_])
                P.dma("sp", S.mT[fc * 128:(fc + 1) * 128, tt * 512:(tt + 1) * 512], m_[:], reads=[m_], writes=[SB["mT"]],
                      sem=f"st_mt{(gi - 1) % 2}")


def layer_norm_tile(C, y, st6, mv, sc, gt_, bt_):
    P = C.P
    for q in range(4):
        P.op("dve", lambda e, q=q: e.bn_stats(out=st6[:, q, :], in_=y[:, q * 512:(q + 1) * 512]), reads=[y], writes=[st6])
    P.op("dve", lambda e: e.bn_aggr(out=mv[:], in_=st6[:]), reads=[st6], writes=[mv])
    P.act(sc[:, 0:1], mv[:, 1:2], AF.Ln, [mv], [sc], bias=LN_EPS)
    P.act(sc[:, 0:1], sc[:, 0:1], AF.Exp, [sc], [sc], scale=-0.5)
    P.stt("dve", sc[:, 1:2], mv[:, 0:1], -1.0, sc[:, 0:1], ALU.mult, ALU.mult, [mv, sc], [sc])
    P.act(y[:], y[:], AF.Identity, [y, sc], [y], bias=sc[:, 1:2], scale=sc[:, 0:1])
    P.tt("pool", y[:], y[:], gt_[:], ALU.mult, [y, gt_], [y])
    P.tt("dve", y[:], y[:], bt_[:], ALU.add, [y, bt_], [y])


def stageE2(C, w, xin, xin_b):
    P, S, SB, T, NB = C.P, C.S, C.SB, C.T, C.NB
    with Scope(P):
        wo = P.sb("wo", [128, 16, D], BF16)
        mt = [P.sb(f"mtl{i}", [128, 16, 512], BF16) for i in range(2)]
        yt = [P.sb(f"yt{i}", [128, D], F32) for i in range(2)]
        xb = [P.sb(f"xb{i}", [128, D], BF16) for i in range(2)]
        lg = P.sb("lg", [128, D], F32)
        lb = P.sb("lb", [128, D], F32)
        st6 = P.sb("st6", [128, 4, 6], F32)
        mv = P.sb("mv", [128, 2], F32)
        sc = P.sb("sc", [128, 2], F32)
        ts = TStore(C, S.x1T, SB["x1T"], "e2")
        for q in range(4):
            P.dma("pool", wo[:, :, q * 512:(q + 1) * 512], w.wout.rearrange("(kc p) n -> p kc n", p=128)[:, :, q * 512:(q + 1) * 512],
                  writes=[wo], sem="ld_wo")
        P.dma("sp", lg[:], w.bc[:, BC["ln1g"][0]:BC["ln1g"][0] + D], writes=[lg], sem="ld_s0")
        P.dma("sp", lb[:], w.bc[:, BC["ln1b"][0]:BC["ln1b"][0] + D], writes=[lb], sem="ld_s1")
        for tb in range(NB):
            g, bi = tb // 4, tb % 4
            m_ = mt[g % 2]
            if bi == 0:
                P.dma("sp", m_[:], S.mT.rearrange("(kc p) t -> p kc t", p=128)[:, :, g * 512:(g + 1) * 512],
                      reads=[SB["mT"]], writes=[m_], sem=f"ld_mt{g % 2}")
            y_ = yt[tb % 2]
            P.dma("sp", y_[:], xin[tb * 128:(tb + 1) * 128, :], reads=[xin_b], writes=[y_], sem=f"ld_y{tb % 2}")
            for cg in range(4):
                ps = C.ps[cg]
                for kc in range(16):
                    P.mm(ps[:], m_[:, kc, bi * 128:(bi + 1) * 128], wo[:, kc, cg * 512:(cg + 1) * 512], kc == 0, kc == 15,
                         [m_, wo], [ps])
                P.stt("dve", y_[:, cg * 512:(cg + 1) * 512], y_[:, cg * 512:(cg + 1) * 512], ALPHA, ps[:], ALU.mult, ALU.add,
                      [y_, ps], [y_])
            layer_norm_tile(C, y_, st6, mv, sc, lg, lb)
            P.dma("pool", S.xr1[tb * 128:(tb + 1) * 128, :], y_[:], reads=[y_], writes=[SB["xr1"]], sem=f"st_y{tb % 2}")
            b_ = xb[tb % 2]
            P.cp("act", b_[:], y_[:], [y_], [b_])
            ts.push(b_, tb)


def stageF(C, w, xout, xout_b):
    P, S, SB, T = C.P, C.S, C.SB, C.T
    NP = DFF // 128
    with Scope(P):
        x1 = P.sb("x1", [128, 16, 512], BF16)
        wu = [P.sb(f"wu{i}", [128, 2, 16, 256], BF16) for i in range(2)]
        aT = P.sb("aT", [128, NP, 512], BF16)
        wd = [P.sb(f"wd{i}", [128, NP, 256], BF16) for i in range(2)]
        yt = P.sb("ytf", [128, 4, D], F32)
        lg = P.sb("lg", [128, D], F32)
        lb = P.sb("lb", [128, D], F32)
        fcw = P.sb("fcw", [128, 86, 3], F32)
        fcb = P.sb("fcb", [128, 86], F32)
        hal = P.sb("hal", [128, 86, 2], F32)
        hb = [P.sb(f"hb{i}", [128, 2, 514], F32) for i in range(1)]
        cv = [P.sb(f"cv{i}", [128, 2, 512], F32) for i in range(1)]
        sg = P.sb("sg", [128, 512], F32)
        st6 = P.sb("st6", [128, 4, 6], F32)
        mv = P.sb("mv", [128, 2], F32)
        sc = P.sb("sc", [128, 2], F32)
        P.dma("sp", lg[:], w.bc[:, BC["ln2g"][0]:BC["ln2g"][0] + D], writes=[lg], sem="ld_s0")
        P.dma("sp", lb[:], w.bc[:, BC["ln2b"][0]:BC["ln2b"][0] + D], writes=[lb], sem="ld_s1")
        P.dma("sp", fcw[:].rearrange("p c k -> p (c k)"), w.pc[:, PC["fcw"][0]:PC["fcw"][0] + 258], writes=[fcw], sem="ld_s2")
        P.dma("sp", fcb[:], w.pc[:, PC["fcb"][0]:PC["fcb"][0] + 86], writes=[fcb], sem="ld_s3")
        P.op("pool", lambda e: e.memset(hal[:], 0.0), writes=[hal])
        wui = 0
        wdi = 0
        pi = 0
        hi = 0
        wupv = w.wup.rearrange("(kc p) (two n) -> p two kc n", p=128, two=2)
        for tt in range(T // 512):
            P.dma("sp", x1[:], S.x1T.rearrange("(kc p) t -> p kc t", p=128)[:, :, tt * 512:(tt + 1) * 512],
                  reads=[SB["x1T"]], writes=[x1], sem="ld_x1")
            for tb in range(4):
                r0 = tt * 512 + tb * 128
                P.dma("sp", yt[:, tb, :], S.xr1[r0:r0 + 128, :], reads=[SB["xr1"]], writes=[yt], sem="ld_ytf")
            for fc in range(NP):
                if fc % 2 == 0:
                    wb = wu[wui % 2]
                    wui += 1
                    ncol = min(256, DFF - fc * 128)
                    for two in range(2):
                        P.dma("pool", wb[:, two, :, 0:ncol], wupv[:, two, :, fc * 128:fc * 128 + ncol], writes=[wb],
                              sem=f"ld_wu{(wui - 1) % 2}")
                co = (fc % 2) * 128
                pg, pv = C.ps[pi % 4], C.ps[(pi + 1) % 4]
                pi += 2
                for two, ps in ((0, pg), (1, pv)):
                    for kc in range(16):
                        P.mm(ps[:], wb[:, two, kc, co:co + 128], x1[:, kc, :], kc == 0, kc == 15, [wb, x1], [ps])
                h_ = hb[0]
                c_ = cv[0]
                hi += 1
                for two, ps in ((0, pg), (1, pv)):
                    ch = two * NP + fc
                    P.cp("pool", h_[:, two, 0:2], hal[:, ch, :], [hal], [h_])
                    P.cp("act", h_[:, two, 2:514], ps[:], [ps], [h_])
                    P.cp("pool", hal[:, ch, :], h_[:, two, 512:514], [h_], [hal])
                    P.ts("dve", c_[:, two, :], h_[:, two, 2:514], fcw[:, ch, 2:3], fcb[:, ch:ch + 1], ALU.mult, ALU.add,
                         [h_, fcw, fcb], [c_])
                    P.stt("dve", c_[:, two, :], h_[:, two, 1:513], fcw[:, ch, 1:2], c_[:, two, :], ALU.mult, ALU.add,
                          [h_, fcw, c_], [c_])
                    P.stt("dve", c_[:, two, :], h_[:, two, 0:512], fcw[:, ch, 0:1], c_[:, two, :], ALU.mult, ALU.add,
                          [h_, fcw, c_], [c_])
                P.act(sg[:], c_[:, 0, :], AF.Silu, [c_], [sg])
                P.tt("pool", aT[:, fc, :], sg[:], c_[:, 1, :], ALU.mult, [sg, c_], [aT])
            for cg in range(D // 256):
                wb = wd[wdi % 2]
                wdi += 1
                for q, (k0, k1) in enumerate(((0, 22), (22, NP))):
                    P.dma("pool", wb[:, k0:k1, :], w.wdn.rearrange("(kc p) n -> p kc n", p=128)[:, k0:k1, cg * 256:(cg + 1) * 256],
                          writes=[wb], sem=f"ld_wd{(wdi - 1) % 2}")
                for tb in range(4):
                    ps = C.ps[pi % 4]
                    pi += 1
                    for kc in range(NP):
                        P.mm(ps[:, 0:256], aT[:, kc, tb * 128:(tb + 1) * 128], wb[:, kc, :], kc == 0, kc == NP - 1, [aT, wb], [ps])
                    ysl = yt[:, tb, cg * 256:(cg + 1) * 256]
                    P.stt("dve", ysl, ysl, ALPHA, ps[:, 0:256], ALU.mult, ALU.add, [yt, ps], [yt])
            for tb in range(4):
                r0 = tt * 512 + tb * 128
                yv = _YV(yt, tb)
                layer_norm_tile(C, yv, st6, mv, sc, lg, lb)
                P.dma("pool", xout[r0:r0 + 128, :], yt[:, tb, :], reads=[yt], writes=[xout_b], sem="st_yf")


class _YV:
    def __init__(self, t, tb):
        self.t, self.tb, self.b = t, tb, t.b

    def __getitem__(self, k):
        v = self.t.t[:, self.tb, :]
        if isinstance(k, tuple):
            return v[k]
        return v


D_HEADS = 8
D_NB = None
D_LEVEL = 5


def stageD(C, w, npar=2):
    P, S, SB, T, NB = C.P, C.S, C.SB, C.T, C.NB
    NBH = NB * 8
    with Scope(P):
        sma = P.sb("sma", [128, NB, 8], F32)
        smb = P.sb("smb", [128, NB, 8], F32)
        hb8 = P.sb("hb8", [128, 2, 8], F32)
        nega = P.sb("nega", [128, 8], F32)
        gg_ = P.sb("g", [128, NB, 8], F32)
        gc = P.sb("gc", [128, NB, 8], F32)
        egc = P.sb("egc", [128, NB, 8], F32)
        kdec = P.sb("kdec", [128, NB, 8], F32)
        beta = P.sb("beta", [128, NB, 8], F32)
        nbeta = P.sb("nbeta", [128, NB, 8], F32)
        eglb = P.sb("eglb", [128, 2, NBH], F32)
        ngb = P.sb("ngb", [128, 128], F32)
        gcw = P.sb("gcw", [128, 24, 4], F32)
        f2 = lambda t: t[:].rearrange("p b h -> p (b h)")
        P.dma("sp", sma[:], S.sm.rearrange("(b p) c -> p b c", p=128)[:, :, 8:16], reads=[SB["sm"]], writes=[sma], sem="ld_s0")
        P.dma("sp", smb[:], S.sm.rearrange("(b p) c -> p b c", p=128)[:, :, 16:24], reads=[SB["sm"]], writes=[smb], sem="ld_s1")
        P.dma("sp", hb8[:].rearrange("p a h -> p (a h)"), w.bc[:, BC["alog"][0]:BC["alog"][0] + 16], writes=[hb8], sem="ld_s2")
        P.dma("sp", ngb[:], w.bc[:, BC["gng"][0]:BC["gng"][0] + 128], writes=[ngb], sem="ld_s3")
        P.dma("sp", gcw[:].rearrange("p c k -> p (c k)"), w.pc[:, PC["gcw"][0]:PC["gcw"][0] + 96], writes=[gcw], sem="ld_s4")
        P.act(nega[:], hb8[:, 0, :], AF.Exp, [hb8], [nega])
        P.ts("dve", nega[:], nega[:], -1.0, None, ALU.mult, None, [nega], [nega])
        P.tt("dve", sma[:], sma[:], hb8[:, 1, :].unsqueeze(1).to_broadcast([128, NB, 8]), ALU.add, [sma, hb8], [sma])
        P.act(sma[:], sma[:], AF.Exp, [sma], [sma])
        P.act(sma[:], sma[:], AF.Ln, [sma], [sma], bias=1.0)
        P.tt("dve", gg_[:], sma[:], nega[:].unsqueeze(1).to_broadcast([128, NB, 8]), ALU.mult, [sma, nega], [gg_])
        P.act(beta[:], smb[:], AF.Sigmoid, [smb], [beta])
        P.ts("dve", nbeta[:], beta[:], -1.0, None, ALU.mult, None, [beta], [nbeta])
        p0, p1 = C.ps[0], C.ps[1]
        P.mm(p0[:, 0:NBH], C.k32("triBD"), f2(gg_), True, True, [C.c32, gg_], [p0])
        P.mm(p0[:, NBH:2 * NBH], C.k32("blk"), f2(gg_), True, True, [C.c32, gg_], [p0])
        P.mm(p1[:, 0:NBH], C.k32("half0"), f2(gg_), True, True, [C.c32, gg_], [p1])
        P.mm(p1[:, NBH:2 * NBH], C.k32("half1"), f2(gg_), True, True, [C.c32, gg_], [p1])
        P.cp("dve", f2(gc), p0[:, 0:NBH], [p0], [gc])
        P.act(f2(egc), p0[:, 0:NBH], AF.Exp, [p0], [egc])
        P.tt("dve", f2(kdec), p0[:, NBH:2 * NBH], f2(gc), ALU.subtract, [p0, gc], [kdec])
        P.act(f2(kdec), f2(kdec), AF.Exp, [kdec], [kdec])
        P.act(eglb[:].rearrange("p c n -> p (c n)"), p1[:, 0:2 * NBH], AF.Exp, [p1], [eglb])
        P.barrier()

        chains = []
        for ci in range(npar):
            R = Ctx()
            R.ci = ci
            R.raw = P.sb("raw", [128, T + 3], BF16)
            R.acc = P.sb("acc", [128, T], F32)
            R.cvT = P.sb("cvT", [128, 3, T], BF16)
            R.ggh = P.sb("ggh", [128, NB, 128], BF16)
            R.ycT = P.sb("ycT", [128, T], BF16)
            R.S32 = P.sb("S32", [128, 128], F32)
            R.Sbf = P.sb("Sbf", [128, 128], BF16)
            R.jk = P.sb("jk", [128, 128], F32)
            R.qsq = P.sb("qsq", [128, 128], BF16)
            R.sc = P.sb("sc", [128, 8], F32)
            R.kn = P.sb("kn", [128, 128], BF16)
            R.kb = P.sb("kb", [128, 128], BF16)
            R.kg = P.sb("kg", [128, 128], BF16)
            R.kd = P.sb("kd", [128, 128], BF16)
            R.knT = P.sb("knT", [128, 128], BF16)
            R.kbT = P.sb("kbT", [128, 128], BF16)
            R.vtm = P.sb("vtm", [128, 128], BF16)
            R.gU = P.sb("gU", [128, 128], F32)
            R.E2 = P.sb("E2", [128, 256], F32)
            R.Em = P.sb("Em", [128, 3, 128], F32)
            R.N = [P.sb(f"N{i}", [128, 128], BF16) for i in range(2)]
            R.NT = [P.sb(f"NT{i}", [128, 128], BF16) for i in range(2)]
            R.qkT = P.sb("qkT", [128, 128], BF16)
            R.X = [P.sb(f"X{i}", [128, 128], BF16) for i in range(2)]
            R.ub = P.sb("ub", [128, 128], F32)
            R.wT = P.sb("wT", [128, 128], BF16)
            R.vnew = P.sb("vnew", [128, 128], BF16)
            R.t1 = P.sb("t1", [128, 128], F32)
            R.o = P.sb("o", [128, 128], F32)
            R.y1 = P.sb("y1", [128, 128], BF16)
            bA, bB, bC = C.ps[ci * 3], C.ps[ci * 3 + 1], C.ps[ci * 3 + 2]
            pb = C.pb[ci]
            R.pA, R.pB, R.pC, R.pb = bA, bB, bC, pb
            R.rg = {"kk": bA.b, "ssq": bA.b, "dd": bB.b, "dbl": bA.b, "x": bC.b, "ws": bC.b, "qs": bC.b, "qv": bC.b,
                    "tk": pb.b, "tn": pb.b, "tv": pb.b, "ty": pb.b}
            chains.append(R)

        def chain(R, heads):
            ci = R.ci
            rg = R.rg
            pA, pB, pC, pb = R.pA, R.pB, R.pC, R.pb
            for h in heads:
                P.op("pool", lambda e: e.memset(R.raw[:, 0:3], 0.0), writes=[R.raw])
                P.dma("sp", R.ggh[:], S.gg.rearrange("(b p) e -> p b e", p=128)[:, :, h * 128:(h + 1) * 128],
                      reads=[SB["gg"]], writes=[R.ggh], sem=f"ld_ggh{ci}")
                yield
                for i in range(3):
                    r0 = i * 1024 + h * 128
                    P.dma("sp", R.raw[:, 3:3 + T], S.g3T[r0:r0 + 128, :], reads=[SB["g3T"]], writes=[R.raw], sem=f"ld_raw{ci}")
                    chn = i * 8 + h
                    P.ts("dve", R.acc[:], R.raw[:, 3:3 + T], gcw[:, chn, 3:4], None, ALU.mult, None, [R.raw, gcw], [R.acc])
                    yield
                    for tap in (2, 1, 0):
                        eng = "dve"
                        P.stt(eng, R.acc[:], R.raw[:, tap:tap + T], gcw[:, chn, tap:tap + 1], R.acc[:], ALU.mult, ALU.add,
                              [R.raw, gcw, R.acc], [R.acc])
                        yield
                    P.act(R.cvT[:, i, :], R.acc[:], AF.Silu, [R.acc], [R.cvT])
                    yield
                P.act(R.ggh[:], R.ggh[:], AF.Silu, [R.ggh], [R.ggh])
                P.tt("pool", R.ggh[:], R.ggh[:], ngb[:].unsqueeze(1).to_broadcast([128, NB, 128]), ALU.mult, [R.ggh, ngb], [R.ggh])
                P.op("pool", lambda e: e.memset(R.S32[:], 0.0), writes=[R.S32])
                P.op("pool", lambda e: e.memset(R.Sbf[:], 0.0), writes=[R.Sbf])
                yield
                qcT, kcT, vcT = R.cvT[:, 0, :], R.cvT[:, 1, :], R.cvT[:, 2, :]
                for b in range(NB if D_NB is None else D_NB):
                    if D_LEVEL < 2.1:
                        break
                    cols = slice(b * 128, (b + 1) * 128)
                    bh = b * 8 + h
                    col = lambda t: t[:, b, h:h + 1]
                    P.tr(pb[:, 0:128], kcT[:, cols], C.kbf("ident"), [R.cvT, C.cbf], [rg["tk"]])
                    P.tr(pb[:, 384:512], vcT[:, cols], C.kbf("ident"), [R.cvT, C.cbf], [rg["tv"]])
                    yield
                    P.act(R.jk[:], pb[:, 0:128], AF.Square, [rg["tk"]], [R.jk, R.sc], accum_out=R.sc[:, 0:1])
                    P.act(R.sc[:, 1:2], R.sc[:, 0:1], AF.Ln, [R.sc], [R.sc], bias=RMS_EPS)
                    P.act(R.sc[:, 1:2], R.sc[:, 1:2], AF.Exp, [R.sc], [R.sc], scale=-0.5)
                    yield
                    P.ts("dve", R.kn[:], pb[:, 0:128], R.sc[:, 1:2], None, ALU.mult, None, [rg["tk"], R.sc], [R.kn])
                    P.cp("act", R.vtm[:], pb[:, 384:512], [rg["tv"]], [R.vtm])
                    yield
                    if D_LEVEL < 2.3:
                        continue
                    P.act(R.kb[:], R.kn[:], AF.Identity, [R.kn, beta], [R.kb], scale=col(beta))
                    P.act(R.kg[:], R.kn[:], AF.Identity, [R.kn, egc], [R.kg], scale=col(egc))
                    P.act(R.kd[:], R.kn[:], AF.Identity, [R.kn, kdec], [R.kd], scale=col(kdec))
                    P.ts("dve", R.gU[:], C.k32("triBD"), col(gg_), None, ALU.mult, None, [C.c32, gg_], [R.gU])
                    yield
                    if D_LEVEL < 2.35:
                        continue
                    P.tr(pb[:, 128:256], R.kn[:], C.kbf("ident"), [R.kn, C.cbf], [rg["tn"]])
                    P.tr(pb[:, 256:384], R.kb[:], C.kbf("ident"), [R.kb, C.cbf], [rg["tn"]])
                    yield
                    P.cp("act", R.knT[:], pb[:, 128:256], [rg["tn"]], [R.knT])
                    P.cp("dve", R.kbT[:], pb[:, 256:384], [rg["tn"]], [R.kbT])
                    if D_LEVEL < 2.5:
                        continue
                    P.act(R.qsq[:], qcT[:, cols], AF.Square, [R.cvT], [R.qsq])
                    yield
                    P.mm(pA[:, 384:385], R.qsq[:], C.kbf("ones")[:, 0:1], True, True, [R.qsq, C.cbf], [rg["ssq"]])
                    if D_LEVEL < 2.6:
                        continue
                    P.mm(pB[:, 0:128], R.gU[:], C.k32("lm"), True, True, [R.gU, C.c32], [rg["dd"]], signal=False)
                    P.mm(pB[:, 128:256], C.k32("lm"), R.gU[:], True, True, [R.gU, C.c32], [rg["dd"]])
                    P.mm(pA[:, 0:128], R.knT[:], R.kbT[:], True, True, [R.knT, R.kbT], [rg["kk"]], signal=False)
                    P.mm(pA[:, 128:256], R.kbT[:], R.knT[:], True, True, [R.knT, R.kbT], [rg["kk"]], signal=False)
                    P.mm(pA[:, 256:384], R.knT[:], qcT[:, cols], True, True, [R.knT, R.cvT], [rg["kk"]])
                    yield
                    if D_LEVEL < 2.8:
                        continue
                    P.act(R.sc[:, 2:3], pA[:, 384:385], AF.Ln, [rg["ssq"]], [R.sc], bias=RMS_EPS)
                    P.act(R.sc[:, 2:3], R.sc[:, 2:3], AF.Exp, [R.sc], [R.sc], scale=-0.5)
                    P.act(R.E2[:], pB[:, 0:256], AF.Exp, [rg["dd"]], [R.E2])
                    yield
                    P.tt("pool", R.Em[:, 0, :], R.E2[:, 0:128], C.k32("sm"), ALU.mult, [R.E2, C.c32], [R.Em])
                    P.tt("pool", R.Em[:, 1, :], R.E2[:, 128:256], C.k32("smT"), ALU.mult, [R.E2, C.c32], [R.Em])
                    P.tt("pool", R.Em[:, 2, :], R.E2[:, 128:256], C.k32("inclT"), ALU.mult, [R.E2, C.c32], [R.Em])
                    yield
                    P.stt("dve", R.NT[0][:], pA[:, 0:128], -1.0, R.Em[:, 0, :], ALU.mult, ALU.mult, [rg["kk"], R.Em], [R.NT[0]])
                    P.stt("dve", R.N[0][:], pA[:, 128:256], -1.0, R.Em[:, 1, :], ALU.mult, ALU.mult, [rg["kk"], R.Em], [R.N[0]])
                    P.tt("dve", R.qkT[:], pA[:, 256:384], R.Em[:, 2, :], ALU.mult, [rg["kk"], R.Em], [R.qkT])
                    P.tt("pool", R.X[0][:], R.N[0][:], C.kbf("ident"), ALU.add, [R.N[0], C.cbf], [R.X[0]])
                    yield
                    if D_LEVEL < 3.5:
                        continue
                    cur = 0
                    for s in range(5):
                        Nc, NTc = R.N[cur], R.NT[cur]
                        Nn, NTn = R.N[1 - cur], R.NT[1 - cur]
                        if s < 4:
                            P.mm(pA[:, 0:128], NTc[:], Nc[:], True, True, [Nc, NTc], [rg["dbl"]], signal=False)
                        P.mm(pA[:, 128:256], Nc[:], NTc[:], True, True, [Nc, NTc], [rg["dbl"]])
                        yield
                        if D_LEVEL < 3.58:
                            continue
                        if s < 4:
                            P.cp("act", Nn[:], pA[:, 0:128], [rg["dbl"]], [Nn])
                        P.cp("dve", NTn[:], pA[:, 128:256], [rg["dbl"]], [NTn])
                        yield
                        if D_LEVEL < 3.65:
                            continue
                        Xc, Xn = R.X[s % 2], R.X[(s + 1) % 2]
                        P.mm(pC[:, 0:128], NTn[:], Xc[:], True, True, [NTn, Xc], [rg["x"]])
                        yield
                        P.tt("dve", Xn[:], pC[:, 0:128], Xc[:], ALU.add, [rg["x"], Xc], [Xn])
                        yield
                        cur = 1 - cur
                    if D_LEVEL < 3.8:
                        continue
                    Xf = R.X[5 % 2]
                    P.mm(pC[:, 0:128], Xf[:], R.vtm[:], True, True, [Xf, R.vtm], [rg["x"]])
                    yield
                    P.ts("dve", R.ub[:], pC[:, 0:128], col(beta), None, ALU.mult, None, [rg["x"], beta], [R.ub])
                    yield
                    P.mm(pC[:, 0:128], R.kg[:], Xf[:], True, True, [R.kg, Xf], [rg["x"]])
                    yield
                    P.cp("act", R.wT[:], pC[:, 0:128], [rg["x"]], [R.wT])
                    yield
                    if D_LEVEL < 5:
                        continue
                    for c in range(2):
                        r = slice(c * 64, (c + 1) * 64)
                        P.mm(pC[:, 128:256], R.wT[:], R.Sbf[:], True, True, [R.wT, R.Sbf], [rg["ws"]])
                        P.mm(pC[:, 256:384], qcT[:, cols], R.Sbf[:], True, True, [R.cvT, R.Sbf], [rg["qs"]])
                        yield
                        P.stt("dve", R.vnew[r, :], pC[r, 128:256], nbeta[r, b, h:h + 1], R.ub[r, :], ALU.mult, ALU.add,
                              [rg["ws"], nbeta, R.ub], [R.vnew])
                        P.act(R.t1[r, :], pC[r, 256:384], AF.Identity, [rg["qs"], egc], [R.t1], scale=egc[r, b, h:h + 1])
                        yield
                        P.mm(pC[:, 384:512], R.qkT[r, :], R.vnew[r, :], True, True, [R.qkT, R.vnew], [rg["qv"]])
                        yield
                        P.tt("dve", R.o[r, :], pC[r, 384:512], R.t1[r, :], ALU.add, [rg["qv"], R.t1], [R.o])
                        yield
                        P.mm(pC[:, 384:512], R.kd[r, :], R.vnew[r, :], True, True, [R.kd, R.vnew], [rg["qv"]])
                        yield
                        P.stt("dve", R.S32[:], R.S32[:], eglb[:, c, bh:bh + 1], pC[:, 384:512], ALU.mult, ALU.add,
                              [R.S32, eglb, rg["qv"]], [R.S32])
                        P.cp("act", R.Sbf[:], R.S32[:], [R.S32], [R.Sbf])
                        yield
                    P.ts("dve", R.sc[:, 3:4], R.sc[:, 2:3], 128 ** -0.5, None, ALU.mult, None, [R.sc], [R.sc])
                    P.act(R.jk[:], R.o[:], AF.Square, [R.o, R.sc], [R.jk, R.sc], scale=R.sc[:, 3:4], accum_out=R.sc[:, 4:5])
                    P.act(R.sc[:, 5:6], R.sc[:, 4:5], AF.Ln, [R.sc], [R.sc], bias=RMS_EPS, scale=1.0 / 128)
                    P.act(R.sc[:, 5:6], R.sc[:, 5:6], AF.Exp, [R.sc], [R.sc], scale=-0.5)
                    P.tt("dve", R.sc[:, 6:7], R.sc[:, 5:6], R.sc[:, 3:4], ALU.mult, [R.sc], [R.sc])
                    yield
                    P.stt("dve", R.y1[:], R.o[:], R.sc[:, 6:7], R.ggh[:, b, :], ALU.mult, ALU.mult, [R.o, R.sc, R.ggh], [R.y1])
                    yield
                    P.tr(pb[:, 0:128], R.y1[:], C.kbf("ident"), [R.y1, C.cbf], [rg["ty"]])
                    yield
                    P.cp("act", R.ycT[:, cols], pb[:, 0:128], [rg["ty"]], [R.ycT])
                    yield
                P.dma("sp", S.ycT[h * 128:(h + 1) * 128, :], R.ycT[:], reads=[R.ycT], writes=[SB["ycT"]], sem=f"st_yc{ci}")
                yield

        gens = [chain(chains[ci], list(range(ci, D_HEADS, npar))) for ci in range(npar)] if D_LEVEL >= 2 else []
        alive = list(gens)
        while alive:
            for g in list(alive):
                try:
                    next(g)
                except StopIteration:
                    alive.remove(g)


_CACHE = {}
FUSED = True


def _prog(nl):
    if nl not in _CACHE:
        _CACHE[nl] = build(nl, SEQ)[0]
    return _CACHE[nl]


def _layer_map(inp, l, slot):
    bc, pc, sguwT = layer_small(inp, l)
    f = lambda a: np.ascontiguousarray(a, dtype=np.float32)
    return {f"w_in{slot}": f(inp["w_in"][l]), f"bc{slot}": bc, f"pc{slot}": pc, f"sguwT{slot}": sguwT,
            f"wpa{slot}": f(inp["w_proj_a"][l]), f"wpb{slot}": f(inp["w_proj_b"][l]), f"wpc{slot}": f(inp["w_proj_c"][l]),
            f"wout{slot}": f(inp["w_out"][l]), f"wup{slot}": f(inp["ffn_w_up"][l]), f"wdn{slot}": f(inp["ffn_w_down"][l])}


def kernel(**inp):
    inp = {k: np.asarray(v) for k, v in inp.items()}
    x = inp["x"].astype(np.float32, copy=False)
    B = x.shape[0]
    consts = make_consts()
    cur = [np.ascontiguousarray(x[b]) for b in range(B)]
    if FUSED:
        nc = _prog(DEPTH)
        wm = {}
        for l in range(DEPTH):
            wm.update(_layer_map(inp, l, l))
        in_maps = [dict(wm, x=cur[b], consts=consts) for b in range(B)]
        res = run_bass_kernel_spmd(nc, in_maps, core_ids=list(range(B)))
        cur = [np.asarray(res.results[b]["y"]) for b in range(B)]
    else:
        nc = _prog(1)
        for l in range(DEPTH):
            wm = _layer_map(inp, l, 0)
            in_maps = [dict(wm, x=cur[b], consts=consts) for b in range(B)]
            res = run_bass_kernel_spmd(nc, in_maps, core_ids=list(range(B)))
            cur = [np.asarray(res.results[b]["y"]) for b in range(B)]
    return np.stack(cur).astype(np.float32)
```

```python
import numpy as np
from contextlib import ExitStack
import concourse.bass as bass
import concourse.mybir as mybir
from concourse.bass_utils import run_bass_kernel_spmd

F32 = mybir.dt.float32
BF16 = mybir.dt.bfloat16
AF = mybir.ActivationFunctionType
ALU = mybir.AluOpType
AX = mybir.AxisListType

D = 2048
NIN = 15384
DFF = 5504
SEQ = 4096
DEPTH = 4
ALPHA = (2 * DEPTH) ** 0.25
LN_EPS = 1e-5
RMS_EPS = 1e-6
OFF = dict(fq=0, fk=1024, fv=2048, ff=3072, su=3080, sv=4104, gq=5128, gk=6152, gv=7176,
           ga=8200, gb=8208, gg=8216, mg=9240)


class Buf:
    __slots__ = ("name", "w", "r", "excl")

    def __init__(self, name="b"):
        self.name = name
        self.w = {}
        self.r = {}
        self.excl = False


class Tl:
    def __init__(self, t, name):
        self.t = t
        self.b = Buf(name)

    def __getitem__(self, k):
        return self.t[k]


class Prog:
    def __init__(self, nc, es):
        self.nc = nc
        self.es = es
        self.ges = es
        self.engs = {"pe": nc.tensor, "act": nc.scalar, "dve": nc.vector,
                     "pool": nc.gpsimd, "sp": nc.sync}
        self.sems = {}
        self.cnt = {}
        self.waited = {e: {} for e in self.engs}
        self.nwait = 0
        self.nops = 0
        self.uid = 0
        self.ekey = {e: e for e in self.engs}

    def sem(self, key):
        if key not in self.sems:
            self.sems[key] = self.ges.enter_context(self.nc.semaphore("s_" + key))
            self.cnt[key] = 0
        return self.sems[key]

    def _wait(self, eng, deps, own=None):
        for k, c in deps.items():
            if c <= 0:
                continue
            if k == own and eng == "pe":
                continue
            if self.waited[eng].get(k, 0) >= c:
                continue
            assert c <= self.cnt[k], f"forward wait {eng} on {k}: {c} > {self.cnt[k]}"
            self.engs[eng].wait_ge(self.sem(k), c)
            self.waited[eng][k] = c
            self.nwait += 1

    @staticmethod
    def _collect(reads, writes):
        deps = {}
        for t in reads:
            for k, c in t.w.items():
                if deps.get(k, 0) < c:
                    deps[k] = c
        for t in writes:
            for k, c in t.w.items():
                if deps.get(k, 0) < c:
                    deps[k] = c
            for k, c in t.r.items():
                if deps.get(k, 0) < c:
                    deps[k] = c
        return deps

    @staticmethod
    def _bufs(xs):
        return [x.b if hasattr(x, "b") else x for x in xs]

    def op(self, eng, fn, reads=(), writes=(), signal=True):
        reads = self._bufs(reads)
        writes = self._bufs(writes)
        writes = writes + [t for t in reads if t.excl and t not in writes]
        key = self.ekey[eng]
        self.sem(key)
        assert signal or eng == "pe"
        self._wait(eng, self._collect(reads, writes), own=key)
        ins = fn(self.engs[eng])
        self.nops += 1
        if signal:
            self.cnt[key] += 1
            ins.then_inc(self.sems[key], 1)
            o = self.cnt[key]
        else:
            o = self.cnt[key] + 1
        for t in reads:
            t.r[key] = o
        for t in writes:
            t.w[key] = o
        return ins

    def epoch(self, tag):
        self.barrier()
        self.ekey = {e: f"{e}_{tag}" for e in self.engs}

    def dma(self, q, out, in_, reads=(), writes=(), sem=None, **kw):
        reads = self._bufs(reads)
        writes = self._bufs(writes)
        self.sem(sem)
        deps = self._collect(reads, writes)
        deps[sem] = max(deps.get(sem, 0), self.cnt[sem])
        self._wait(q, deps)
        ins = self.engs[q].dma_start(out=out, in_=in_, **kw)
        ins.then_inc(self.sems[sem], 16)
        self.nops += 1
        self.cnt[sem] += 16
        o = self.cnt[sem]
        for t in reads:
            t.r[sem] = o
        for t in writes:
            t.w[sem] = o
        return ins

    def barrier(self):
        for e in self.engs:
            self._wait(e, dict(self.cnt), own=self.ekey[e])

    def sb(self, name, shape, dt):
        self.uid += 1
        nm = f"{name}_{self.uid}"
        return Tl(self.es.enter_context(self.nc.sbuf_tensor(nm, list(shape), dt)), nm)

    def mm(self, out, lhsT, rhs, start, stop, reads, writes, signal=None):
        return self.op("pe", lambda e: e.matmul(out, lhsT=lhsT, rhs=rhs, start=start, stop=stop),
                       reads=reads, writes=writes, signal=(stop if signal is None else signal))

    def tr(self, out, in_, ident, reads, writes, signal=True):
        return self.op("pe", lambda e: e.transpose(out, in_, ident), reads=reads, writes=writes, signal=signal)

    def act(self, out, in_, func, reads, writes, bias=0.0, scale=1.0, accum_out=None):
        if accum_out is None:
            return self.op("act", lambda e: e.activation(out=out, in_=in_, func=func, bias=bias, scale=scale),
                           reads=reads, writes=writes)
        return self.op("act", lambda e: e.activation(out=out, in_=in_, func=func, bias=bias, scale=scale,
                                                      accum_out=accum_out), reads=reads, writes=writes)

    def tt(self, eng, out, in0, in1, op, reads, writes):
        return self.op(eng, lambda e: e.tensor_tensor(out=out, in0=in0, in1=in1, op=op), reads=reads, writes=writes)

    def ts(self, eng, out, in0, s1, s2, op0, op1, reads, writes):
        if op1 is None:
            return self.op(eng, lambda e: e.tensor_scalar(out=out, in0=in0, scalar1=s1, scalar2=None, op0=op0),
                           reads=reads, writes=writes)
        return self.op(eng, lambda e: e.tensor_scalar(out=out, in0=in0, scalar1=s1, scalar2=s2, op0=op0, op1=op1),
                       reads=reads, writes=writes)

    def stt(self, eng, out, in0, scalar, in1, op0, op1, reads, writes):
        return self.op(eng, lambda e: e.scalar_tensor_tensor(out=out, in0=in0, scalar=scalar, in1=in1, op0=op0, op1=op1),
                       reads=reads, writes=writes)

    def cp(self, eng, out, in_, reads, writes):
        if eng == "act":
            return self.op("act", lambda e: e.activation(out=out, in_=in_, func=AF.Identity), reads=reads, writes=writes)
        return self.op(eng, lambda e: e.tensor_copy(out=out, in_=in_), reads=reads, writes=writes)


class Scope:
    def __init__(self, P):
        self.P = P

    def __enter__(self):
        self.old = self.P.es
        self.st = ExitStack()
        self.st.__enter__()
        self.P.es = self.st
        return self

    def __exit__(self, *a):
        self.P.barrier()
        self.P.es = self.old
        return self.st.__exit__(*a)


CN = ["ident", "ones", "triKQ", "triU", "triBD", "half0", "half1", "blk", "lm", "sm", "smT", "inclT", "sguT"]


def make_consts():
    i = np.arange(128)
    ch = i // 64
    same = ch[:, None] == ch[None, :]
    c = {}
    c["ident"] = np.eye(128)
    c["ones"] = np.ones((128, 128))
    c["triKQ"] = (i[None, :] >= i[:, None])
    c["triU"] = (i[:, None] <= i[None, :])
    c["triBD"] = same & (i[:, None] <= i[None, :])
    c["half0"] = np.broadcast_to((i < 64)[:, None], (128, 128))
    c["half1"] = np.broadcast_to((i >= 64)[:, None], (128, 128))
    c["blk"] = same
    c["lm"] = same & (i[:, None] > i[None, :])
    c["sm"] = same & (i[:, None] > i[None, :])
    c["smT"] = same & (i[None, :] > i[:, None])
    c["inclT"] = same & (i[None, :] >= i[:, None])
    c["sguT"] = (ch[None, :] >= ch[:, None])
    return np.ascontiguousarray(np.stack([c[n].astype(np.float32) for n in CN], axis=1))


FM_RANGES = [("fq", 0, 1024), ("fk", 1024, 1024), ("su", 3080, 1024), ("g3", 5128, 3072), ("mg", 9240, 6144)]
TM_RANGES = [("fv", 2048, 1024), ("sv", 4104, 1024), ("gg", 8216, 1024)]
SM_COLS = list(range(3072, 3080)) + list(range(8200, 8216))
NFM = sum(r[2] for r in FM_RANGES) // 128

BC = {}
_o = 0
for _n, _w in [("btm", 3072), ("bsm", 24), ("ln1g", D), ("ln1b", D), ("ln2g", D), ("ln2b", D),
               ("slg", 1024), ("slb", 1024), ("gng", 128), ("alog", 8), ("dtb", 8), ("sgub", 1024)]:
    BC[_n] = (_o, _w)
    _o += _w
BCW = _o
PC = {}
_o = 0
for _n, _w in [("bfm", NFM), ("gcw", 24 * 4), ("fcw", 86 * 3), ("fcb", 86)]:
    PC[_n] = (_o, _w)
    _o += _w
PCW = _o


def layer_small(inp, l):
    bin_ = inp["b_in"][l]
    rows = [np.concatenate([bin_[c0:c0 + w] for _, c0, w in TM_RANGES]), bin_[SM_COLS],
            inp["ln1_g"][l], inp["ln1_b"][l], inp["ln2_g"][l], inp["ln2_b"][l],
            inp["sgu_ln_g"][l], inp["sgu_ln_b"][l], inp["gdn_norm_g"][l], inp["gdn_a_log"][l],
            inp["gdn_dt_bias"][l], inp["sgu_b"][l].reshape(-1)]
    row = np.concatenate(rows).astype(np.float32)
    assert row.shape[0] == BCW
    bc = np.ascontiguousarray(np.broadcast_to(row[None, :], (128, BCW)))
    bfm = np.concatenate([bin_[c0:c0 + w] for _, c0, w in FM_RANGES]).reshape(NFM, 128).T
    gcw = inp["gdn_conv_w"][l].reshape(4, 24, 128).transpose(2, 1, 0).reshape(128, 96)
    fcw = inp["ffn_conv_w"][l].reshape(3, 86, 128).transpose(2, 1, 0).reshape(128, 258)
    fcb = inp["ffn_conv_b"][l].reshape(86, 128).T
    pc = np.ascontiguousarray(np.concatenate([bfm, gcw, fcw, fcb], axis=1).astype(np.float32))
    assert pc.shape[1] == PCW
    sguwT = np.ascontiguousarray(inp["sgu_w"][l].transpose(2, 0, 1))
    return bc, pc, sguwT


class Ctx:
    pass


def build(nl, T, dbg_in=(), dbg_out=(), stages="0ABCDEF"):
    nc = bass.Bass("TRN2", target_bir_lowering=False)
    NB = T // 128
    C = Ctx()
    C.nc, C.T, C.NB = nc, T, NB

    def ext_in(name, shape, dt=F32):
        return nc.dram_tensor(name, list(shape), dt, kind="ExternalInput").ap()

    def scr(name, shape, dt):
        kind = "ExternalInput" if name in dbg_in else ("ExternalOutput" if name in dbg_out else "Internal")
        return nc.dram_tensor(name, list(shape), dt, kind=kind).ap()

    x_in = ext_in("x", [T, D])
    consts = ext_in("consts", [128, len(CN), 128])
    W = []
    for l in range(nl):
        w = Ctx()
        w.w_in = ext_in(f"w_in{l}", [D, NIN])
        w.bc = ext_in(f"bc{l}", [128, BCW])
        w.pc = ext_in(f"pc{l}", [128, PCW])
        w.sguwT = ext_in(f"sguwT{l}", [128, 8, 128])
        w.wpa = ext_in(f"wpa{l}", [1024, D])
        w.wpb = ext_in(f"wpb{l}", [1024, D])
        w.wpc = ext_in(f"wpc{l}", [1024, D])
        w.wout = ext_in(f"wout{l}", [D, D])
        w.wup = ext_in(f"wup{l}", [D, 2 * DFF])
        w.wdn = ext_in(f"wdn{l}", [DFF, D])
        W.append(w)
    y_out = nc.dram_tensor("y", [T, D], F32, kind="ExternalOutput").ap()

    S = Ctx()
    S.xr1 = scr("xr1", [T, D], F32)
    S.xr0 = scr("xr0", [T, D], F32)
    S.xT = scr("xT", [D, T], BF16)
    S.x1T = scr("x1T", [D, T], BF16)
    S.fqT = scr("fqT", [1024, T], BF16)
    S.fkT = scr("fkT", [1024, T], BF16)
    S.suT = scr("suT", [1024, T], BF16)
    S.g3T = scr("g3T", [3072, T], BF16)
    S.mgT = scr("mgT", [6144, T], BF16)
    S.fv = scr("fv", [T, 1024], BF16)
    S.sv = scr("sv", [T, 1024], BF16)
    S.gg = scr("gg", [T, 1024], BF16)
    S.sm = scr("sm", [T, 24], F32)
    S.yaT = scr("yaT", [1024, T], BF16)
    S.ybT = scr("ybT", [1024, T], BF16)
    S.ycT = scr("ycT", [1024, T], BF16)
    S.mT = scr("mT", [D, T], BF16)
    S.wupb = scr("wupb", [22, 128, 2, 16, 256], BF16)
    S.wdnb = scr("wdnb", [8, 128, 43, 256], BF16)
    SB = {k: Buf(k) for k in vars(S)}
    SB["y"] = Buf("y")
    SB["x"] = Buf("x")
    C.S, C.SB = S, SB

    with ExitStack() as es:
        P = Prog(nc, es)
        C.P = P
        C.ps = [Tl(es.enter_context(nc.psum_tensor(f"ps{i}", [128, 512], F32)), f"ps{i}") for i in range(6)]
        C.pb = [Tl(es.enter_context(nc.psum_tensor(f"pb{i}", [128, 1024], BF16)), f"pb{i}") for i in range(2)]
        for t_ in C.ps + C.pb:
            t_.b.excl = True
        C.c32 = P.sb("c32", [128, len(CN), 128], F32)
        C.cbf = P.sb("cbf", [128, len(CN), 128], BF16)
        P.dma("sp", C.c32[:], consts, writes=[C.c32], sem="ld_c")
        P.cp("dve", C.cbf[:], C.c32[:], [C.c32], [C.cbf])
        C.k32 = lambda n: C.c32[:, CN.index(n), :]
        C.kbf = lambda n: C.cbf[:, CN.index(n), :]

        for l in range(nl):
            w = W[l]
            xin, xin_b = (x_in, SB["x"]) if l == 0 else (S.xr0, SB["xr0"])
            last = (l == nl - 1)
            if l > 0:
                P.epoch(l)
            xout, xout_b = (y_out, SB["y"]) if last else (S.xr0, SB["xr0"])
            if "0" in stages:
                stage0(C, xin, xin_b)
            if "A" in stages:
                stageA(C, w)
            if "B" in stages:
                stageB(C, w)
            if "C" in stages:
                stageC(C, w)
            if "D" in stages:
                stageD(C, w)
            if "E" in stages:
                stageE1(C, w)
                stageE2(C, w, xin, xin_b)
            if "F" in stages:
                stageF(C, w, xout, xout_b)
        P.barrier()
        C.nops, C.nwait = P.nops, P.nwait
    return nc, C


class TStore:
    def __init__(self, C, dst, dst_buf, name):
        P = C.P
        self.C, self.dst, self.dst_buf = C, dst, dst_buf
        self.tiles = [P.sb(f"{name}_xTt{i}", [128, 16, 512], BF16) for i in range(2)]
        self.name = name
        self.n = 0

    def push(self, src, tb):
        C, P = self.C, self.C.P
        g, bi = tb // 4, tb % 4
        xt = self.tiles[g % 2]
        for q in range(4):
            pb = C.pb[self.n % 2]
            self.n += 1
            for j in range(4):
                kc = q * 4 + j
                P.tr(pb[:, j * 128:(j + 1) * 128], src[:, kc * 128:(kc + 1) * 128], C.kbf("ident"),
                     [src, C.cbf], [pb], signal=(j == 3))
            eng = "act" if (q % 2) else "dve"
            P.cp(eng, xt[:, q * 4:(q + 1) * 4, bi * 128:(bi + 1) * 128],
                 pb[:, 0:512].rearrange("p (j t) -> p j t", j=4), [pb], [xt])
        if bi == 3:
            P.dma("pool", self.dst.rearrange("(kc p) t -> p kc t", p=128)[:, :, g * 512:(g + 1) * 512], xt[:],
                  reads=[xt], writes=[self.dst_buf], sem=f"st_{self.name}{g % 2}")


def stage0(C, xin, xin_b):
    P, S, SB = C.P, C.S, C.SB
    with Scope(P):
        ts = TStore(C, S.xT, SB["xT"], "s0")
        x32 = [P.sb(f"x32_{i}", [128, D], F32) for i in range(2)]
        xbf = [P.sb(f"xbf_{i}", [128, D], BF16) for i in range(2)]
        for tb in range(C.NB):
            a, b = x32[tb % 2], xbf[tb % 2]
            P.dma("sp", a[:], xin[tb * 128:(tb + 1) * 128, :], reads=[xin_b], writes=[a], sem=f"ld_x{tb % 2}")
            P.cp("pool", b[:, 0:1024], a[:, 0:1024], [a], [b])
            P.cp("act", b[:, 1024:2048], a[:, 1024:2048], [a], [b])
            ts.push(b, tb)


def stageA(C, w):
    P, S, SB, T = C.P, C.S, C.SB, C.T
    TT = min(T, 2048)
    fm_dst = {"fq": (S.fqT, "fqT"), "fk": (S.fkT, "fkT"), "su": (S.suT, "suT"), "g3": (S.g3T, "g3T"), "mg": (S.mgT, "mgT")}
    tm_dst = {"fv": (S.fv, "fv"), "sv": (S.sv, "sv"), "gg": (S.gg, "gg")}
    with Scope(P):
        xs = P.sb("xs", [128, 16, TT], BF16)
        wt = [P.sb(f"wt{i}", [128, 16, 512], BF16) for i in range(2)]
        wsm = P.sb("wsm", [128, 16, 24], BF16)
        bfm = P.sb("bfm", [128, NFM], F32)
        btm = P.sb("btm", [128, 3072 + 24], F32)
        ofm = [P.sb(f"ofm{i}", [128, TT], BF16) for i in range(2)]
        otm = [P.sb(f"otm{i}", [128, 512], BF16) for i in range(2)]
        osm = [P.sb(f"osm{i}", [128, 24], F32) for i in range(2)]
        P.dma("sp", bfm[:], w.pc[:, PC["bfm"][0]:PC["bfm"][0] + NFM], writes=[bfm], sem="ld_s0")
        P.dma("sp", btm[:], w.bc[:, 0:3096], writes=[btm], sem="ld_s1")
        wi = 0
        pi = 0
        for ts_ in range(T // TT):
            t0 = ts_ * TT
            P.dma("sp", xs[:], S.xT.rearrange("(kc p) t -> p kc t", p=128)[:, :, t0:t0 + TT],
                  reads=[SB["xT"]], writes=[xs], sem="ld_xs")
            ci = 0
            oi = 0
            for name, c0, width in FM_RANGES:
                dst, dname = fm_dst[name]
                func = AF.Sigmoid if name == "mg" else AF.Identity
                for g in range(width // 512):
                    wb = wt[wi % 2]
                    P.dma("pool", wb[:], w.w_in.rearrange("(kc p) n -> p kc n", p=128)[:, :, c0 + g * 512:c0 + (g + 1) * 512],
                          writes=[wb], sem=f"ld_w{wi % 2}")
                    wi += 1
                    for c in range(4):
                        ob = ofm[oi % 2]
                        oi += 1
                        for tt in range(TT // 512):
                            ps = C.ps[pi % 4]
                            pi += 1
                            for kc in range(16):
                                P.mm(ps[:], wb[:, kc, c * 128:(c + 1) * 128], xs[:, kc, tt * 512:(tt + 1) * 512],
                                     kc == 0, kc == 15, [wb, xs], [ps])
                            P.act(ob[:, tt * 512:(tt + 1) * 512], ps[:], func, [ps, bfm], [ob],
                                  bias=bfm[:, ci:ci + 1])
                        r0 = g * 512 + c * 128
                        P.dma("sp", dst[r0:r0 + 128, t0:t0 + TT], ob[:], reads=[ob], writes=[SB[dname]],
                              sem=f"st_ofm{(oi - 1) % 2}")
                        ci += 1
            bo = 0
            oi = 0
            for name, c0, width in TM_RANGES:
                dst, dname = tm_dst[name]
                for g in range(width // 512):
                    wb = wt[wi % 2]
                    P.dma("pool", wb[:], w.w_in.rearrange("(kc p) n -> p kc n", p=128)[:, :, c0 + g * 512:c0 + (g + 1) * 512],
                          writes=[wb], sem=f"ld_w{wi % 2}")
                    wi += 1
                    for tb in range(TT // 128):
                        ps = C.ps[pi % 4]
                        pi += 1
                        for kc in range(16):
                            P.mm(ps[:], xs[:, kc, tb * 128:(tb + 1) * 128], wb[:, kc, :], kc == 0, kc == 15, [wb, xs], [ps])
                        ob = otm[oi % 2]
                        oi += 1
                        P.tt("dve", ob[:], ps[:], btm[:, bo:bo + 512], ALU.add, [ps, btm], [ob])
                        P.dma("sp", dst[t0 + tb * 128:t0 + (tb + 1) * 128, g * 512:(g + 1) * 512], ob[:], reads=[ob],
                              writes=[SB[dname]], sem=f"st_otm{(oi - 1) % 2}")
                    bo += 512
            if ts_ == 0:
                for i, c0 in enumerate((3072, 8200, 8208)):
                    P.dma("pool", wsm[:, :, i * 8:(i + 1) * 8], w.w_in.rearrange("(kc p) n -> p kc n", p=128)[:, :, c0:c0 + 8],
                          writes=[wsm], sem="ld_wsm")
            oi = 0
            for tb in range(TT // 128):
                ps = C.ps[pi % 4]
                pi += 1
                for kc in range(16):
                    P.mm(ps[:, 0:24], xs[:, kc, tb * 128:(tb + 1) * 128], wsm[:, kc, :], kc == 0, kc == 15, [wsm, xs], [ps])
                ob = osm[oi % 2]
                oi += 1
                P.tt("dve", ob[:], ps[:, 0:24], btm[:, 3072:3096], ALU.add, [ps, btm], [ob])
                P.dma("sp", S.sm[t0 + tb * 128:t0 + (tb + 1) * 128, :], ob[:], reads=[ob], writes=[SB["sm"]],
                      sem=f"st_osm{(oi - 1) % 2}")


def stageB(C, w):
    P, S, SB, T, NB = C.P, C.S, C.SB, C.T, C.NB
    scale = 128 ** -0.5
    with Scope(P):
        smf = P.sb("smf", [128, NB, 8], F32)
        Lp = P.sb("Lp", [128, NB, 8], F32)
        Lc = P.sb("Lc", [128, NB, 8], F32)
        tot = P.sb("tot", [128, NB, 8], F32)
        carry = P.sb("carry", [128, NB + 1, 8], F32)
        BT = [P.sb(f"BT{i}", [128, NB, NB], F32) for i in range(2)]
        kT = [P.sb(f"kT{i}", [128, T], BF16) for i in range(2)]
        qT = [P.sb(f"qT{i}", [128, T], BF16) for i in range(2)]
        vv = [P.sb(f"vv{i}", [128, NB, 128], BF16) for i in range(2)]
        pt = [P.sb(f"pt{i}", [128, 512], BF16) for i in range(3)]
        rl = P.sb("rl", [128, 512], F32)
        yat = [P.sb(f"yat{i}", [128, 512], BF16) for i in range(2)]
        P.dma("sp", smf[:], S.sm.rearrange("(b p) c -> p b c", p=128)[:, :, 0:8], reads=[SB["sm"]], writes=[smf], sem="ld_s0")
        P.act(Lp[:], smf[:], AF.Exp, [smf], [Lp], scale=-1.0)
        P.act(Lp[:], Lp[:], AF.Ln, [Lp], [Lp], bias=1.0)
        Lp2 = Lp[:].rearrange("p b h -> p (b h)")
        psW, psT = C.ps[4], C.ps[5]
        P.mm(psW[:, 0:NB * 8], C.k32("triU"), Lp2, True, True, [C.c32, Lp], [psW])
        P.mm(psT[:, 0:NB * 8], C.k32("ones"), Lp2, True, True, [C.c32, Lp], [psT])
        P.cp("dve", tot[:].rearrange("p b h -> p (b h)"), psT[:, 0:NB * 8], [psT], [tot])
        P.op("dve", lambda e: e.memset(carry[:, 0, :], 0.0), writes=[carry])
        for b in range(NB):
            P.tt("dve", carry[:, b + 1, :], carry[:, b, :], tot[:, b, :], ALU.add, [carry, tot], [carry])
        P.tt("dve", Lc[:].rearrange("p b h -> p (b h)"), psW[:, 0:NB * 8],
             carry[:, 0:NB, :].rearrange("p b h -> p (b h)"), ALU.add, [psW, carry], [Lc])
        pi = 0
        pti = 0
        yi = 0
        for h in range(8):
            k_, q_, v_, bt = kT[h % 2], qT[h % 2], vv[h % 2], BT[h % 2]
            P.dma("sp", k_[:], S.fkT[h * 128:(h + 1) * 128, :], reads=[SB["fkT"]], writes=[k_], sem=f"ld_k{h % 2}")
            P.dma("sp", q_[:], S.fqT[h * 128:(h + 1) * 128, :], reads=[SB["fqT"]], writes=[q_], sem=f"ld_q{h % 2}")
            P.dma("sp", v_[:], S.fv.rearrange("(b p) d -> p b d", p=128)[:, :, h * 128:(h + 1) * 128],
                  reads=[SB["fv"]], writes=[v_], sem=f"ld_v{h % 2}")
            P.tt("dve", bt[:], Lc[:, :, h].unsqueeze(2).to_broadcast([128, NB, NB]),
                 carry[:, 1:NB + 1, h].unsqueeze(1).to_broadcast([128, NB, NB]), ALU.subtract, [Lc, carry], [bt])
            for qt in range(T // 512):
                pso = C.ps[4 + (qt % 2)]
                psl = C.ps[2 + (qt % 2)]
                nk = 4 * qt + 4
                for i in range(nk):
                    jmin = max(0, i - 4 * qt)
                    c0 = jmin * 128
                    pss = C.ps[pi % 2]
                    pi += 1
                    P.mm(pss[:, c0:512], k_[:, i * 128:(i + 1) * 128], q_[:, qt * 512 + c0:(qt + 1) * 512], True, True,
                         [k_, q_], [pss])
                    p_ = pt[pti % 3]
                    pti += 1
                    for j in range(jmin, 4):
                        P.act(p_[:, j * 128:(j + 1) * 128], pss[:, j * 128:(j + 1) * 128], AF.Exp, [pss, bt], [p_],
                              bias=bt[:, i, 4 * qt + j:4 * qt + j + 1], scale=scale)
                    if i >= 4 * qt:
                        P.tt("pool", p_[:, c0:c0 + 128], p_[:, c0:c0 + 128], C.kbf("triKQ"), ALU.mult, [p_, C.cbf], [p_])
                    P.mm(pso[:, c0:512], v_[:, i, :], p_[:, c0:512], i == 0, i == nk - 1, [v_, p_], [pso])
                    P.mm(psl[:, c0:512], C.kbf("ones"), p_[:, c0:512], i == 0, i == nk - 1, [C.cbf, p_], [psl])
                P.op("dve", lambda e: e.reciprocal(out=rl[:], in_=psl[:]), reads=[psl], writes=[rl])
                y_ = yat[yi % 2]
                yi += 1
                P.tt("dve", y_[:], pso[:], rl[:], ALU.mult, [pso, rl], [y_])
                P.dma("sp", S.yaT[h * 128:(h + 1) * 128, qt * 512:(qt + 1) * 512], y_[:], reads=[y_], writes=[SB["yaT"]],
                      sem=f"st_ya{(yi - 1) % 2}")


def stageC(C, w):
    P, S, SB, T, NB = C.P, C.S, C.SB, C.T, C.NB
    with Scope(P):
        wT32 = P.sb("wT32", [128, 8, 128], F32)
        wTm = P.sb("wTm", [128, 8, 128], BF16)
        lg = P.sb("lg", [128, 1024], F32)
        lb = P.sb("lb", [128, 1024], F32)
        bsb = P.sb("bsb", [128, 8, 128], F32)
        sv = [P.sb(f"sv{i}", [128, 1024], BF16) for i in range(2)]
        su = [P.sb(f"su{i}", [128, 8, 512], BF16) for i in range(2)]
        yb = [P.sb(f"yb{i}", [128, 8, 512], BF16) for i in range(2)]
        sq = P.sb("sq", [128, 1024], F32)
        vn = P.sb("vn", [128, 1024], F32)
        vg = [P.sb(f"vg{i}", [128, 1024], BF16) for i in range(2)]
        tmp = P.sb("tmp", [128, 8, 128], F32)
        st = P.sb("st", [128, 4, 8], F32)
        P.dma("sp", wT32[:], w.sguwT, writes=[wT32], sem="ld_s0")
        P.dma("sp", lg[:], w.bc[:, BC["slg"][0]:BC["slg"][0] + 1024], writes=[lg], sem="ld_s1")
        P.dma("sp", lb[:], w.bc[:, BC["slb"][0]:BC["slb"][0] + 1024], writes=[lb], sem="ld_s2")
        P.dma("sp", bsb[:].rearrange("p g t -> p (g t)"), w.bc[:, BC["sgub"][0]:BC["sgub"][0] + 1024], writes=[bsb], sem="ld_s3")
        P.tt("dve", wTm[:], wT32[:], C.k32("sguT").unsqueeze(1).to_broadcast([128, 8, 128]), ALU.mult, [wT32, C.c32], [wTm])
        for g4 in range(NB // 4):
            u_, y_ = su[g4 % 2], yb[g4 % 2]
            P.dma("sp", u_[:], S.suT.rearrange("(g c) t -> c g t", c=128)[:, :, g4 * 512:(g4 + 1) * 512],
                  reads=[SB["suT"]], writes=[u_], sem=f"ld_su{g4 % 2}")
            for bi in range(4):
                sp_ = g4 * 4 + bi
                v_ = sv[sp_ % 2]
                P.dma("sp", v_[:], S.sv[sp_ * 128:(sp_ + 1) * 128, :], reads=[SB["sv"]], writes=[v_], sem=f"ld_sv{sp_ % 2}")
                v3 = v_[:].rearrange("p (g c) -> p g c", g=8)
                P.op("dve", lambda e: e.tensor_reduce(out=st[:, 0, :], in_=v3, axis=AX.X, op=ALU.add), reads=[v_], writes=[st])
                P.act(sq[:], v_[:], AF.Square, [v_], [sq])
                P.op("dve", lambda e: e.tensor_reduce(out=st[:, 1, :], in_=sq[:].rearrange("p (g c) -> p g c", g=8),
                                                      axis=AX.X, op=ALU.add), reads=[sq], writes=[st])
                P.ts("dve", st[:, 2, :], st[:, 0, :], 1.0 / 128, None, ALU.mult, None, [st], [st])
                P.tt("dve", st[:, 0, :], st[:, 2, :], st[:, 2, :], ALU.mult, [st], [st])
                P.stt("dve", st[:, 1, :], st[:, 1, :], 1.0 / 128, st[:, 0, :], ALU.mult, ALU.subtract, [st], [st])
                P.act(st[:, 3, :], st[:, 1, :], AF.Ln, [st], [st], bias=LN_EPS)
                P.act(st[:, 3, :], st[:, 3, :], AF.Exp, [st], [st], scale=-0.5)
                vn3 = vn[:].rearrange("p (g c) -> p g c", g=8)
                P.tt("dve", vn3, v3, st[:, 2, :].unsqueeze(2).to_broadcast([128, 8, 128]), ALU.subtract, [v_, st], [vn])
                P.tt("pool", vn3, vn3, st[:, 3, :].unsqueeze(2).to_broadcast([128, 8, 128]), ALU.mult, [vn, st], [vn])
                P.tt("pool", vn[:], vn[:], lg[:], ALU.mult, [vn, lg], [vn])
                vg_ = vg[sp_ % 2]
                P.tt("dve", vg_[:], vn[:], lb[:], ALU.add, [vn, lb], [vg_])
                for half in range(2):
                    ps = C.ps[(sp_ * 2 + half) % 4]
                    for gg in range(4):
                        g = half * 4 + gg
                        P.mm(ps[:, gg * 128:(gg + 1) * 128], vg_[:, g * 128:(g + 1) * 128], wTm[:, g, :], True, True,
                             [vg_, wTm], [ps], signal=(gg == 3))
                    t3 = tmp[:, half * 4:(half + 1) * 4, :]
                    P.tt("dve", t3, ps[:].rearrange("p (g t) -> p g t", g=4), bsb[:, half * 4:(half + 1) * 4, :], ALU.add,
                         [ps, bsb], [tmp])
                    P.tt("pool", y_[:, half * 4:(half + 1) * 4, bi * 128:(bi + 1) * 128], t3,
                         u_[:, half * 4:(half + 1) * 4, bi * 128:(bi + 1) * 128], ALU.mult, [tmp, u_], [y_])
            P.dma("sp", S.ybT.rearrange("(g c) t -> c g t", c=128)[:, :, g4 * 512:(g4 + 1) * 512], y_[:], reads=[y_],
                  writes=[SB["ybT"]], sem=f"st_yb{g4 % 2}")


def stageE1(C, w):
    P, S, SB, T = C.P, C.S, C.SB, C.T
    with Scope(P):
        wp = [P.sb(f"wp{i}", [128, 8, D], BF16) for i in range(3)]
        ys = [[P.sb(f"ys{i}_{j}", [128, 8, 512], BF16) for j in range(3)] for i in range(2)]
        gt = [P.sb(f"gt{i}", [128, 3, 512], BF16) for i in range(2)]
        acc = P.sb("acc", [128, 512], F32)
        t1 = P.sb("t1", [128, 512], F32)
        t2 = P.sb("t2", [128, 512], F32)
        mt = [P.sb(f"mt{i}", [128, 512], BF16) for i in range(2)]
        for i, wsrc in enumerate((w.wpa, w.wpb, w.wpc)):
            for hh in range(2):
                P.dma("pool", wp[i][:, :, hh * 1024:(hh + 1) * 1024],
                      wsrc.rearrange("(kc p) n -> p kc n", p=128)[:, :, hh * 1024:(hh + 1) * 1024], writes=[wp[i]], sem=f"ld_wp{i}")
        srcs = [(S.yaT, "yaT"), (S.ybT, "ybT"), (S.ycT, "ycT")]
        gi = 0
        for tt in range(T // 512):
            yt = ys[tt % 2]
            for i, (src, nm) in enumerate(srcs):
                P.dma("sp", yt[i][:], src.rearrange("(kc p) t -> p kc t", p=128)[:, :, tt * 512:(tt + 1) * 512],
                      reads=[SB[nm]], writes=[yt[i]], sem=f"ld_ys{tt % 2}_{i}")
            for fc in range(16):
                g_ = gt[gi % 2]
                m_ = mt[gi % 2]
                P.dma("sp", g_[:], S.mgT.rearrange("(b fc p) t -> p b fc t", b=3, p=128)[:, :, fc, tt * 512:(tt + 1) * 512],
                      reads=[SB["mgT"]], writes=[g_], sem=f"ld_gt{gi % 2}")
                pss = [C.ps[(gi % 2) * 3 + i] for i in range(3)]
                gi += 1
                for i in range(3):
                    for kc in range(8):
                        P.mm(pss[i][:], wp[i][:, kc, fc * 128:(fc + 1) * 128], yt[i][:, kc, :], kc == 0, kc == 7,
                             [wp[i], yt[i]], [pss[i]])
                P.tt("dve", acc[:], pss[0][:], g_[:, 0, :], ALU.mult, [pss[0], g_], [acc])
                P.tt("dve", t1[:], pss[1][:], g_[:, 1, :], ALU.mult, [pss[1], g_], [t1])
                P.tt("dve", t2[:], pss[2][:], g_[:, 2, :], ALU.mult, [pss[2], g_], [t2])
                P.tt("pool", acc[:], acc[:], t1[:], ALU.add, [acc, t1], [acc])
                P.tt("pool", m_[:], acc[:], t2[:], ALU.add, [acc, t2], [m_])
                P.dma("sp", S.mT[fc * 128:(fc + 1) * 128, tt * 512:(tt + 1) * 512], m_[:], reads=[m_], writes=[SB["mT"]],
                      sem=f"st_mt{(gi - 1) % 2}")


def layer_norm_tile(C, y, st6, mv, sc, gt_, bt_):
    P = C.P
    for q in range(4):
        P.op("dve", lambda e, q=q: e.bn_stats(out=st6[:, q, :], in_=y[:, q * 512:(q + 1) * 512]), reads=[y], writes=[st6])
    P.op("dve", lambda e: e.bn_aggr(out=mv[:], in_=st6[:]), reads=[st6], writes=[mv])
    P.act(sc[:, 0:1], mv[:, 1:2], AF.Ln, [mv], [sc], bias=LN_EPS)
    P.act(sc[:, 0:1], sc[:, 0:1], AF.Exp, [sc], [sc], scale=-0.5)
    P.stt("dve", sc[:, 1:2], mv[:, 0:1], -1.0, sc[:, 0:1], ALU.mult, ALU.mult, [mv, sc], [sc])
    P.act(y[:], y[:], AF.Identity, [y, sc], [y], bias=sc[:, 1:2], scale=sc[:, 0:1])
    P.tt("pool", y[:], y[:], gt_[:], ALU.mult, [y, gt_], [y])
    P.tt("dve", y[:], y[:], bt_[:], ALU.add, [y, bt_], [y])


def stageE2(C, w, xin, xin_b):
    P, S, SB, T, NB = C.P, C.S, C.SB, C.T, C.NB
    with Scope(P):
        wo = P.sb("wo", [128, 16, D], BF16)
        mt = [P.sb(f"mtl{i}", [128, 16, 512], BF16) for i in range(2)]
        yt = [P.sb(f"yt{i}", [128, D], F32) for i in range(2)]
        xb = [P.sb(f"xb{i}", [128, D], BF16) for i in range(2)]
        lg = P.sb("lg", [128, D], F32)
        lb = P.sb("lb", [128, D], F32)
        st6 = P.sb("st6", [128, 4, 6], F32)
        mv = P.sb("mv", [128, 2], F32)
        sc = P.sb("sc", [128, 2], F32)
        ts = TStore(C, S.x1T, SB["x1T"], "e2")
        for q in range(4):
            P.dma("pool", wo[:, :, q * 512:(q + 1) * 512], w.wout.rearrange("(kc p) n -> p kc n", p=128)[:, :, q * 512:(q + 1) * 512],
                  writes=[wo], sem="ld_wo")
        P.dma("sp", lg[:], w.bc[:, BC["ln1g"][0]:BC["ln1g"][0] + D], writes=[lg], sem="ld_s0")
        P.dma("sp", lb[:], w.bc[:, BC["ln1b"][0]:BC["ln1b"][0] + D], writes=[lb], sem="ld_s1")
        for tb in range(NB):
            g, bi = tb // 4, tb % 4
            m_ = mt[g % 2]
            if bi == 0:
                P.dma("sp", m_[:], S.mT.rearrange("(kc p) t -> p kc t", p=128)[:, :, g * 512:(g + 1) * 512],
                      reads=[SB["mT"]], writes=[m_], sem=f"ld_mt{g % 2}")
            y_ = yt[tb % 2]
            P.dma("sp", y_[:], xin[tb * 128:(tb + 1) * 128, :], reads=[xin_b], writes=[y_], sem=f"ld_y{tb % 2}")
            for cg in range(4):
                ps = C.ps[cg]
                for kc in range(16):
                    P.mm(ps[:], m_[:, kc, bi * 128:(bi + 1) * 128], wo[:, kc, cg * 512:(cg + 1) * 512], kc == 0, kc == 15,
                         [m_, wo], [ps])
                P.stt("dve", y_[:, cg * 512:(cg + 1) * 512], y_[:, cg * 512:(cg + 1) * 512], ALPHA, ps[:], ALU.mult, ALU.add,
                      [y_, ps], [y_])
            layer_norm_tile(C, y_, st6, mv, sc, lg, lb)
            P.dma("pool", S.xr1[tb * 128:(tb + 1) * 128, :], y_[:], reads=[y_], writes=[SB["xr1"]], sem=f"st_y{tb % 2}")
            b_ = xb[tb % 2]
            P.cp("act", b_[:], y_[:], [y_], [b_])
            ts.push(b_, tb)


def stageF(C, w, xout, xout_b):
    P, S, SB, T = C.P, C.S, C.SB, C.T
    NP = DFF // 128
    with Scope(P):
        x1 = P.sb("x1", [128, 16, 512], BF16)
        wu = [P.sb(f"wu{i}", [128, 2, 16, 256], BF16) for i in range(2)]
        aT = P.sb("aT", [128, NP, 512], BF16)
        wd = [P.sb(f"wd{i}", [128, NP, 256], BF16) for i in range(2)]
        yt = P.sb("ytf", [128, 4, D], F32)
        lg = P.sb("lg", [128, D], F32)
        lb = P.sb("lb", [128, D], F32)
        fcw = P.sb("fcw", [128, 86, 3], F32)
        fcb = P.sb("fcb", [128, 86], F32)
        hal = P.sb("hal", [128, 86, 2], F32)
        hb = [P.sb(f"hb{i}", [128, 2, 514], F32) for i in range(1)]
        cv = [P.sb(f"cv{i}", [128, 2, 512], F32) for i in range(1)]
        sg = P.sb("sg", [128, 512], F32)
        st6 = P.sb("st6", [128, 4, 6], F32)
        mv = P.sb("mv", [128, 2], F32)
        sc = P.sb("sc", [128, 2], F32)
        P.dma("sp", lg[:], w.bc[:, BC["ln2g"][0]:BC["ln2g"][0] + D], writes=[lg], sem="ld_s0")
        P.dma("sp", lb[:], w.bc[:, BC["ln2b"][0]:BC["ln2b"][0] + D], writes=[lb], sem="ld_s1")
        P.dma("sp", fcw[:].rearrange("p c k -> p (c k)"), w.pc[:, PC["fcw"][0]:PC["fcw"][0] + 258], writes=[fcw], sem="ld_s2")
        P.dma("sp", fcb[:], w.pc[:, PC["fcb"][0]:PC["fcb"][0] + 86], writes=[fcb], sem="ld_s3")
        P.op("pool", lambda e: e.memset(hal[:], 0.0), writes=[hal])
        wui = 0
        wdi = 0
        pi = 0
        hi = 0
        wupv = w.wup.rearrange("(kc p) (two n) -> p two kc n", p=128, two=2)
        for tt in range(T // 512):
            P.dma("sp", x1[:], S.x1T.rearrange("(kc p) t -> p kc t", p=128)[:, :, tt * 512:(tt + 1) * 512],
                  reads=[SB["x1T"]], writes=[x1], sem="ld_x1")
            for tb in range(4):
                r0 = tt * 512 + tb * 128
                P.dma("sp", yt[:, tb, :], S.xr1[r0:r0 + 128, :], reads=[SB["xr1"]], writes=[yt], sem="ld_ytf")
            for fc in range(NP):
                if fc % 2 == 0:
                    wb = wu[wui % 2]
                    wui += 1
                    if PRECAST:
                        P.dma("sp", wb[:], S.wupb[fc // 2], reads=[SB["wupb"]], writes=[wb], sem=f"ld_wu{(wui - 1) % 2}")
                    else:
                        ncol = min(256, DFF - fc * 128)
                        for two in range(2):
                            P.dma("pool", wb[:, two, :, 0:ncol], wupv[:, two, :, fc * 128:fc * 128 + ncol], writes=[wb],
                                  sem=f"ld_wu{(wui - 1) % 2}")
                co = (fc % 2) * 128
                pg, pv = C.ps[pi % 4], C.ps[(pi + 1) % 4]
                pi += 2
                for two, ps in ((0, pg), (1, pv)):
                    for kc in range(16):
                        P.mm(ps[:], wb[:, two, kc, co:co + 128], x1[:, kc, :], kc == 0, kc == 15, [wb, x1], [ps])
                h_ = hb[0]
                c_ = cv[0]
                hi += 1
                for two, ps in ((0, pg), (1, pv)):
                    ch = two * NP + fc
                    P.cp("pool", h_[:, two, 0:2], hal[:, ch, :], [hal], [h_])
                    P.cp("act", h_[:, two, 2:514], ps[:], [ps], [h_])
                    P.cp("pool", hal[:, ch, :], h_[:, two, 512:514], [h_], [hal])
                    P.ts("dve", c_[:, two, :], h_[:, two, 2:514], fcw[:, ch, 2:3], fcb[:, ch:ch + 1], ALU.mult, ALU.add,
                         [h_, fcw, fcb], [c_])
                    P.stt("dve", c_[:, two, :], h_[:, two, 1:513], fcw[:, ch, 1:2], c_[:, two, :], ALU.mult, ALU.add,
                          [h_, fcw, c_], [c_])
                    P.stt("dve", c_[:, two, :], h_[:, two, 0:512], fcw[:, ch, 0:1], c_[:, two, :], ALU.mult, ALU.add,
                          [h_, fcw, c_], [c_])
                P.act(sg[:], c_[:, 0, :], AF.Silu, [c_], [sg])
                P.tt("pool", aT[:, fc, :], sg[:], c_[:, 1, :], ALU.mult, [sg, c_], [aT])
            for cg in range(D // 256):
                wb = wd[wdi % 2]
                wdi += 1
                if PRECAST:
                    P.dma("sp", wb[:], S.wdnb[cg], reads=[SB["wdnb"]], writes=[wb], sem=f"ld_wd{(wdi - 1) % 2}")
                else:
                    for q, (k0, k1) in enumerate(((0, 22), (22, NP))):
                        P.dma("pool", wb[:, k0:k1, :], w.wdn.rearrange("(kc p) n -> p kc n", p=128)[:, k0:k1, cg * 256:(cg + 1) * 256],
                              writes=[wb], sem=f"ld_wd{(wdi - 1) % 2}")
                for tb in range(4):
                    ps = C.ps[pi % 4]
                    pi += 1
                    for kc in range(NP):
                        P.mm(ps[:, 0:256], aT[:, kc, tb * 128:(tb + 1) * 128], wb[:, kc, :], kc == 0, kc == NP - 1, [aT, wb], [ps])
                    ysl = yt[:, tb, cg * 256:(cg + 1) * 256]
                    P.stt("dve", ysl, ysl, ALPHA, ps[:, 0:256], ALU.mult, ALU.add, [yt, ps], [yt])
            for tb in range(4):
                r0 = tt * 512 + tb * 128
                yv = _YV(yt, tb)
                layer_norm_tile(C, yv, st6, mv, sc, lg, lb)
                P.dma("pool", xout[r0:r0 + 128, :], yt[:, tb, :], reads=[yt], writes=[xout_b], sem="st_yf")


class _YV:
    def __init__(self, t, tb):
        self.t, self.tb, self.b = t, tb, t.b

    def __getitem__(self, k):
        v = self.t.t[:, self.tb, :]
        if isinstance(k, tuple):
            return v[k]
        return v


def precast_ffn(C, w, stg):
    P, S, SB = C.P, C.S, C.SB
    i = 0
    wupv = w.wup.rearrange("(kc p) n -> p kc n", p=128)
    for two in range(2):
        for c0 in range(0, DFF, 512):
            ncol = min(512, DFF - c0)
            st = stg[i % 2]
            sl = i % 2
            i += 1
            P.dma("pool", st[:, :, 0:ncol], wupv[:, :, two * DFF + c0:two * DFF + c0 + ncol], writes=[st], sem=f"ld_pc{sl}")
            yield
            for j in range(0, ncol, 256):
                n2 = min(256, ncol - j)
                pr = (c0 + j) // 256
                P.dma("sp", S.wupb[pr, :, two, :, 0:n2], st[:, :, j:j + n2], reads=[st], writes=[SB["wupb"]], sem=f"st_pc{sl}")
            yield
    wdnv = w.wdn.rearrange("(kc p) n -> p kc n", p=128)
    for c0 in range(0, D, 512):
        for k0 in range(0, 43, 16):
            nk = min(16, 43 - k0)
            st = stg[i % 2]
            sl = i % 2
            i += 1
            P.dma("pool", st[:, 0:nk, :], wdnv[:, k0:k0 + nk, c0:c0 + 512], writes=[st], sem=f"ld_pc{sl}")
            yield
            for j in range(2):
                cg = c0 // 256 + j
                P.dma("sp", S.wdnb[cg, :, k0:k0 + nk, :], st[:, 0:nk, j * 256:(j + 1) * 256], reads=[st], writes=[SB["wdnb"]],
                      sem=f"st_pc{sl}")
            yield


D_HEADS = 8
D_NB = None
D_LEVEL = 5
PRECAST = True


def stageD(C, w, npar=2):
    P, S, SB, T, NB = C.P, C.S, C.SB, C.T, C.NB
    NBH = NB * 8
    with Scope(P):
        sma = P.sb("sma", [128, NB, 8], F32)
        smb = P.sb("smb", [128, NB, 8], F32)
        hb8 = P.sb("hb8", [128, 2, 8], F32)
        nega = P.sb("nega", [128, 8], F32)
        gg_ = P.sb("g", [128, NB, 8], F32)
        gc = P.sb("gc", [128, NB, 8], F32)
        egc = P.sb("egc", [128, NB, 8], F32)
        kdec = P.sb("kdec", [128, NB, 8], F32)
        beta = P.sb("beta", [128, NB, 8], F32)
        nbeta = P.sb("nbeta", [128, NB, 8], F32)
        eglb = P.sb("eglb", [128, 2, NBH], F32)
        ngb = P.sb("ngb", [128, 128], F32)
        gcw = P.sb("gcw", [128, 24, 4], F32)
        f2 = lambda t: t[:].rearrange("p b h -> p (b h)")
        P.dma("sp", sma[:], S.sm.rearrange("(b p) c -> p b c", p=128)[:, :, 8:16], reads=[SB["sm"]], writes=[sma], sem="ld_s0")
        P.dma("sp", smb[:], S.sm.rearrange("(b p) c -> p b c", p=128)[:, :, 16:24], reads=[SB["sm"]], writes=[smb], sem="ld_s1")
        P.dma("sp", hb8[:].rearrange("p a h -> p (a h)"), w.bc[:, BC["alog"][0]:BC["alog"][0] + 16], writes=[hb8], sem="ld_s2")
        P.dma("sp", ngb[:], w.bc[:, BC["gng"][0]:BC["gng"][0] + 128], writes=[ngb], sem="ld_s3")
        P.dma("sp", gcw[:].rearrange("p c k -> p (c k)"), w.pc[:, PC["gcw"][0]:PC["gcw"][0] + 96], writes=[gcw], sem="ld_s4")
        P.act(nega[:], hb8[:, 0, :], AF.Exp, [hb8], [nega])
        P.ts("dve", nega[:], nega[:], -1.0, None, ALU.mult, None, [nega], [nega])
        P.tt("dve", sma[:], sma[:], hb8[:, 1, :].unsqueeze(1).to_broadcast([128, NB, 8]), ALU.add, [sma, hb8], [sma])
        P.act(sma[:], sma[:], AF.Exp, [sma], [sma])
        P.act(sma[:], sma[:], AF.Ln, [sma], [sma], bias=1.0)
        P.tt("dve", gg_[:], sma[:], nega[:].unsqueeze(1).to_broadcast([128, NB, 8]), ALU.mult, [sma, nega], [gg_])
        P.act(beta[:], smb[:], AF.Sigmoid, [smb], [beta])
        P.ts("dve", nbeta[:], beta[:], -1.0, None, ALU.mult, None, [beta], [nbeta])
        p0, p1 = C.ps[0], C.ps[1]
        P.mm(p0[:, 0:NBH], C.k32("triBD"), f2(gg_), True, True, [C.c32, gg_], [p0])
        P.mm(p0[:, NBH:2 * NBH], C.k32("blk"), f2(gg_), True, True, [C.c32, gg_], [p0])
        P.mm(p1[:, 0:NBH], C.k32("half0"), f2(gg_), True, True, [C.c32, gg_], [p1])
        P.mm(p1[:, NBH:2 * NBH], C.k32("half1"), f2(gg_), True, True, [C.c32, gg_], [p1])
        P.cp("dve", f2(gc), p0[:, 0:NBH], [p0], [gc])
        P.act(f2(egc), p0[:, 0:NBH], AF.Exp, [p0], [egc])
        P.tt("dve", f2(kdec), p0[:, NBH:2 * NBH], f2(gc), ALU.subtract, [p0, gc], [kdec])
        P.act(f2(kdec), f2(kdec), AF.Exp, [kdec], [kdec])
        P.act(eglb[:].rearrange("p c n -> p (c n)"), p1[:, 0:2 * NBH], AF.Exp, [p1], [eglb])
        P.barrier()

        chains = []
        for ci in range(npar):
            R = Ctx()
            R.ci = ci
            R.raw = P.sb("raw", [128, T + 3], BF16)
            R.acc = P.sb("acc", [128, T], F32)
            R.cvT = P.sb("cvT", [128, 3, T], BF16)
            R.ggh = P.sb("ggh", [128, NB, 128], BF16)
            R.ycT = P.sb("ycT", [128, T], BF16)
            R.S32 = P.sb("S32", [128, 128], F32)
            R.Sbf = P.sb("Sbf", [128, 128], BF16)
            R.jk = P.sb("jk", [128, 128], F32)
            R.qsq = P.sb("qsq", [128, 128], BF16)
            R.sc = P.sb("sc", [128, 8], F32)
            R.kn = P.sb("kn", [128, 128], BF16)
            R.kb = P.sb("kb", [128, 128], BF16)
            R.kg = P.sb("kg", [128, 128], BF16)
            R.kd = P.sb("kd", [128, 128], BF16)
            R.knT = P.sb("knT", [128, 128], BF16)
            R.kbT = P.sb("kbT", [128, 128], BF16)
            R.vtm = P.sb("vtm", [128, 128], BF16)
            R.gU = P.sb("gU", [128, 128], F32)
            R.E2 = P.sb("E2", [128, 256], F32)
            R.Em = P.sb("Em", [128, 3, 128], F32)
            R.N = [P.sb(f"N{i}", [128, 128], BF16) for i in range(2)]
            R.NT = [P.sb(f"NT{i}", [128, 128], BF16) for i in range(2)]
            R.qkT = P.sb("qkT", [128, 128], BF16)
            R.X = [P.sb(f"X{i}", [128, 128], BF16) for i in range(2)]
            R.ub = P.sb("ub", [128, 128], F32)
            R.wT = P.sb("wT", [128, 128], BF16)
            R.vnew = P.sb("vnew", [128, 128], BF16)
            R.t1 = P.sb("t1", [128, 128], F32)
            R.o = P.sb("o", [128, 128], F32)
            R.y1 = P.sb("y1", [128, 128], BF16)
            bA, bB, bC = C.ps[ci * 3], C.ps[ci * 3 + 1], C.ps[ci * 3 + 2]
            pb = C.pb[ci]
            R.pA, R.pB, R.pC, R.pb = bA, bB, bC, pb
            R.rg = {"kk": bA.b, "ssq": bA.b, "dd": bB.b, "dbl": bA.b, "x": bC.b, "ws": bC.b, "qs": bC.b, "qv": bC.b,
                    "tk": pb.b, "tn": pb.b, "tv": pb.b, "ty": pb.b}
            chains.append(R)

        def chain(R, heads):
            ci = R.ci
            rg = R.rg
            pA, pB, pC, pb = R.pA, R.pB, R.pC, R.pb
            for h in heads:
                P.op("pool", lambda e: e.memset(R.raw[:, 0:3], 0.0), writes=[R.raw])
                P.dma("sp", R.ggh[:], S.gg.rearrange("(b p) e -> p b e", p=128)[:, :, h * 128:(h + 1) * 128],
                      reads=[SB["gg"]], writes=[R.ggh], sem=f"ld_ggh{ci}")
                yield
                for i in range(3):
                    r0 = i * 1024 + h * 128
                    P.dma("sp", R.raw[:, 3:3 + T], S.g3T[r0:r0 + 128, :], reads=[SB["g3T"]], writes=[R.raw], sem=f"ld_raw{ci}")
                    chn = i * 8 + h
                    P.ts("dve", R.acc[:], R.raw[:, 3:3 + T], gcw[:, chn, 3:4], None, ALU.mult, None, [R.raw, gcw], [R.acc])
                    yield
                    for tap in (2, 1, 0):
                        eng = "dve"
                        P.stt(eng, R.acc[:], R.raw[:, tap:tap + T], gcw[:, chn, tap:tap + 1], R.acc[:], ALU.mult, ALU.add,
                              [R.raw, gcw, R.acc], [R.acc])
                        yield
                    P.act(R.cvT[:, i, :], R.acc[:], AF.Silu, [R.acc], [R.cvT])
                    yield
                P.act(R.ggh[:], R.ggh[:], AF.Silu, [R.ggh], [R.ggh])
                P.tt("pool", R.ggh[:], R.ggh[:], ngb[:].unsqueeze(1).to_broadcast([128, NB, 128]), ALU.mult, [R.ggh, ngb], [R.ggh])
                P.op("pool", lambda e: e.memset(R.S32[:], 0.0), writes=[R.S32])
                P.op("pool", lambda e: e.memset(R.Sbf[:], 0.0), writes=[R.Sbf])
                yield
                qcT, kcT, vcT = R.cvT[:, 0, :], R.cvT[:, 1, :], R.cvT[:, 2, :]
                for b in range(NB if D_NB is None else D_NB):
                    if D_LEVEL < 2.1:
                        break
                    cols = slice(b * 128, (b + 1) * 128)
                    bh = b * 8 + h
                    col = lambda t: t[:, b, h:h + 1]
                    P.tr(pb[:, 0:128], kcT[:, cols], C.kbf("ident"), [R.cvT, C.cbf], [rg["tk"]])
                    P.tr(pb[:, 384:512], vcT[:, cols], C.kbf("ident"), [R.cvT, C.cbf], [rg["tv"]])
                    yield
                    P.act(R.jk[:], pb[:, 0:128], AF.Square, [rg["tk"]], [R.jk, R.sc], accum_out=R.sc[:, 0:1])
                    P.act(R.sc[:, 1:2], R.sc[:, 0:1], AF.Ln, [R.sc], [R.sc], bias=RMS_EPS)
                    P.act(R.sc[:, 1:2], R.sc[:, 1:2], AF.Exp, [R.sc], [R.sc], scale=-0.5)
                    yield
                    P.ts("dve", R.kn[:], pb[:, 0:128], R.sc[:, 1:2], None, ALU.mult, None, [rg["tk"], R.sc], [R.kn])
                    P.cp("act", R.vtm[:], pb[:, 384:512], [rg["tv"]], [R.vtm])
                    yield
                    if D_LEVEL < 2.3:
                        continue
                    P.act(R.kb[:], R.kn[:], AF.Identity, [R.kn, beta], [R.kb], scale=col(beta))
                    P.act(R.kg[:], R.kn[:], AF.Identity, [R.kn, egc], [R.kg], scale=col(egc))
                    P.act(R.kd[:], R.kn[:], AF.Identity, [R.kn, kdec], [R.kd], scale=col(kdec))
                    P.ts("dve", R.gU[:], C.k32("triBD"), col(gg_), None, ALU.mult, None, [C.c32, gg_], [R.gU])
                    yield
                    if D_LEVEL < 2.35:
                        continue
                    P.tr(pb[:, 128:256], R.kn[:], C.kbf("ident"), [R.kn, C.cbf], [rg["tn"]])
                    P.tr(pb[:, 256:384], R.kb[:], C.kbf("ident"), [R.kb, C.cbf], [rg["tn"]])
                    yield
                    P.cp("act", R.knT[:], pb[:, 128:256], [rg["tn"]], [R.knT])
                    P.cp("dve", R.kbT[:], pb[:, 256:384], [rg["tn"]], [R.kbT])
                    if D_LEVEL < 2.5:
                        continue
                    P.act(R.qsq[:], qcT[:, cols], AF.Square, [R.cvT], [R.qsq])
                    yield
                    P.mm(pA[:, 384:385], R.qsq[:], C.kbf("ones")[:, 0:1], True, True, [R.qsq, C.cbf], [rg["ssq"]])
                    if D_LEVEL < 2.6:
                        continue
                    P.mm(pB[:, 0:128], R.gU[:], C.k32("lm"), True, True, [R.gU, C.c32], [rg["dd"]], signal=False)
                    P.mm(pB[:, 128:256], C.k32("lm"), R.gU[:], True, True, [R.gU, C.c32], [rg["dd"]])
                    P.mm(pA[:, 0:128], R.knT[:], R.kbT[:], True, True, [R.knT, R.kbT], [rg["kk"]], signal=False)
                    P.mm(pA[:, 128:256], R.kbT[:], R.knT[:], True, True, [R.knT, R.kbT], [rg["kk"]], signal=False)
                    P.mm(pA[:, 256:384], R.knT[:], qcT[:, cols], True, True, [R.knT, R.cvT], [rg["kk"]])
                    yield
                    if D_LEVEL < 2.8:
                        continue
                    P.act(R.sc[:, 2:3], pA[:, 384:385], AF.Ln, [rg["ssq"]], [R.sc], bias=RMS_EPS)
                    P.act(R.sc[:, 2:3], R.sc[:, 2:3], AF.Exp, [R.sc], [R.sc], scale=-0.5)
                    P.act(R.E2[:], pB[:, 0:256], AF.Exp, [rg["dd"]], [R.E2])
                    yield
                    P.tt("pool", R.Em[:, 0, :], R.E2[:, 0:128], C.k32("sm"), ALU.mult, [R.E2, C.c32], [R.Em])
                    P.tt("pool", R.Em[:, 1, :], R.E2[:, 128:256], C.k32("smT"), ALU.mult, [R.E2, C.c32], [R.Em])
                    P.tt("pool", R.Em[:, 2, :], R.E2[:, 128:256], C.k32("inclT"), ALU.mult, [R.E2, C.c32], [R.Em])
                    yield
                    P.stt("dve", R.NT[0][:], pA[:, 0:128], -1.0, R.Em[:, 0, :], ALU.mult, ALU.mult, [rg["kk"], R.Em], [R.NT[0]])
                    P.stt("dve", R.N[0][:], pA[:, 128:256], -1.0, R.Em[:, 1, :], ALU.mult, ALU.mult, [rg["kk"], R.Em], [R.N[0]])
                    P.tt("dve", R.qkT[:], pA[:, 256:384], R.Em[:, 2, :], ALU.mult, [rg["kk"], R.Em], [R.qkT])
                    P.tt("pool", R.X[0][:], R.N[0][:], C.kbf("ident"), ALU.add, [R.N[0], C.cbf], [R.X[0]])
                    yield
                    if D_LEVEL < 3.5:
                        continue
                    cur = 0
                    for s in range(5):
                        Nc, NTc = R.N[cur], R.NT[cur]
                        Nn, NTn = R.N[1 - cur], R.NT[1 - cur]
                        if s < 4:
                            P.mm(pA[:, 0:128], NTc[:], Nc[:], True, True, [Nc, NTc], [rg["dbl"]], signal=False)
                        P.mm(pA[:, 128:256], Nc[:], NTc[:], True, True, [Nc, NTc], [rg["dbl"]])
                        yield
                        if D_LEVEL < 3.58:
                            continue
                        if s < 4:
                            P.cp("act", Nn[:], pA[:, 0:128], [rg["dbl"]], [Nn])
                        P.cp("dve", NTn[:], pA[:, 128:256], [rg["dbl"]], [NTn])
                        yield
                        if D_LEVEL < 3.65:
                            continue
                        Xc, Xn = R.X[s % 2], R.X[(s + 1) % 2]
                        P.mm(pC[:, 0:128], NTn[:], Xc[:], True, True, [NTn, Xc], [rg["x"]])
                        yield
                        P.tt("dve", Xn[:], pC[:, 0:128], Xc[:], ALU.add, [rg["x"], Xc], [Xn])
                        yield
                        cur = 1 - cur
                    if D_LEVEL < 3.8:
                        continue
                    Xf = R.X[5 % 2]
                    P.mm(pC[:, 0:128], Xf[:], R.vtm[:], True, True, [Xf, R.vtm], [rg["x"]])
                    yield
                    P.ts("dve", R.ub[:], pC[:, 0:128], col(beta), None, ALU.mult, None, [rg["x"], beta], [R.ub])
                    yield
                    P.mm(pC[:, 0:128], R.kg[:], Xf[:], True, True, [R.kg, Xf], [rg["x"]])
                    yield
                    P.cp("act", R.wT[:], pC[:, 0:128], [rg["x"]], [R.wT])
                    yield
                    if D_LEVEL < 5:
                        continue
                    for c in range(2):
                        r = slice(c * 64, (c + 1) * 64)
                        P.mm(pC[:, 128:256], R.wT[:], R.Sbf[:], True, True, [R.wT, R.Sbf], [rg["ws"]])
                        P.mm(pC[:, 256:384], qcT[:, cols], R.Sbf[:], True, True, [R.cvT, R.Sbf], [rg["qs"]])
                        yield
                        P.stt("dve", R.vnew[r, :], pC[r, 128:256], nbeta[r, b, h:h + 1], R.ub[r, :], ALU.mult, ALU.add,
                              [rg["ws"], nbeta, R.ub], [R.vnew])
                        P.act(R.t1[r, :], pC[r, 256:384], AF.Identity, [rg["qs"], egc], [R.t1], scale=egc[r, b, h:h + 1])
                        yield
                        P.mm(pC[:, 384:512], R.qkT[r, :], R.vnew[r, :], True, True, [R.qkT, R.vnew], [rg["qv"]])
                        yield
                        P.tt("dve", R.o[r, :], pC[r, 384:512], R.t1[r, :], ALU.add, [rg["qv"], R.t1], [R.o])
                        yield
                        P.mm(pC[:, 384:512], R.kd[r, :], R.vnew[r, :], True, True, [R.kd, R.vnew], [rg["qv"]])
                        yield
                        P.stt("dve", R.S32[:], R.S32[:], eglb[:, c, bh:bh + 1], pC[:, 384:512], ALU.mult, ALU.add,
                              [R.S32, eglb, rg["qv"]], [R.S32])
                        P.cp("act", R.Sbf[:], R.S32[:], [R.S32], [R.Sbf])
                        yield
                    P.ts("dve", R.sc[:, 3:4], R.sc[:, 2:3], 128 ** -0.5, None, ALU.mult, None, [R.sc], [R.sc])
                    P.act(R.jk[:], R.o[:], AF.Square, [R.o, R.sc], [R.jk, R.sc], scale=R.sc[:, 3:4], accum_out=R.sc[:, 4:5])
                    P.act(R.sc[:, 5:6], R.sc[:, 4:5], AF.Ln, [R.sc], [R.sc], bias=RMS_EPS, scale=1.0 / 128)
                    P.act(R.sc[:, 5:6], R.sc[:, 5:6], AF.Exp, [R.sc], [R.sc], scale=-0.5)
                    P.tt("dve", R.sc[:, 6:7], R.sc[:, 5:6], R.sc[:, 3:4], ALU.mult, [R.sc], [R.sc])
                    yield
                    P.stt("dve", R.y1[:], R.o[:], R.sc[:, 6:7], R.ggh[:, b, :], ALU.mult, ALU.mult, [R.o, R.sc, R.ggh], [R.y1])
                    yield
                    P.tr(pb[:, 0:128], R.y1[:], C.kbf("ident"), [R.y1, C.cbf], [rg["ty"]])
                    yield
                    P.cp("act", R.ycT[:, cols], pb[:, 0:128], [rg["ty"]], [R.ycT])
                    yield
                P.dma("sp", S.ycT[h * 128:(h + 1) * 128, :], R.ycT[:], reads=[R.ycT], writes=[SB["ycT"]], sem=f"st_yc{ci}")
                yield

        gens = [chain(chains[ci], list(range(ci, D_HEADS, npar))) for ci in range(npar)] if D_LEVEL >= 2 else []
        if PRECAST:
            stg = [P.sb(f"pcs{i}", [128, 16, 512], BF16) for i in range(2)]
            gens.append(precast_ffn(C, w, stg))
        alive = list(gens)
        while alive:
            for g in list(alive):
                try:
                    next(g)
                except StopIteration:
                    alive.remove(g)


_CACHE = {}
FUSED = True


def _prog(nl):
    if nl not in _CACHE:
        _CACHE[nl] = build(nl, SEQ)[0]
    return _CACHE[nl]


def _layer_map(inp, l, slot):
    bc, pc, sguwT = layer_small(inp, l)
    f = lambda a: np.ascontiguousarray(a, dtype=np.float32)
    return {f"w_in{slot}": f(inp["w_in"][l]), f"bc{slot}": bc, f"pc{slot}": pc, f"sguwT{slot}": sguwT,
            f"wpa{slot}": f(inp["w_proj_a"][l]), f"wpb{slot}": f(inp["w_proj_b"][l]), f"wpc{slot}": f(inp["w_proj_c"][l]),
            f"wout{slot}": f(inp["w_out"][l]), f"wup{slot}": f(inp["ffn_w_up"][l]), f"wdn{slot}": f(inp["ffn_w_down"][l])}


def kernel(**inp):
    inp = {k: np.asarray(v) for k, v in inp.items()}
    x = inp["x"].astype(np.float32, copy=False)
    B = x.shape[0]
    consts = make_consts()
    cur = [np.ascontiguousarray(x[b]) for b in range(B)]
    if FUSED:
        nc = _prog(DEPTH)
        wm = {}
        for l in range(DEPTH):
            wm.update(_layer_map(inp, l, l))
        in_maps = [dict(wm, x=cur[b], consts=consts) for b in range(B)]
        res = run_bass_kernel_spmd(nc, in_maps, core_ids=list(range(B)))
        cur = [np.asarray(res.results[b]["y"]) for b in range(B)]
    else:
        nc = _prog(1)
        for l in range(DEPTH):
            wm = _layer_map(inp, l, 0)
            in_maps = [dict(wm, x=cur[b], consts=consts) for b in range(B)]
            res = run_bass_kernel_spmd(nc, in_maps, core_ids=list(range(B)))
            cur = [np.asarray(res.results[b]["y"]) for b in range(B)]
    return np.stack(cur).astype(np.float32)
```

```python
import numpy as np
from contextlib import ExitStack
import concourse.bass as bass
import concourse.mybir as mybir
from concourse.bass_utils import run_bass_kernel_spmd

F32 = mybir.dt.float32
BF16 = mybir.dt.bfloat16
AF = mybir.ActivationFunctionType
ALU = mybir.AluOpType
AX = mybir.AxisListType

D = 2048
NIN = 15384
DFF = 5504
SEQ = 4096
DEPTH = 4
ALPHA = (2 * DEPTH) ** 0.25
LN_EPS = 1e-5
RMS_EPS = 1e-6
OFF = dict(fq=0, fk=1024, fv=2048, ff=3072, su=3080, sv=4104, gq=5128, gk=6152, gv=7176,
           ga=8200, gb=8208, gg=8216, mg=9240)


class Buf:
    __slots__ = ("name", "w", "r", "excl")

    def __init__(self, name="b"):
        self.name = name
        self.w = {}
        self.r = {}
        self.excl = False


class Tl:
    def __init__(self, t, name):
        self.t = t
        self.b = Buf(name)

    def __getitem__(self, k):
        return self.t[k]


class Prog:
    def __init__(self, nc, es):
        self.nc = nc
        self.es = es
        self.ges = es
        self.engs = {"pe": nc.tensor, "act": nc.scalar, "dve": nc.vector,
                     "pool": nc.gpsimd, "sp": nc.sync}
        self.sems = {}
        self.cnt = {}
        self.waited = {e: {} for e in self.engs}
        self.nwait = 0
        self.nops = 0
        self.uid = 0
        self.ekey = {e: e for e in self.engs}

    def sem(self, key):
        if key not in self.sems:
            self.sems[key] = self.ges.enter_context(self.nc.semaphore("s_" + key))
            self.cnt[key] = 0
        return self.sems[key]

    def _wait(self, eng, deps, own=None):
        for k, c in deps.items():
            if c <= 0:
                continue
            if k == own and eng == "pe":
                continue
            if self.waited[eng].get(k, 0) >= c:
                continue
            assert c <= self.cnt[k], f"forward wait {eng} on {k}: {c} > {self.cnt[k]}"
            self.engs[eng].wait_ge(self.sem(k), c)
            self.waited[eng][k] = c
            self.nwait += 1

    @staticmethod
    def _collect(reads, writes):
        deps = {}
        for t in reads:
            for k, c in t.w.items():
                if deps.get(k, 0) < c:
                    deps[k] = c
        for t in writes:
            for k, c in t.w.items():
                if deps.get(k, 0) < c:
                    deps[k] = c
            for k, c in t.r.items():
                if deps.get(k, 0) < c:
                    deps[k] = c
        return deps

    @staticmethod
    def _bufs(xs):
        return [x.b if hasattr(x, "b") else x for x in xs]

    def op(self, eng, fn, reads=(), writes=(), signal=True):
        reads = self._bufs(reads)
        writes = self._bufs(writes)
        writes = writes + [t for t in reads if t.excl and t not in writes]
        key = self.ekey[eng]
        self.sem(key)
        assert signal or eng == "pe"
        self._wait(eng, self._collect(reads, writes), own=key)
        ins = fn(self.engs[eng])
        self.nops += 1
        if signal:
            self.cnt[key] += 1
            ins.then_inc(self.sems[key], 1)
            o = self.cnt[key]
        else:
            o = self.cnt[key] + 1
        for t in reads:
            t.r[key] = o
        for t in writes:
            t.w[key] = o
        return ins

    def epoch(self, tag):
        self.barrier()
        self.ekey = {e: f"{e}_{tag}" for e in self.engs}

    def dma(self, q, out, in_, reads=(), writes=(), sem=None, **kw):
        reads = self._bufs(reads)
        writes = self._bufs(writes)
        self.sem(sem)
        deps = self._collect(reads, writes)
        deps[sem] = max(deps.get(sem, 0), self.cnt[sem])
        self._wait(q, deps)
        ins = self.engs[q].dma_start(out=out, in_=in_, **kw)
        ins.then_inc(self.sems[sem], 16)
        self.nops += 1
        self.cnt[sem] += 16
        o = self.cnt[sem]
        for t in reads:
            t.r[sem] = o
        for t in writes:
            t.w[sem] = o
        return ins

    def barrier(self):
        for e in self.engs:
            self._wait(e, dict(self.cnt), own=self.ekey[e])

    def sb(self, name, shape, dt):
        self.uid += 1
        nm = f"{name}_{self.uid}"
        return Tl(self.es.enter_context(self.nc.sbuf_tensor(nm, list(shape), dt)), nm)

    def mm(self, out, lhsT, rhs, start, stop, reads, writes, signal=None):
        return self.op("pe", lambda e: e.matmul(out, lhsT=lhsT, rhs=rhs, start=start, stop=stop),
                       reads=reads, writes=writes, signal=(stop if signal is None else signal))

    def tr(self, out, in_, ident, reads, writes, signal=True):
        return self.op("pe", lambda e: e.transpose(out, in_, ident), reads=reads, writes=writes, signal=signal)

    def act(self, out, in_, func, reads, writes, bias=0.0, scale=1.0, accum_out=None):
        if accum_out is None:
            return self.op("act", lambda e: e.activation(out=out, in_=in_, func=func, bias=bias, scale=scale),
                           reads=reads, writes=writes)
        return self.op("act", lambda e: e.activation(out=out, in_=in_, func=func, bias=bias, scale=scale,
                                                      accum_out=accum_out), reads=reads, writes=writes)

    def tt(self, eng, out, in0, in1, op, reads, writes):
        return self.op(eng, lambda e: e.tensor_tensor(out=out, in0=in0, in1=in1, op=op), reads=reads, writes=writes)

    def ts(self, eng, out, in0, s1, s2, op0, op1, reads, writes):
        if op1 is None:
            return self.op(eng, lambda e: e.tensor_scalar(out=out, in0=in0, scalar1=s1, scalar2=None, op0=op0),
                           reads=reads, writes=writes)
        return self.op(eng, lambda e: e.tensor_scalar(out=out, in0=in0, scalar1=s1, scalar2=s2, op0=op0, op1=op1),
                       reads=reads, writes=writes)

    def stt(self, eng, out, in0, scalar, in1, op0, op1, reads, writes):
        return self.op(eng, lambda e: e.scalar_tensor_tensor(out=out, in0=in0, scalar=scalar, in1=in1, op0=op0, op1=op1),
                       reads=reads, writes=writes)

    def cp(self, eng, out, in_, reads, writes):
        if eng == "act":
            return self.op("act", lambda e: e.activation(out=out, in_=in_, func=AF.Identity), reads=reads, writes=writes)
        return self.op(eng, lambda e: e.tensor_copy(out=out, in_=in_), reads=reads, writes=writes)


class Scope:
    def __init__(self, P):
        self.P = P

    def __enter__(self):
        self.old = self.P.es
        self.st = ExitStack()
        self.st.__enter__()
        self.P.es = self.st
        return self

    def __exit__(self, *a):
        self.P.barrier()
        self.P.es = self.old
        return self.st.__exit__(*a)


CN = ["ident", "ones", "triKQ", "triU", "triBD", "half0", "half1", "blk", "lm", "sm", "smT", "inclT", "sguT"]


def make_consts():
    i = np.arange(128)
    ch = i // 64
    same = ch[:, None] == ch[None, :]
    c = {}
    c["ident"] = np.eye(128)
    c["ones"] = np.ones((128, 128))
    c["triKQ"] = (i[None, :] >= i[:, None])
    c["triU"] = (i[:, None] <= i[None, :])
    c["triBD"] = same & (i[:, None] <= i[None, :])
    c["half0"] = np.broadcast_to((i < 64)[:, None], (128, 128))
    c["half1"] = np.broadcast_to((i >= 64)[:, None], (128, 128))
    c["blk"] = same
    c["lm"] = same & (i[:, None] > i[None, :])
    c["sm"] = same & (i[:, None] > i[None, :])
    c["smT"] = same & (i[None, :] > i[:, None])
    c["inclT"] = same & (i[None, :] >= i[:, None])
    c["sguT"] = (ch[None, :] >= ch[:, None])
    return np.ascontiguousarray(np.stack([c[n].astype(np.float32) for n in CN], axis=1))


FM_RANGES = [("fq", 0, 1024), ("fk", 1024, 1024), ("su", 3080, 1024), ("g3", 5128, 3072), ("mg", 9240, 6144)]
TM_RANGES = [("fv", 2048, 1024), ("sv", 4104, 1024), ("gg", 8216, 1024)]
SM_COLS = list(range(3072, 3080)) + list(range(8200, 8216))
NFM = sum(r[2] for r in FM_RANGES) // 128

BC = {}
_o = 0
for _n, _w in [("btm", 3072), ("bsm", 24), ("ln1g", D), ("ln1b", D), ("ln2g", D), ("ln2b", D),
               ("slg", 1024), ("slb", 1024), ("gng", 128), ("alog", 8), ("dtb", 8), ("sgub", 1024)]:
    BC[_n] = (_o, _w)
    _o += _w
BCW = _o
PC = {}
_o = 0
for _n, _w in [("bfm", NFM), ("gcw", 24 * 4), ("fcw", 86 * 3), ("fcb", 86)]:
    PC[_n] = (_o, _w)
    _o += _w
PCW = _o


def layer_small(inp, l):
    bin_ = inp["b_in"][l]
    rows = [np.concatenate([bin_[c0:c0 + w] for _, c0, w in TM_RANGES]), bin_[SM_COLS],
            inp["ln1_g"][l], inp["ln1_b"][l], inp["ln2_g"][l], inp["ln2_b"][l],
            inp["sgu_ln_g"][l], inp["sgu_ln_b"][l], inp["gdn_norm_g"][l], inp["gdn_a_log"][l],
            inp["gdn_dt_bias"][l], inp["sgu_b"][l].reshape(-1)]
    row = np.concatenate(rows).astype(np.float32)
    assert row.shape[0] == BCW
    bc = np.ascontiguousarray(np.broadcast_to(row[None, :], (128, BCW)))
    bfm = np.concatenate([bin_[c0:c0 + w] for _, c0, w in FM_RANGES]).reshape(NFM, 128).T
    gcw = inp["gdn_conv_w"][l].reshape(4, 24, 128).transpose(2, 1, 0).reshape(128, 96)
    fcw = inp["ffn_conv_w"][l].reshape(3, 86, 128).transpose(2, 1, 0).reshape(128, 258)
    fcb = inp["ffn_conv_b"][l].reshape(86, 128).T
    pc = np.ascontiguousarray(np.concatenate([bfm, gcw, fcw, fcb], axis=1).astype(np.float32))
    assert pc.shape[1] == PCW
    sguwT = np.ascontiguousarray(inp["sgu_w"][l].transpose(2, 0, 1))
    return bc, pc, sguwT


class Ctx:
    pass


def build(nl, T, dbg_in=(), dbg_out=(), stages="0ABCDEF"):
    nc = bass.Bass("TRN2", target_bir_lowering=False)
    NB = T // 128
    C = Ctx()
    C.nc, C.T, C.NB = nc, T, NB

    def ext_in(name, shape, dt=F32):
        return nc.dram_tensor(name, list(shape), dt, kind="ExternalInput").ap()

    def scr(name, shape, dt):
        kind = "ExternalInput" if name in dbg_in else ("ExternalOutput" if name in dbg_out else "Internal")
        return nc.dram_tensor(name, list(shape), dt, kind=kind).ap()

    x_in = ext_in("x", [T, D])
    consts = ext_in("consts", [128, len(CN), 128])
    W = []
    for l in range(nl):
        w = Ctx()
        w.w_in = ext_in(f"w_in{l}", [D, NIN])
        w.bc = ext_in(f"bc{l}", [128, BCW])
        w.pc = ext_in(f"pc{l}", [128, PCW])
        w.sguwT = ext_in(f"sguwT{l}", [128, 8, 128])
        w.wpa = ext_in(f"wpa{l}", [1024, D])
        w.wpb = ext_in(f"wpb{l}", [1024, D])
        w.wpc = ext_in(f"wpc{l}", [1024, D])
        w.wout = ext_in(f"wout{l}", [D, D])
        w.wup = ext_in(f"wup{l}", [D, 2 * DFF])
        w.wdn = ext_in(f"wdn{l}", [DFF, D])
        W.append(w)
    y_out = nc.dram_tensor("y", [T, D], F32, kind="ExternalOutput").ap()

    S = Ctx()
    S.xr1 = scr("xr1", [T, D], F32)
    S.xr0 = scr("xr0", [T, D], F32)
    S.xT = scr("xT", [D, T], BF16)
    S.x1T = scr("x1T", [D, T], BF16)
    S.fqT = scr("fqT", [1024, T], BF16)
    S.fkT = scr("fkT", [1024, T], BF16)
    S.suT = scr("suT", [1024, T], BF16)
    S.g3T = scr("g3T", [3072, T], BF16)
    S.mgT = scr("mgT", [6144, T], BF16)
    S.fv = scr("fv", [T, 1024], BF16)
    S.sv = scr("sv", [T, 1024], BF16)
    S.gg = scr("gg", [T, 1024], BF16)
    S.sm = scr("sm", [T, 24], F32)
    S.yaT = scr("yaT", [1024, T], BF16)
    S.ybT = scr("ybT", [1024, T], BF16)
    S.ycT = scr("ycT", [1024, T], BF16)
    S.mT = scr("mT", [D, T], BF16)
    S.wupb = scr("wupb", [22, 128, 2, 16, 256], BF16)
    S.wdnb = scr("wdnb", [8, 128, 43, 256], BF16)
    SB = {k: Buf(k) for k in vars(S)}
    SB["y"] = Buf("y")
    SB["x"] = Buf("x")
    C.S, C.SB = S, SB

    with ExitStack() as es:
        P = Prog(nc, es)
        C.P = P
        C.ps = [Tl(es.enter_context(nc.psum_tensor(f"ps{i}", [128, 512], F32)), f"ps{i}") for i in range(6)]
        C.pb = [Tl(es.enter_context(nc.psum_tensor(f"pb{i}", [128, 1024], BF16)), f"pb{i}") for i in range(2)]
        for t_ in C.ps + C.pb:
            t_.b.excl = True
        C.c32 = P.sb("c32", [128, len(CN), 128], F32)
        C.cbf = P.sb("cbf", [128, len(CN), 128], BF16)
        P.dma("sp", C.c32[:], consts, writes=[C.c32], sem="ld_c")
        P.cp("dve", C.cbf[:], C.c32[:], [C.c32], [C.cbf])
        C.k32 = lambda n: C.c32[:, CN.index(n), :]
        C.kbf = lambda n: C.cbf[:, CN.index(n), :]

        for l in range(nl):
            w = W[l]
            xin, xin_b = (x_in, SB["x"]) if l == 0 else (S.xr0, SB["xr0"])
            last = (l == nl - 1)
            if l > 0:
                P.epoch(l)
            xout, xout_b = (y_out, SB["y"]) if last else (S.xr0, SB["xr0"])
            if "0" in stages:
                stage0(C, xin, xin_b)
            if "A" in stages:
                stageA(C, w)
            if "B" in stages:
                stageB(C, w)
            if "C" in stages:
                stageC(C, w)
            if "D" in stages:
                stageD(C, w)
            if "E" in stages:
                stageE1(C, w)
                stageE2(C, w, xin, xin_b)
            if "F" in stages:
                stageF(C, w, xout, xout_b)
        P.barrier()
        C.nops, C.nwait = P.nops, P.nwait
    return nc, C


class TStore:
    def __init__(self, C, dst, dst_buf, name):
        P = C.P
        self.C, self.dst, self.dst_buf = C, dst, dst_buf
        self.tiles = [P.sb(f"{name}_xTt{i}", [128, 16, 512], BF16) for i in range(2)]
        self.name = name
        self.n = 0

    def push(self, src, tb):
        C, P = self.C, self.C.P
        g, bi = tb // 4, tb % 4
        xt = self.tiles[g % 2]
        for q in range(4):
            pb = C.pb[self.n % 2]
            self.n += 1
            for j in range(4):
                kc = q * 4 + j
                P.tr(pb[:, j * 128:(j + 1) * 128], src[:, kc * 128:(kc + 1) * 128], C.kbf("ident"),
                     [src, C.cbf], [pb], signal=(j == 3))
            eng = "act" if (q % 2) else "dve"
            P.cp(eng, xt[:, q * 4:(q + 1) * 4, bi * 128:(bi + 1) * 128],
                 pb[:, 0:512].rearrange("p (j t) -> p j t", j=4), [pb], [xt])
        if bi == 3:
            P.dma("pool", self.dst.rearrange("(kc p) t -> p kc t", p=128)[:, :, g * 512:(g + 1) * 512], xt[:],
                  reads=[xt], writes=[self.dst_buf], sem=f"st_{self.name}{g % 2}")


def stage0(C, xin, xin_b):
    P, S, SB = C.P, C.S, C.SB
    with Scope(P):
        ts = TStore(C, S.xT, SB["xT"], "s0")
        x32 = [P.sb(f"x32_{i}", [128, D], F32) for i in range(2)]
        xbf = [P.sb(f"xbf_{i}", [128, D], BF16) for i in range(2)]
        for tb in range(C.NB):
            a, b = x32[tb % 2], xbf[tb % 2]
            P.dma("sp", a[:], xin[tb * 128:(tb + 1) * 128, :], reads=[xin_b], writes=[a], sem=f"ld_x{tb % 2}")
            P.cp("pool", b[:, 0:1024], a[:, 0:1024], [a], [b])
            P.cp("act", b[:, 1024:2048], a[:, 1024:2048], [a], [b])
            ts.push(b, tb)


def stageA(C, w):
    P, S, SB, T = C.P, C.S, C.SB, C.T
    TT = min(T, 2048)
    fm_dst = {"fq": (S.fqT, "fqT"), "fk": (S.fkT, "fkT"), "su": (S.suT, "suT"), "g3": (S.g3T, "g3T"), "mg": (S.mgT, "mgT")}
    tm_dst = {"fv": (S.fv, "fv"), "sv": (S.sv, "sv"), "gg": (S.gg, "gg")}
    with Scope(P):
        xs = P.sb("xs", [128, 16, TT], BF16)
        wt = [P.sb(f"wt{i}", [128, 16, 512], BF16) for i in range(2)]
        wsm = P.sb("wsm", [128, 16, 24], BF16)
        bfm = P.sb("bfm", [128, NFM], F32)
        btm = P.sb("btm", [128, 3072 + 24], F32)
        ofm = [P.sb(f"ofm{i}", [128, TT], BF16) for i in range(2)]
        otm = [P.sb(f"otm{i}", [128, 512], BF16) for i in range(2)]
        osm = [P.sb(f"osm{i}", [128, 24], F32) for i in range(2)]
        P.dma("sp", bfm[:], w.pc[:, PC["bfm"][0]:PC["bfm"][0] + NFM], writes=[bfm], sem="ld_s0")
        P.dma("sp", btm[:], w.bc[:, 0:3096], writes=[btm], sem="ld_s1")
        wi = 0
        pi = 0
        for ts_ in range(T // TT):
            t0 = ts_ * TT
            P.dma("sp", xs[:], S.xT.rearrange("(kc p) t -> p kc t", p=128)[:, :, t0:t0 + TT],
                  reads=[SB["xT"]], writes=[xs], sem="ld_xs")
            ci = 0
            oi = 0
            for name, c0, width in FM_RANGES:
                dst, dname = fm_dst[name]
                func = AF.Sigmoid if name == "mg" else AF.Identity
                for g in range(width // 512):
                    wb = wt[wi % 2]
                    P.dma("pool", wb[:], w.w_in.rearrange("(kc p) n -> p kc n", p=128)[:, :, c0 + g * 512:c0 + (g + 1) * 512],
                          writes=[wb], sem=f"ld_w{wi % 2}")
                    wi += 1
                    for c in range(4):
                        ob = ofm[oi % 2]
                        oi += 1
                        for tt in range(TT // 512):
                            ps = C.ps[pi % 4]
                            pi += 1
                            for kc in range(16):
                                P.mm(ps[:], wb[:, kc, c * 128:(c + 1) * 128], xs[:, kc, tt * 512:(tt + 1) * 512],
                                     kc == 0, kc == 15, [wb, xs], [ps])
                            P.act(ob[:, tt * 512:(tt + 1) * 512], ps[:], func, [ps, bfm], [ob],
                                  bias=bfm[:, ci:ci + 1])
                        r0 = g * 512 + c * 128
                        P.dma("sp", dst[r0:r0 + 128, t0:t0 + TT], ob[:], reads=[ob], writes=[SB[dname]],
                              sem=f"st_ofm{(oi - 1) % 2}")
                        ci += 1
            bo = 0
            oi = 0
            for name, c0, width in TM_RANGES:
                dst, dname = tm_dst[name]
                for g in range(width // 512):
                    wb = wt[wi % 2]
                    P.dma("pool", wb[:], w.w_in.rearrange("(kc p) n -> p kc n", p=128)[:, :, c0 + g * 512:c0 + (g + 1) * 512],
                          writes=[wb], sem=f"ld_w{wi % 2}")
                    wi += 1
                    for tb in range(TT // 128):
                        ps = C.ps[pi % 4]
                        pi += 1
                        for kc in range(16):
                            P.mm(ps[:], xs[:, kc, tb * 128:(tb + 1) * 128], wb[:, kc, :], kc == 0, kc == 15, [wb, xs], [ps])
                        ob = otm[oi % 2]
                        oi += 1
                        P.tt("dve", ob[:], ps[:], btm[:, bo:bo + 512], ALU.add, [ps, btm], [ob])
                        P.dma("sp", dst[t0 + tb * 128:t0 + (tb + 1) * 128, g * 512:(g + 1) * 512], ob[:], reads=[ob],
                              writes=[SB[dname]], sem=f"st_otm{(oi - 1) % 2}")
                    bo += 512
            if ts_ == 0:
                for i, c0 in enumerate((3072, 8200, 8208)):
                    P.dma("pool", wsm[:, :, i * 8:(i + 1) * 8], w.w_in.rearrange("(kc p) n -> p kc n", p=128)[:, :, c0:c0 + 8],
                          writes=[wsm], sem="ld_wsm")
            oi = 0
            for tb in range(TT // 128):
                ps = C.ps[pi % 4]
                pi += 1
                for kc in range(16):
                    P.mm(ps[:, 0:24], xs[:, kc, tb * 128:(tb + 1) * 128], wsm[:, kc, :], kc == 0, kc == 15, [wsm, xs], [ps])
                ob = osm[oi % 2]
                oi += 1
                P.tt("dve", ob[:], ps[:, 0:24], btm[:, 3072:3096], ALU.add, [ps, btm], [ob])
                P.dma("sp", S.sm[t0 + tb * 128:t0 + (tb + 1) * 128, :], ob[:], reads=[ob], writes=[SB["sm"]],
                      sem=f"st_osm{(oi - 1) % 2}")


def stageB(C, w):
    P, S, SB, T, NB = C.P, C.S, C.SB, C.T, C.NB
    scale = 128 ** -0.5
    with Scope(P):
        smf = P.sb("smf", [128, NB, 8], F32)
        Lp = P.sb("Lp", [128, NB, 8], F32)
        Lc = P.sb("Lc", [128, NB, 8], F32)
        tot = P.sb("tot", [128, NB, 8], F32)
        carry = P.sb("carry", [128, NB + 1, 8], F32)
        BT = [P.sb(f"BT{i}", [128, NB, NB], F32) for i in range(2)]
        kT = [P.sb(f"kT{i}", [128, T], BF16) for i in range(2)]
        qT = [P.sb(f"qT{i}", [128, T], BF16) for i in range(2)]
        vv = [P.sb(f"vv{i}", [128, NB, 128], BF16) for i in range(2)]
        pt = [P.sb(f"pt{i}", [128, 512], BF16) for i in range(3)]
        rl = P.sb("rl", [128, 512], F32)
        yat = [P.sb(f"yat{i}", [128, 512], BF16) for i in range(2)]
        P.dma("sp", smf[:], S.sm.rearrange("(b p) c -> p b c", p=128)[:, :, 0:8], reads=[SB["sm"]], writes=[smf], sem="ld_s0")
        P.act(Lp[:], smf[:], AF.Exp, [smf], [Lp], scale=-1.0)
        P.act(Lp[:], Lp[:], AF.Ln, [Lp], [Lp], bias=1.0)
        Lp2 = Lp[:].rearrange("p b h -> p (b h)")
        psW, psT = C.ps[4], C.ps[5]
        P.mm(psW[:, 0:NB * 8], C.k32("triU"), Lp2, True, True, [C.c32, Lp], [psW])
        P.mm(psT[:, 0:NB * 8], C.k32("ones"), Lp2, True, True, [C.c32, Lp], [psT])
        P.cp("dve", tot[:].rearrange("p b h -> p (b h)"), psT[:, 0:NB * 8], [psT], [tot])
        P.op("dve", lambda e: e.memset(carry[:, 0, :], 0.0), writes=[carry])
        for b in range(NB):
            P.tt("dve", carry[:, b + 1, :], carry[:, b, :], tot[:, b, :], ALU.add, [carry, tot], [carry])
        P.tt("dve", Lc[:].rearrange("p b h -> p (b h)"), psW[:, 0:NB * 8],
             carry[:, 0:NB, :].rearrange("p b h -> p (b h)"), ALU.add, [psW, carry], [Lc])
        iters = [(h, qt, i, 4 * qt + 4) for h in range(8) for qt in range(T // 512) for i in range(4 * qt + 4)]
        yi = [0]

        def qk_exp(n):
            h, qt, i, nk = iters[n]
            k_, q_, v_, bt = kT[h % 2], qT[h % 2], vv[h % 2], BT[h % 2]
            if qt == 0 and i == 0:
                P.dma("sp", k_[:], S.fkT[h * 128:(h + 1) * 128, :], reads=[SB["fkT"]], writes=[k_], sem=f"ld_k{h % 2}")
                P.dma("sp", q_[:], S.fqT[h * 128:(h + 1) * 128, :], reads=[SB["fqT"]], writes=[q_], sem=f"ld_q{h % 2}")
                P.dma("sp", v_[:], S.fv.rearrange("(b p) d -> p b d", p=128)[:, :, h * 128:(h + 1) * 128],
                      reads=[SB["fv"]], writes=[v_], sem=f"ld_v{h % 2}")
                P.tt("dve", bt[:], Lc[:, :, h].unsqueeze(2).to_broadcast([128, NB, NB]),
                     carry[:, 1:NB + 1, h].unsqueeze(1).to_broadcast([128, NB, NB]), ALU.subtract, [Lc, carry], [bt])
            jmin = max(0, i - 4 * qt)
            c0 = jmin * 128
            pss = C.ps[n % 2]
            P.mm(pss[:, c0:512], k_[:, i * 128:(i + 1) * 128], q_[:, qt * 512 + c0:(qt + 1) * 512], True, True,
                 [k_, q_], [pss])
            p_ = pt[n % 3]
            for j in range(jmin, 4):
                P.act(p_[:, j * 128:(j + 1) * 128], pss[:, j * 128:(j + 1) * 128], AF.Exp, [pss, bt], [p_],
                      bias=bt[:, i, 4 * qt + j:4 * qt + j + 1], scale=scale)
            if i >= 4 * qt:
                P.tt("pool", p_[:, c0:c0 + 128], p_[:, c0:c0 + 128], C.kbf("triKQ"), ALU.mult, [p_, C.cbf], [p_])

        def pv_l(n):
            h, qt, i, nk = iters[n]
            v_ = vv[h % 2]
            p_ = pt[n % 3]
            pso = C.ps[4 + (qt % 2)]
            psl = C.ps[2 + (qt % 2)]
            c0 = max(0, i - 4 * qt) * 128
            P.mm(pso[:, c0:512], v_[:, i, :], p_[:, c0:512], i == 0, i == nk - 1, [v_, p_], [pso])
            P.mm(psl[:, c0:512], C.kbf("ones"), p_[:, c0:512], i == 0, i == nk - 1, [C.cbf, p_], [psl])
            if i == nk - 1:
                P.op("dve", lambda e: e.reciprocal(out=rl[:], in_=psl[:]), reads=[psl], writes=[rl])
                y_ = yat[yi[0] % 2]
                sl = yi[0] % 2
                yi[0] += 1
                P.tt("dve", y_[:], pso[:], rl[:], ALU.mult, [pso, rl], [y_])
                P.dma("sp", S.yaT[h * 128:(h + 1) * 128, qt * 512:(qt + 1) * 512], y_[:], reads=[y_], writes=[SB["yaT"]],
                      sem=f"st_ya{sl}")

        for n in range(len(iters) + 1):
            if n < len(iters):
                qk_exp(n)
            if n >= 1:
                pv_l(n - 1)


def stageC(C, w):
    P, S, SB, T, NB = C.P, C.S, C.SB, C.T, C.NB
    with Scope(P):
        wT32 = P.sb("wT32", [128, 8, 128], F32)
        wTm = P.sb("wTm", [128, 8, 128], BF16)
        lg = P.sb("lg", [128, 1024], F32)
        lb = P.sb("lb", [128, 1024], F32)
        bsb = P.sb("bsb", [128, 8, 128], F32)
        sv = [P.sb(f"sv{i}", [128, 1024], BF16) for i in range(2)]
        su = [P.sb(f"su{i}", [128, 8, 512], BF16) for i in range(2)]
        yb = [P.sb(f"yb{i}", [128, 8, 512], BF16) for i in range(2)]
        sq = P.sb("sq", [128, 1024], F32)
        vn = P.sb("vn", [128, 1024], F32)
        vg = [P.sb(f"vg{i}", [128, 1024], BF16) for i in range(2)]
        tmp = P.sb("tmp", [128, 8, 128], F32)
        st = P.sb("st", [128, 4, 8], F32)
        P.dma("sp", wT32[:], w.sguwT, writes=[wT32], sem="ld_s0")
        P.dma("sp", lg[:], w.bc[:, BC["slg"][0]:BC["slg"][0] + 1024], writes=[lg], sem="ld_s1")
        P.dma("sp", lb[:], w.bc[:, BC["slb"][0]:BC["slb"][0] + 1024], writes=[lb], sem="ld_s2")
        P.dma("sp", bsb[:].rearrange("p g t -> p (g t)"), w.bc[:, BC["sgub"][0]:BC["sgub"][0] + 1024], writes=[bsb], sem="ld_s3")
        P.tt("dve", wTm[:], wT32[:], C.k32("sguT").unsqueeze(1).to_broadcast([128, 8, 128]), ALU.mult, [wT32, C.c32], [wTm])
        for g4 in range(NB // 4):
            u_, y_ = su[g4 % 2], yb[g4 % 2]
            P.dma("sp", u_[:], S.suT.rearrange("(g c) t -> c g t", c=128)[:, :, g4 * 512:(g4 + 1) * 512],
                  reads=[SB["suT"]], writes=[u_], sem=f"ld_su{g4 % 2}")
            for bi in range(4):
                sp_ = g4 * 4 + bi
                v_ = sv[sp_ % 2]
                P.dma("sp", v_[:], S.sv[sp_ * 128:(sp_ + 1) * 128, :], reads=[SB["sv"]], writes=[v_], sem=f"ld_sv{sp_ % 2}")
                v3 = v_[:].rearrange("p (g c) -> p g c", g=8)
                P.op("dve", lambda e: e.tensor_reduce(out=st[:, 0, :], in_=v3, axis=AX.X, op=ALU.add), reads=[v_], writes=[st])
                P.act(sq[:], v_[:], AF.Square, [v_], [sq])
                P.op("dve", lambda e: e.tensor_reduce(out=st[:, 1, :], in_=sq[:].rearrange("p (g c) -> p g c", g=8),
                                                      axis=AX.X, op=ALU.add), reads=[sq], writes=[st])
                P.ts("dve", st[:, 2, :], st[:, 0, :], 1.0 / 128, None, ALU.mult, None, [st], [st])
                P.tt("dve", st[:, 0, :], st[:, 2, :], st[:, 2, :], ALU.mult, [st], [st])
                P.stt("dve", st[:, 1, :], st[:, 1, :], 1.0 / 128, st[:, 0, :], ALU.mult, ALU.subtract, [st], [st])
                P.act(st[:, 3, :], st[:, 1, :], AF.Ln, [st], [st], bias=LN_EPS)
                P.act(st[:, 3, :], st[:, 3, :], AF.Exp, [st], [st], scale=-0.5)
                vn3 = vn[:].rearrange("p (g c) -> p g c", g=8)
                P.tt("dve", vn3, v3, st[:, 2, :].unsqueeze(2).to_broadcast([128, 8, 128]), ALU.subtract, [v_, st], [vn])
                P.tt("pool", vn3, vn3, st[:, 3, :].unsqueeze(2).to_broadcast([128, 8, 128]), ALU.mult, [vn, st], [vn])
                P.tt("pool", vn[:], vn[:], lg[:], ALU.mult, [vn, lg], [vn])
                vg_ = vg[sp_ % 2]
                P.tt("dve", vg_[:], vn[:], lb[:], ALU.add, [vn, lb], [vg_])
                for half in range(2):
                    ps = C.ps[(sp_ * 2 + half) % 4]
                    for gg in range(4):
                        g = half * 4 + gg
                        P.mm(ps[:, gg * 128:(gg + 1) * 128], vg_[:, g * 128:(g + 1) * 128], wTm[:, g, :], True, True,
                             [vg_, wTm], [ps], signal=(gg == 3))
                    t3 = tmp[:, half * 4:(half + 1) * 4, :]
                    P.tt("dve", t3, ps[:].rearrange("p (g t) -> p g t", g=4), bsb[:, half * 4:(half + 1) * 4, :], ALU.add,
                         [ps, bsb], [tmp])
                    P.tt("pool", y_[:, half * 4:(half + 1) * 4, bi * 128:(bi + 1) * 128], t3,
                         u_[:, half * 4:(half + 1) * 4, bi * 128:(bi + 1) * 128], ALU.mult, [tmp, u_], [y_])
            P.dma("sp", S.ybT.rearrange("(g c) t -> c g t", c=128)[:, :, g4 * 512:(g4 + 1) * 512], y_[:], reads=[y_],
                  writes=[SB["ybT"]], sem=f"st_yb{g4 % 2}")


def stageE1(C, w):
    P, S, SB, T = C.P, C.S, C.SB, C.T
    with Scope(P):
        wp = [P.sb(f"wp{i}", [128, 8, D], BF16) for i in range(3)]
        ys = [[P.sb(f"ys{i}_{j}", [128, 8, 512], BF16) for j in range(3)] for i in range(2)]
        gt = [P.sb(f"gt{i}", [128, 3, 512], BF16) for i in range(2)]
        acc = P.sb("acc", [128, 512], F32)
        t1 = P.sb("t1", [128, 512], F32)
        t2 = P.sb("t2", [128, 512], F32)
        mt = [P.sb(f"mt{i}", [128, 512], BF16) for i in range(2)]
        for i, wsrc in enumerate((w.wpa, w.wpb, w.wpc)):
            for hh in range(2):
                P.dma("pool", wp[i][:, :, hh * 1024:(hh + 1) * 1024],
                      wsrc.rearrange("(kc p) n -> p kc n", p=128)[:, :, hh * 1024:(hh + 1) * 1024], writes=[wp[i]], sem=f"ld_wp{i}")
        srcs = [(S.yaT, "yaT"), (S.ybT, "ybT"), (S.ycT, "ycT")]
        gi = 0
        for tt in range(T // 512):
            yt = ys[tt % 2]
            for i, (src, nm) in enumerate(srcs):
                P.dma("sp", yt[i][:], src.rearrange("(kc p) t -> p kc t", p=128)[:, :, tt * 512:(tt + 1) * 512],
                      reads=[SB[nm]], writes=[yt[i]], sem=f"ld_ys{tt % 2}_{i}")
            for fc in range(16):
                g_ = gt[gi % 2]
                m_ = mt[gi % 2]
                P.dma("sp", g_[:], S.mgT.rearrange("(b fc p) t -> p b fc t", b=3, p=128)[:, :, fc, tt * 512:(tt + 1) * 512],
                      reads=[SB["mgT"]], writes=[g_], sem=f"ld_gt{gi % 2}")
                pss = [C.ps[(gi % 2) * 3 + i] for i in range(3)]
                gi += 1
                for i in range(3):
                    for kc in range(8):
                        P.mm(pss[i][:], wp[i][:, kc, fc * 128:(fc + 1) * 128], yt[i][:, kc, :], kc == 0, kc == 7,
                             [wp[i], yt[i]], [pss[i]])
                P.tt("dve", acc[:], pss[0][:], g_[:, 0, :], ALU.mult, [pss[0], g_], [acc])
                P.tt("dve", t1[:], pss[1][:], g_[:, 1, :], ALU.mult, [pss[1], g_], [t1])
                P.tt("dve", t2[:], pss[2][:], g_[:, 2, :], ALU.mult, [pss[2], g_], [t2])
                P.tt("pool", acc[:], acc[:], t1[:], ALU.add, [acc, t1], [acc])
                P.tt("pool", m_[:], acc[:], t2[:], ALU.add, [acc, t2], [m_])
                P.dma("sp", S.mT[fc * 128:(fc + 1) * 128, tt * 512:(tt + 1) * 512], m_[:], reads=[m_], writes=[SB["mT"]],
                      sem=f"st_mt{(gi - 1) % 2}")


def layer_norm_tile(C, y, st6, mv, sc, gt_, bt_):
    P = C.P
    for q in range(4):
        P.op("dve", lambda e, q=q: e.bn_stats(out=st6[:, q, :], in_=y[:, q * 512:(q + 1) * 512]), reads=[y], writes=[st6])
    P.op("dve", lambda e: e.bn_aggr(out=mv[:], in_=st6[:]), reads=[st6], writes=[mv])
    P.act(sc[:, 0:1], mv[:, 1:2], AF.Ln, [mv], [sc], bias=LN_EPS)
    P.act(sc[:, 0:1], sc[:, 0:1], AF.Exp, [sc], [sc], scale=-0.5)
    P.stt("dve", sc[:, 1:2], mv[:, 0:1], -1.0, sc[:, 0:1], ALU.mult, ALU.mult, [mv, sc], [sc])
    P.act(y[:], y[:], AF.Identity, [y, sc], [y], bias=sc[:, 1:2], scale=sc[:, 0:1])
    P.tt("pool", y[:], y[:], gt_[:], ALU.mult, [y, gt_], [y])
    P.tt("dve", y[:], y[:], bt_[:], ALU.add, [y, bt_], [y])


def stageE2(C, w, xin, xin_b):
    P, S, SB, T, NB = C.P, C.S, C.SB, C.T, C.NB
    with Scope(P):
        wo = P.sb("wo", [128, 16, D], BF16)
        mt = [P.sb(f"mtl{i}", [128, 16, 512], BF16) for i in range(2)]
        yt = [P.sb(f"yt{i}", [128, D], F32) for i in range(2)]
        xb = [P.sb(f"xb{i}", [128, D], BF16) for i in range(2)]
        lg = P.sb("lg", [128, D], F32)
        lb = P.sb("lb", [128, D], F32)
        st6 = P.sb("st6", [128, 4, 6], F32)
        mv = P.sb("mv", [128, 2], F32)
        sc = P.sb("sc", [128, 2], F32)
        ts = TStore(C, S.x1T, SB["x1T"], "e2")
        for q in range(4):
            P.dma("pool", wo[:, :, q * 512:(q + 1) * 512], w.wout.rearrange("(kc p) n -> p kc n", p=128)[:, :, q * 512:(q + 1) * 512],
                  writes=[wo], sem="ld_wo")
        P.dma("sp", lg[:], w.bc[:, BC["ln1g"][0]:BC["ln1g"][0] + D], writes=[lg], sem="ld_s0")
        P.dma("sp", lb[:], w.bc[:, BC["ln1b"][0]:BC["ln1b"][0] + D], writes=[lb], sem="ld_s1")
        pend = None
        for tb in range(NB):
            g, bi = tb // 4, tb % 4
            m_ = mt[g % 2]
            if bi == 0:
                P.dma("sp", m_[:], S.mT.rearrange("(kc p) t -> p kc t", p=128)[:, :, g * 512:(g + 1) * 512],
                      reads=[SB["mT"]], writes=[m_], sem=f"ld_mt{g % 2}")
            y_ = yt[tb % 2]
            P.dma("sp", y_[:], xin[tb * 128:(tb + 1) * 128, :], reads=[xin_b], writes=[y_], sem=f"ld_y{tb % 2}")
            for cg in range(4):
                ps = C.ps[cg]
                for kc in range(16):
                    P.mm(ps[:], m_[:, kc, bi * 128:(bi + 1) * 128], wo[:, kc, cg * 512:(cg + 1) * 512], kc == 0, kc == 15,
                         [m_, wo], [ps])
                P.stt("dve", y_[:, cg * 512:(cg + 1) * 512], y_[:, cg * 512:(cg + 1) * 512], ALPHA, ps[:], ALU.mult, ALU.add,
                      [y_, ps], [y_])
            if pend is not None:
                ts.push(*pend)
            layer_norm_tile(C, y_, st6, mv, sc, lg, lb)
            P.dma("pool", S.xr1[tb * 128:(tb + 1) * 128, :], y_[:], reads=[y_], writes=[SB["xr1"]], sem=f"st_y{tb % 2}")
            b_ = xb[tb % 2]
            P.cp("act", b_[:], y_[:], [y_], [b_])
            pend = (b_, tb)
        ts.push(*pend)


def stageF(C, w, xout, xout_b):
    P, S, SB, T = C.P, C.S, C.SB, C.T
    NP = DFF // 128
    with Scope(P):
        x1 = P.sb("x1", [128, 16, 512], BF16)
        wu = [P.sb(f"wu{i}", [128, 2, 16, 256], BF16) for i in range(2)]
        aT = P.sb("aT", [128, NP, 512], BF16)
        wd = [P.sb(f"wd{i}", [128, NP, 256], BF16) for i in range(2)]
        yt = P.sb("ytf", [128, 4, D], F32)
        lg = P.sb("lg", [128, D], F32)
        lb = P.sb("lb", [128, D], F32)
        fcw = P.sb("fcw", [128, 86, 3], F32)
        fcb = P.sb("fcb", [128, 86], F32)
        hal = P.sb("hal", [128, 86, 2], F32)
        hb = [P.sb(f"hb{i}", [128, 2, 514], F32) for i in range(1)]
        cv = [P.sb(f"cv{i}", [128, 2, 512], F32) for i in range(1)]
        sg = P.sb("sg", [128, 512], F32)
        st6 = P.sb("st6", [128, 4, 6], F32)
        mv = P.sb("mv", [128, 2], F32)
        sc = P.sb("sc", [128, 2], F32)
        P.dma("sp", lg[:], w.bc[:, BC["ln2g"][0]:BC["ln2g"][0] + D], writes=[lg], sem="ld_s0")
        P.dma("sp", lb[:], w.bc[:, BC["ln2b"][0]:BC["ln2b"][0] + D], writes=[lb], sem="ld_s1")
        P.dma("sp", fcw[:].rearrange("p c k -> p (c k)"), w.pc[:, PC["fcw"][0]:PC["fcw"][0] + 258], writes=[fcw], sem="ld_s2")
        P.dma("sp", fcb[:], w.pc[:, PC["fcb"][0]:PC["fcb"][0] + 86], writes=[fcb], sem="ld_s3")
        P.op("pool", lambda e: e.memset(hal[:], 0.0), writes=[hal])
        wui = 0
        wdi = 0
        pi = 0
        hi = 0
        wupv = w.wup.rearrange("(kc p) (two n) -> p two kc n", p=128, two=2)
        for tt in range(T // 512):
            P.dma("sp", x1[:], S.x1T.rearrange("(kc p) t -> p kc t", p=128)[:, :, tt * 512:(tt + 1) * 512],
                  reads=[SB["x1T"]], writes=[x1], sem="ld_x1")
            for fc in range(NP):
                if fc == NP // 2:
                    for tb in range(4):
                        r0 = tt * 512 + tb * 128
                        P.dma("sp", yt[:, tb, :], S.xr1[r0:r0 + 128, :], reads=[SB["xr1"]], writes=[yt], sem="ld_ytf")
                if fc % 2 == 0:
                    wb = wu[wui % 2]
                    wui += 1
                    if PRECAST:
                        P.dma("sp", wb[:], S.wupb[fc // 2], reads=[SB["wupb"]], writes=[wb], sem=f"ld_wu{(wui - 1) % 2}")
                    else:
                        ncol = min(256, DFF - fc * 128)
                        for two in range(2):
                            P.dma("pool", wb[:, two, :, 0:ncol], wupv[:, two, :, fc * 128:fc * 128 + ncol], writes=[wb],
                                  sem=f"ld_wu{(wui - 1) % 2}")
                co = (fc % 2) * 128
                pg, pv = C.ps[pi % 4], C.ps[(pi + 1) % 4]
                pi += 2
                for two, ps in ((0, pg), (1, pv)):
                    for kc in range(16):
                        P.mm(ps[:], wb[:, two, kc, co:co + 128], x1[:, kc, :], kc == 0, kc == 15, [wb, x1], [ps])
                h_ = hb[0]
                c_ = cv[0]
                hi += 1
                for two, ps in ((0, pg), (1, pv)):
                    ch = two * NP + fc
                    P.cp("pool", h_[:, two, 0:2], hal[:, ch, :], [hal], [h_])
                    P.cp("act", h_[:, two, 2:514], ps[:], [ps], [h_])
                    P.cp("pool", hal[:, ch, :], h_[:, two, 512:514], [h_], [hal])
                    P.ts("dve", c_[:, two, :], h_[:, two, 2:514], fcw[:, ch, 2:3], fcb[:, ch:ch + 1], ALU.mult, ALU.add,
                         [h_, fcw, fcb], [c_])
                    P.stt("dve", c_[:, two, :], h_[:, two, 1:513], fcw[:, ch, 1:2], c_[:, two, :], ALU.mult, ALU.add,
                          [h_, fcw, c_], [c_])
                    P.stt("dve", c_[:, two, :], h_[:, two, 0:512], fcw[:, ch, 0:1], c_[:, two, :], ALU.mult, ALU.add,
                          [h_, fcw, c_], [c_])
                P.act(sg[:], c_[:, 0, :], AF.Silu, [c_], [sg])
                P.tt("pool", aT[:, fc, :], sg[:], c_[:, 1, :], ALU.mult, [sg, c_], [aT])
            for cg in range(D // 256):
                wb = wd[wdi % 2]
                wdi += 1
                if PRECAST:
                    P.dma("sp", wb[:], S.wdnb[cg], reads=[SB["wdnb"]], writes=[wb], sem=f"ld_wd{(wdi - 1) % 2}")
                else:
                    for q, (k0, k1) in enumerate(((0, 22), (22, NP))):
                        P.dma("pool", wb[:, k0:k1, :], w.wdn.rearrange("(kc p) n -> p kc n", p=128)[:, k0:k1, cg * 256:(cg + 1) * 256],
                              writes=[wb], sem=f"ld_wd{(wdi - 1) % 2}")
                for tb in range(4):
                    ps = C.ps[pi % 4]
                    pi += 1
                    for kc in range(NP):
                        P.mm(ps[:, 0:256], aT[:, kc, tb * 128:(tb + 1) * 128], wb[:, kc, :], kc == 0, kc == NP - 1, [aT, wb], [ps])
                    ysl = yt[:, tb, cg * 256:(cg + 1) * 256]
                    P.stt("dve", ysl, ysl, ALPHA, ps[:, 0:256], ALU.mult, ALU.add, [yt, ps], [yt])
            for tb in range(4):
                r0 = tt * 512 + tb * 128
                yv = _YV(yt, tb)
                layer_norm_tile(C, yv, st6, mv, sc, lg, lb)
                P.dma("pool", xout[r0:r0 + 128, :], yt[:, tb, :], reads=[yt], writes=[xout_b], sem="st_yf")


class _YV:
    def __init__(self, t, tb):
        self.t, self.tb, self.b = t, tb, t.b

    def __getitem__(self, k):
        v = self.t.t[:, self.tb, :]
        if isinstance(k, tuple):
            return v[k]
        return v


def precast_ffn(C, w, stg):
    P, S, SB = C.P, C.S, C.SB
    i = 0
    wupv = w.wup.rearrange("(kc p) n -> p kc n", p=128)
    for two in range(2):
        for c0 in range(0, DFF, 512):
            ncol = min(512, DFF - c0)
            st = stg[i % 2]
            sl = i % 2
            i += 1
            P.dma("pool", st[:, :, 0:ncol], wupv[:, :, two * DFF + c0:two * DFF + c0 + ncol], writes=[st], sem=f"ld_pc{sl}")
            yield
            for j in range(0, ncol, 256):
                n2 = min(256, ncol - j)
                pr = (c0 + j) // 256
                P.dma("sp", S.wupb[pr, :, two, :, 0:n2], st[:, :, j:j + n2], reads=[st], writes=[SB["wupb"]], sem=f"st_pc{sl}")
            yield
    wdnv = w.wdn.rearrange("(kc p) n -> p kc n", p=128)
    for c0 in range(0, D, 512):
        for k0 in range(0, 43, 16):
            nk = min(16, 43 - k0)
            st = stg[i % 2]
            sl = i % 2
            i += 1
            P.dma("pool", st[:, 0:nk, :], wdnv[:, k0:k0 + nk, c0:c0 + 512], writes=[st], sem=f"ld_pc{sl}")
            yield
            for j in range(2):
                cg = c0 // 256 + j
                P.dma("sp", S.wdnb[cg, :, k0:k0 + nk, :], st[:, 0:nk, j * 256:(j + 1) * 256], reads=[st], writes=[SB["wdnb"]],
                      sem=f"st_pc{sl}")
            yield


D_HEADS = 8
D_NB = None
D_LEVEL = 5
PRECAST = True


def stageD(C, w, npar=2):
    P, S, SB, T, NB = C.P, C.S, C.SB, C.T, C.NB
    NBH = NB * 8
    with Scope(P):
        sma = P.sb("sma", [128, NB, 8], F32)
        smb = P.sb("smb", [128, NB, 8], F32)
        hb8 = P.sb("hb8", [128, 2, 8], F32)
        nega = P.sb("nega", [128, 8], F32)
        gg_ = P.sb("g", [128, NB, 8], F32)
        gc = P.sb("gc", [128, NB, 8], F32)
        egc = P.sb("egc", [128, NB, 8], F32)
        kdec = P.sb("kdec", [128, NB, 8], F32)
        beta = P.sb("beta", [128, NB, 8], F32)
        nbeta = P.sb("nbeta", [128, NB, 8], F32)
        eglb = P.sb("eglb", [128, 2, NBH], F32)
        ngb = P.sb("ngb", [128, 128], F32)
        gcw = P.sb("gcw", [128, 24, 4], F32)
        f2 = lambda t: t[:].rearrange("p b h -> p (b h)")
        P.dma("sp", sma[:], S.sm.rearrange("(b p) c -> p b c", p=128)[:, :, 8:16], reads=[SB["sm"]], writes=[sma], sem="ld_s0")
        P.dma("sp", smb[:], S.sm.rearrange("(b p) c -> p b c", p=128)[:, :, 16:24], reads=[SB["sm"]], writes=[smb], sem="ld_s1")
        P.dma("sp", hb8[:].rearrange("p a h -> p (a h)"), w.bc[:, BC["alog"][0]:BC["alog"][0] + 16], writes=[hb8], sem="ld_s2")
        P.dma("sp", ngb[:], w.bc[:, BC["gng"][0]:BC["gng"][0] + 128], writes=[ngb], sem="ld_s3")
        P.dma("sp", gcw[:].rearrange("p c k -> p (c k)"), w.pc[:, PC["gcw"][0]:PC["gcw"][0] + 96], writes=[gcw], sem="ld_s4")
        P.act(nega[:], hb8[:, 0, :], AF.Exp, [hb8], [nega])
        P.ts("dve", nega[:], nega[:], -1.0, None, ALU.mult, None, [nega], [nega])
        P.tt("dve", sma[:], sma[:], hb8[:, 1, :].unsqueeze(1).to_broadcast([128, NB, 8]), ALU.add, [sma, hb8], [sma])
        P.act(sma[:], sma[:], AF.Exp, [sma], [sma])
        P.act(sma[:], sma[:], AF.Ln, [sma], [sma], bias=1.0)
        P.tt("dve", gg_[:], sma[:], nega[:].unsqueeze(1).to_broadcast([128, NB, 8]), ALU.mult, [sma, nega], [gg_])
        P.act(beta[:], smb[:], AF.Sigmoid, [smb], [beta])
        P.ts("dve", nbeta[:], beta[:], -1.0, None, ALU.mult, None, [beta], [nbeta])
        p0, p1 = C.ps[0], C.ps[1]
        P.mm(p0[:, 0:NBH], C.k32("triBD"), f2(gg_), True, True, [C.c32, gg_], [p0])
        P.mm(p0[:, NBH:2 * NBH], C.k32("blk"), f2(gg_), True, True, [C.c32, gg_], [p0])
        P.mm(p1[:, 0:NBH], C.k32("half0"), f2(gg_), True, True, [C.c32, gg_], [p1])
        P.mm(p1[:, NBH:2 * NBH], C.k32("half1"), f2(gg_), True, True, [C.c32, gg_], [p1])
        P.cp("dve", f2(gc), p0[:, 0:NBH], [p0], [gc])
        P.act(f2(egc), p0[:, 0:NBH], AF.Exp, [p0], [egc])
        P.tt("dve", f2(kdec), p0[:, NBH:2 * NBH], f2(gc), ALU.subtract, [p0, gc], [kdec])
        P.act(f2(kdec), f2(kdec), AF.Exp, [kdec], [kdec])
        P.act(eglb[:].rearrange("p c n -> p (c n)"), p1[:, 0:2 * NBH], AF.Exp, [p1], [eglb])
        P.barrier()

        chains = []
        for ci in range(npar):
            R = Ctx()
            R.ci = ci
            R.raw = P.sb("raw", [128, T + 3], BF16)
            R.acc = P.sb("acc", [128, T], F32)
            R.cvT = P.sb("cvT", [128, 3, T], BF16)
            R.ggh = P.sb("ggh", [128, NB, 128], BF16)
            R.ycT = P.sb("ycT", [128, T], BF16)
            R.S32 = P.sb("S32", [128, 128], F32)
            R.Sbf = P.sb("Sbf", [128, 128], BF16)
            R.jk = P.sb("jk", [128, 128], F32)
            R.qsq = P.sb("qsq", [128, 128], BF16)
            R.sc = P.sb("sc", [128, 8], F32)
            R.kn = P.sb("kn", [128, 128], BF16)
            R.kb = P.sb("kb", [128, 128], BF16)
            R.kg = P.sb("kg", [128, 128], BF16)
            R.kd = P.sb("kd", [128, 128], BF16)
            R.knT = P.sb("knT", [128, 128], BF16)
            R.kbT = P.sb("kbT", [128, 128], BF16)
            R.vtm = P.sb("vtm", [128, 128], BF16)
            R.gU = P.sb("gU", [128, 128], F32)
            R.E2 = P.sb("E2", [128, 256], F32)
            R.Em = P.sb("Em", [128, 3, 128], F32)
            R.N = [P.sb(f"N{i}", [128, 128], BF16) for i in range(2)]
            R.NT = [P.sb(f"NT{i}", [128, 128], BF16) for i in range(2)]
            R.qkT = P.sb("qkT", [128, 128], BF16)
            R.X = [P.sb(f"X{i}", [128, 128], BF16) for i in range(2)]
            R.ub = P.sb("ub", [128, 128], F32)
            R.wT = P.sb("wT", [128, 128], BF16)
            R.vnew = P.sb("vnew", [128, 128], BF16)
            R.t1 = P.sb("t1", [128, 128], F32)
            R.o = P.sb("o", [128, 128], F32)
            R.y1 = P.sb("y1", [128, 128], BF16)
            bA, bB, bC = C.ps[ci * 3], C.ps[ci * 3 + 1], C.ps[ci * 3 + 2]
            pb = C.pb[ci]
            R.pA, R.pB, R.pC, R.pb = bA, bB, bC, pb
            R.rg = {"kk": bA.b, "ssq": bA.b, "dd": bB.b, "dbl": bA.b, "x": bC.b, "ws": bC.b, "qs": bC.b, "qv": bC.b,
                    "tk": pb.b, "tn": pb.b, "tv": pb.b, "ty": pb.b}
            chains.append(R)

        def chain(R, heads):
            ci = R.ci
            rg = R.rg
            pA, pB, pC, pb = R.pA, R.pB, R.pC, R.pb
            for h in heads:
                P.op("pool", lambda e: e.memset(R.raw[:, 0:3], 0.0), writes=[R.raw])
                P.dma("sp", R.ggh[:], S.gg.rearrange("(b p) e -> p b e", p=128)[:, :, h * 128:(h + 1) * 128],
                      reads=[SB["gg"]], writes=[R.ggh], sem=f"ld_ggh{ci}")
                yield
                for i in range(3):
                    r0 = i * 1024 + h * 128
                    P.dma("sp", R.raw[:, 3:3 + T], S.g3T[r0:r0 + 128, :], reads=[SB["g3T"]], writes=[R.raw], sem=f"ld_raw{ci}")
                    chn = i * 8 + h
                    P.ts("dve", R.acc[:], R.raw[:, 3:3 + T], gcw[:, chn, 3:4], None, ALU.mult, None, [R.raw, gcw], [R.acc])
                    yield
                    for tap in (2, 1, 0):
                        eng = "dve"
                        P.stt(eng, R.acc[:], R.raw[:, tap:tap + T], gcw[:, chn, tap:tap + 1], R.acc[:], ALU.mult, ALU.add,
                              [R.raw, gcw, R.acc], [R.acc])
                        yield
                    P.act(R.cvT[:, i, :], R.acc[:], AF.Silu, [R.acc], [R.cvT])
                    yield
                P.act(R.ggh[:], R.ggh[:], AF.Silu, [R.ggh], [R.ggh])
                P.tt("pool", R.ggh[:], R.ggh[:], ngb[:].unsqueeze(1).to_broadcast([128, NB, 128]), ALU.mult, [R.ggh, ngb], [R.ggh])
                P.op("pool", lambda e: e.memset(R.S32[:], 0.0), writes=[R.S32])
                P.op("pool", lambda e: e.memset(R.Sbf[:], 0.0), writes=[R.Sbf])
                yield
                qcT, kcT, vcT = R.cvT[:, 0, :], R.cvT[:, 1, :], R.cvT[:, 2, :]
                for b in range(NB if D_NB is None else D_NB):
                    if D_LEVEL < 2.1:
                        break
                    cols = slice(b * 128, (b + 1) * 128)
                    bh = b * 8 + h
                    col = lambda t: t[:, b, h:h + 1]
                    P.tr(pb[:, 0:128], kcT[:, cols], C.kbf("ident"), [R.cvT, C.cbf], [rg["tk"]])
                    P.tr(pb[:, 384:512], vcT[:, cols], C.kbf("ident"), [R.cvT, C.cbf], [rg["tv"]])
                    yield
                    P.act(R.jk[:], pb[:, 0:128], AF.Square, [rg["tk"]], [R.jk, R.sc], accum_out=R.sc[:, 0:1])
                    P.act(R.sc[:, 1:2], R.sc[:, 0:1], AF.Ln, [R.sc], [R.sc], bias=RMS_EPS)
                    P.act(R.sc[:, 1:2], R.sc[:, 1:2], AF.Exp, [R.sc], [R.sc], scale=-0.5)
                    yield
                    P.ts("dve", R.kn[:], pb[:, 0:128], R.sc[:, 1:2], None, ALU.mult, None, [rg["tk"], R.sc], [R.kn])
                    P.cp("act", R.vtm[:], pb[:, 384:512], [rg["tv"]], [R.vtm])
                    yield
                    if D_LEVEL < 2.3:
                        continue
                    P.act(R.kb[:], R.kn[:], AF.Identity, [R.kn, beta], [R.kb], scale=col(beta))
                    P.act(R.kg[:], R.kn[:], AF.Identity, [R.kn, egc], [R.kg], scale=col(egc))
                    P.act(R.kd[:], R.kn[:], AF.Identity, [R.kn, kdec], [R.kd], scale=col(kdec))
                    P.ts("dve", R.gU[:], C.k32("triBD"), col(gg_), None, ALU.mult, None, [C.c32, gg_], [R.gU])
                    yield
                    if D_LEVEL < 2.35:
                        continue
                    P.tr(pb[:, 128:256], R.kn[:], C.kbf("ident"), [R.kn, C.cbf], [rg["tn"]])
                    P.tr(pb[:, 256:384], R.kb[:], C.kbf("ident"), [R.kb, C.cbf], [rg["tn"]])
                    yield
                    P.cp("act", R.knT[:], pb[:, 128:256], [rg["tn"]], [R.knT])
                    P.cp("dve", R.kbT[:], pb[:, 256:384], [rg["tn"]], [R.kbT])
                    if D_LEVEL < 2.5:
                        continue
                    P.act(R.qsq[:], qcT[:, cols], AF.Square, [R.cvT], [R.qsq])
                    yield
                    P.mm(pA[:, 384:385], R.qsq[:], C.kbf("ones")[:, 0:1], True, True, [R.qsq, C.cbf], [rg["ssq"]])
                    if D_LEVEL < 2.6:
                        continue
                    P.mm(pB[:, 0:128], R.gU[:], C.k32("lm"), True, True, [R.gU, C.c32], [rg["dd"]], signal=False)
                    P.mm(pB[:, 128:256], C.k32("lm"), R.gU[:], True, True, [R.gU, C.c32], [rg["dd"]])
                    P.mm(pA[:, 0:128], R.knT[:], R.kbT[:], True, True, [R.knT, R.kbT], [rg["kk"]], signal=False)
                    P.mm(pA[:, 128:256], R.kbT[:], R.knT[:], True, True, [R.knT, R.kbT], [rg["kk"]], signal=False)
                    P.mm(pA[:, 256:384], R.knT[:], qcT[:, cols], True, True, [R.knT, R.cvT], [rg["kk"]])
                    yield
                    if D_LEVEL < 2.8:
                        continue
                    P.act(R.sc[:, 2:3], pA[:, 384:385], AF.Ln, [rg["ssq"]], [R.sc], bias=RMS_EPS)
                    P.act(R.sc[:, 2:3], R.sc[:, 2:3], AF.Exp, [R.sc], [R.sc], scale=-0.5)
                    P.act(R.E2[:], pB[:, 0:256], AF.Exp, [rg["dd"]], [R.E2])
                    yield
                    P.tt("pool", R.Em[:, 0, :], R.E2[:, 0:128], C.k32("sm"), ALU.mult, [R.E2, C.c32], [R.Em])
                    P.tt("pool", R.Em[:, 1, :], R.E2[:, 128:256], C.k32("smT"), ALU.mult, [R.E2, C.c32], [R.Em])
                    P.tt("pool", R.Em[:, 2, :], R.E2[:, 128:256], C.k32("inclT"), ALU.mult, [R.E2, C.c32], [R.Em])
                    yield
                    P.stt("dve", R.NT[0][:], pA[:, 0:128], -1.0, R.Em[:, 0, :], ALU.mult, ALU.mult, [rg["kk"], R.Em], [R.NT[0]])
                    P.stt("dve", R.N[0][:], pA[:, 128:256], -1.0, R.Em[:, 1, :], ALU.mult, ALU.mult, [rg["kk"], R.Em], [R.N[0]])
                    P.tt("dve", R.qkT[:], pA[:, 256:384], R.Em[:, 2, :], ALU.mult, [rg["kk"], R.Em], [R.qkT])
                    P.tt("pool", R.X[0][:], R.N[0][:], C.kbf("ident"), ALU.add, [R.N[0], C.cbf], [R.X[0]])
                    yield
                    if D_LEVEL < 3.5:
                        continue
                    cur = 0
                    for s in range(5):
                        Nc, NTc = R.N[cur], R.NT[cur]
                        Nn, NTn = R.N[1 - cur], R.NT[1 - cur]
                        if s < 4:
                            P.mm(pA[:, 0:128], NTc[:], Nc[:], True, True, [Nc, NTc], [rg["dbl"]], signal=False)
                        P.mm(pA[:, 128:256], Nc[:], NTc[:], True, True, [Nc, NTc], [rg["dbl"]])
                        yield
                        if D_LEVEL < 3.58:
                            continue
                        if s < 4:
                            P.cp("act", Nn[:], pA[:, 0:128], [rg["dbl"]], [Nn])
                        P.cp("dve", NTn[:], pA[:, 128:256], [rg["dbl"]], [NTn])
                        yield
                        if D_LEVEL < 3.65:
                            continue
                        Xc, Xn = R.X[s % 2], R.X[(s + 1) % 2]
                        P.mm(pC[:, 0:128], NTn[:], Xc[:], True, True, [NTn, Xc], [rg["x"]])
                        yield
                        P.tt("dve", Xn[:], pC[:, 0:128], Xc[:], ALU.add, [rg["x"], Xc], [Xn])
                        yield
                        cur = 1 - cur
                    if D_LEVEL < 3.8:
                        continue
                    Xf = R.X[5 % 2]
                    P.mm(pC[:, 0:128], Xf[:], R.vtm[:], True, True, [Xf, R.vtm], [rg["x"]])
                    yield
                    P.ts("dve", R.ub[:], pC[:, 0:128], col(beta), None, ALU.mult, None, [rg["x"], beta], [R.ub])
                    yield
                    P.mm(pC[:, 0:128], R.kg[:], Xf[:], True, True, [R.kg, Xf], [rg["x"]])
                    yield
                    P.cp("act", R.wT[:], pC[:, 0:128], [rg["x"]], [R.wT])
                    yield
                    if D_LEVEL < 5:
                        continue
                    for c in range(2):
                        r = slice(c * 64, (c + 1) * 64)
                        P.mm(pC[:, 128:256], R.wT[:], R.Sbf[:], True, True, [R.wT, R.Sbf], [rg["ws"]])
                        P.mm(pC[:, 256:384], qcT[:, cols], R.Sbf[:], True, True, [R.cvT, R.Sbf], [rg["qs"]])
                        yield
                        P.stt("dve", R.vnew[r, :], pC[r, 128:256], nbeta[r, b, h:h + 1], R.ub[r, :], ALU.mult, ALU.add,
                              [rg["ws"], nbeta, R.ub], [R.vnew])
                        P.act(R.t1[r, :], pC[r, 256:384], AF.Identity, [rg["qs"], egc], [R.t1], scale=egc[r, b, h:h + 1])
                        yield
                        P.mm(pC[:, 384:512], R.qkT[r, :], R.vnew[r, :], True, True, [R.qkT, R.vnew], [rg["qv"]])
                        yield
                        P.tt("dve", R.o[r, :], pC[r, 384:512], R.t1[r, :], ALU.add, [rg["qv"], R.t1], [R.o])
                        yield
                        P.mm(pC[:, 384:512], R.kd[r, :], R.vnew[r, :], True, True, [R.kd, R.vnew], [rg["qv"]])
                        yield
                        P.stt("dve", R.S32[:], R.S32[:], eglb[:, c, bh:bh + 1], pC[:, 384:512], ALU.mult, ALU.add,
                              [R.S32, eglb, rg["qv"]], [R.S32])
                        P.cp("act", R.Sbf[:], R.S32[:], [R.S32], [R.Sbf])
                        yield
                    P.ts("dve", R.sc[:, 3:4], R.sc[:, 2:3], 128 ** -0.5, None, ALU.mult, None, [R.sc], [R.sc])
                    P.act(R.jk[:], R.o[:], AF.Square, [R.o, R.sc], [R.jk, R.sc], scale=R.sc[:, 3:4], accum_out=R.sc[:, 4:5])
                    P.act(R.sc[:, 5:6], R.sc[:, 4:5], AF.Ln, [R.sc], [R.sc], bias=RMS_EPS, scale=1.0 / 128)
                    P.act(R.sc[:, 5:6], R.sc[:, 5:6], AF.Exp, [R.sc], [R.sc], scale=-0.5)
                    P.tt("dve", R.sc[:, 6:7], R.sc[:, 5:6], R.sc[:, 3:4], ALU.mult, [R.sc], [R.sc])
                    yield
                    P.stt("dve", R.y1[:], R.o[:], R.sc[:, 6:7], R.ggh[:, b, :], ALU.mult, ALU.mult, [R.o, R.sc, R.ggh], [R.y1])
                    yield
                    P.tr(pb[:, 0:128], R.y1[:], C.kbf("ident"), [R.y1, C.cbf], [rg["ty"]])
                    yield
                    P.cp("act", R.ycT[:, cols], pb[:, 0:128], [rg["ty"]], [R.ycT])
                    yield
                P.dma("sp", S.ycT[h * 128:(h + 1) * 128, :], R.ycT[:], reads=[R.ycT], writes=[SB["ycT"]], sem=f"st_yc{ci}")
                yield

        gens = [chain(chains[ci], list(range(ci, D_HEADS, npar))) for ci in range(npar)] if D_LEVEL >= 2 else []
        if PRECAST:
            stg = [P.sb(f"pcs{i}", [128, 16, 512], BF16) for i in range(2)]
            gens.append(precast_ffn(C, w, stg))
        alive = list(gens)
        while alive:
            for g in list(alive):
                try:
                    next(g)
                except StopIteration:
                    alive.remove(g)


_CACHE = {}
FUSED = True


def _prog(nl):
    if nl not in _CACHE:
        _CACHE[nl] = build(nl, SEQ)[0]
    return _CACHE[nl]


def _layer_map(inp, l, slot):
    bc, pc, sguwT = layer_small(inp, l)
    f = lambda a: np.ascontiguousarray(a, dtype=np.float32)
    return {f"w_in{slot}": f(inp["w_in"][l]), f"bc{slot}": bc, f"pc{slot}": pc, f"sguwT{slot}": sguwT,
            f"wpa{slot}": f(inp["w_proj_a"][l]), f"wpb{slot}": f(inp["w_proj_b"][l]), f"wpc{slot}": f(inp["w_proj_c"][l]),
            f"wout{slot}": f(inp["w_out"][l]), f"wup{slot}": f(inp["ffn_w_up"][l]), f"wdn{slot}": f(inp["ffn_w_down"][l])}


def kernel(**inp):
    inp = {k: np.asarray(v) for k, v in inp.items()}
    x = inp["x"].astype(np.float32, copy=False)
    B = x.shape[0]
    consts = make_consts()
    cur = [np.ascontiguousarray(x[b]) for b in range(B)]
    if FUSED:
        nc = _prog(DEPTH)
        wm = {}
        for l in range(DEPTH):
            wm.update(_layer_map(inp, l, l))
        in_maps = [dict(wm, x=cur[b], consts=consts) for b in range(B)]
        res = run_bass_kernel_spmd(nc, in_maps, core_ids=list(range(B)))
        cur = [np.asarray(res.results[b]["y"]) for b in range(B)]
    else:
        nc = _prog(1)
        for l in range(DEPTH):
            wm = _layer_map(inp, l, 0)
            in_maps = [dict(wm, x=cur[b], consts=consts) for b in range(B)]
            res = run_bass_kernel_spmd(nc, in_maps, core_ids=list(range(B)))
            cur = [np.asarray(res.results[b]["y"]) for b in range(B)]
    return np.stack(cur).astype(np.float32)
```

```python
import numpy as np
from contextlib import ExitStack
import concourse.bass as bass
import concourse.mybir as mybir
from concourse.bass_utils import run_bass_kernel_spmd

F32 = mybir.dt.float32
BF16 = mybir.dt.bfloat16
AF = mybir.ActivationFunctionType
ALU = mybir.AluOpType
AX = mybir.AxisListType

D = 2048
NIN = 15384
DFF = 5504
SEQ = 4096
DEPTH = 4
ALPHA = (2 * DEPTH) ** 0.25
LN_EPS = 1e-5
RMS_EPS = 1e-6
OFF = dict(fq=0, fk=1024, fv=2048, ff=3072, su=3080, sv=4104, gq=5128, gk=6152, gv=7176,
           ga=8200, gb=8208, gg=8216, mg=9240)


class Buf:
    __slots__ = ("name", "w", "r", "excl")

    def __init__(self, name="b"):
        self.name = name
        self.w = {}
        self.r = {}
        self.excl = False


class Tl:
    def __init__(self, t, name):
        self.t = t
        self.b = Buf(name)

    def __getitem__(self, k):
        return self.t[k]


class Prog:
    def __init__(self, nc, es):
        self.nc = nc
        self.es = es
        self.ges = es
        self.engs = {"pe": nc.tensor, "act": nc.scalar, "dve": nc.vector,
                     "pool": nc.gpsimd, "sp": nc.sync}
        self.sems = {}
        self.cnt = {}
        self.waited = {e: {} for e in self.engs}
        self.nwait = 0
        self.nops = 0
        self.uid = 0
        self.ekey = {e: e for e in self.engs}

    def sem(self, key):
        if key not in self.sems:
            self.sems[key] = self.ges.enter_context(self.nc.semaphore("s_" + key))
            self.cnt[key] = 0
        return self.sems[key]

    def _wait(self, eng, deps, own=None):
        for k, c in deps.items():
            if c <= 0:
                continue
            if k == own and eng == "pe":
                continue
            if self.waited[eng].get(k, 0) >= c:
                continue
            assert c <= self.cnt[k], f"forward wait {eng} on {k}: {c} > {self.cnt[k]}"
            self.engs[eng].wait_ge(self.sem(k), c)
            self.waited[eng][k] = c
            self.nwait += 1

    @staticmethod
    def _collect(reads, writes):
        deps = {}
        for t in reads:
            for k, c in t.w.items():
                if deps.get(k, 0) < c:
                    deps[k] = c
        for t in writes:
            for k, c in t.w.items():
                if deps.get(k, 0) < c:
                    deps[k] = c
            for k, c in t.r.items():
                if deps.get(k, 0) < c:
                    deps[k] = c
        return deps

    @staticmethod
    def _bufs(xs):
        return [x.b if hasattr(x, "b") else x for x in xs]

    def op(self, eng, fn, reads=(), writes=(), signal=True):
        reads = self._bufs(reads)
        writes = self._bufs(writes)
        writes = writes + [t for t in reads if t.excl and t not in writes]
        key = self.ekey[eng]
        self.sem(key)
        assert signal or eng == "pe"
        self._wait(eng, self._collect(reads, writes), own=key)
        ins = fn(self.engs[eng])
        self.nops += 1
        if signal:
            self.cnt[key] += 1
            ins.then_inc(self.sems[key], 1)
            o = self.cnt[key]
        else:
            o = self.cnt[key] + 1
        for t in reads:
            t.r[key] = o
        for t in writes:
            t.w[key] = o
        return ins

    def epoch(self, tag):
        self.barrier()
        self.ekey = {e: f"{e}_{tag}" for e in self.engs}

    def dma(self, q, out, in_, reads=(), writes=(), sem=None, **kw):
        reads = self._bufs(reads)
        writes = self._bufs(writes)
        self.sem(sem)
        deps = self._collect(reads, writes)
        deps[sem] = max(deps.get(sem, 0), self.cnt[sem])
        self._wait(q, deps)
        ins = self.engs[q].dma_start(out=out, in_=in_, **kw)
        ins.then_inc(self.sems[sem], 16)
        self.nops += 1
        self.cnt[sem] += 16
        o = self.cnt[sem]
        for t in reads:
            t.r[sem] = o
        for t in writes:
            t.w[sem] = o
        return ins

    def barrier(self):
        for e in self.engs:
            self._wait(e, dict(self.cnt), own=self.ekey[e])

    def sb(self, name, shape, dt):
        self.uid += 1
        nm = f"{name}_{self.uid}"
        return Tl(self.es.enter_context(self.nc.sbuf_tensor(nm, list(shape), dt)), nm)

    def mm(self, out, lhsT, rhs, start, stop, reads, writes, signal=None):
        return self.op("pe", lambda e: e.matmul(out, lhsT=lhsT, rhs=rhs, start=start, stop=stop),
                       reads=reads, writes=writes, signal=(stop if signal is None else signal))

    def tr(self, out, in_, ident, reads, writes, signal=True):
        return self.op("pe", lambda e: e.transpose(out, in_, ident), reads=reads, writes=writes, signal=signal)

    def act(self, out, in_, func, reads, writes, bias=0.0, scale=1.0, accum_out=None):
        if accum_out is None:
            return self.op("act", lambda e: e.activation(out=out, in_=in_, func=func, bias=bias, scale=scale),
                           reads=reads, writes=writes)
        return self.op("act", lambda e: e.activation(out=out, in_=in_, func=func, bias=bias, scale=scale,
                                                      accum_out=accum_out), reads=reads, writes=writes)

    def tt(self, eng, out, in0, in1, op, reads, writes):
        return self.op(eng, lambda e: e.tensor_tensor(out=out, in0=in0, in1=in1, op=op), reads=reads, writes=writes)

    def ts(self, eng, out, in0, s1, s2, op0, op1, reads, writes):
        if op1 is None:
            return self.op(eng, lambda e: e.tensor_scalar(out=out, in0=in0, scalar1=s1, scalar2=None, op0=op0),
                           reads=reads, writes=writes)
        return self.op(eng, lambda e: e.tensor_scalar(out=out, in0=in0, scalar1=s1, scalar2=s2, op0=op0, op1=op1),
                       reads=reads, writes=writes)

    def stt(self, eng, out, in0, scalar, in1, op0, op1, reads, writes):
        return self.op(eng, lambda e: e.scalar_tensor_tensor(out=out, in0=in0, scalar=scalar, in1=in1, op0=op0, op1=op1),
                       reads=reads, writes=writes)

    def cp(self, eng, out, in_, reads, writes):
        if eng == "act":
            return self.op("act", lambda e: e.activation(out=out, in_=in_, func=AF.Identity), reads=reads, writes=writes)
        return self.op(eng, lambda e: e.tensor_copy(out=out, in_=in_), reads=reads, writes=writes)


class Scope:
    def __init__(self, P):
        self.P = P

    def __enter__(self):
        self.old = self.P.es
        self.st = ExitStack()
        self.st.__enter__()
        self.P.es = self.st
        return self

    def __exit__(self, *a):
        self.P.barrier()
        self.P.es = self.old
        return self.st.__exit__(*a)


CN = ["ident", "ones", "triKQ", "triU", "triBD", "half0", "half1", "blk", "lm", "sm", "smT", "inclT", "sguT"]


def make_consts():
    i = np.arange(128)
    ch = i // 64
    same = ch[:, None] == ch[None, :]
    c = {}
    c["ident"] = np.eye(128)
    c["ones"] = np.ones((128, 128))
    c["triKQ"] = (i[None, :] >= i[:, None])
    c["triU"] = (i[:, None] <= i[None, :])
    c["triBD"] = same & (i[:, None] <= i[None, :])
    c["half0"] = np.broadcast_to((i < 64)[:, None], (128, 128))
    c["half1"] = np.broadcast_to((i >= 64)[:, None], (128, 128))
    c["blk"] = same
    c["lm"] = same & (i[:, None] > i[None, :])
    c["sm"] = same & (i[:, None] > i[None, :])
    c["smT"] = same & (i[None, :] > i[:, None])
    c["inclT"] = same & (i[None, :] >= i[:, None])
    c["sguT"] = (ch[None, :] >= ch[:, None])
    return np.ascontiguousarray(np.stack([c[n].astype(np.float32) for n in CN], axis=1))


FM_RANGES = [("fq", 0, 1024), ("fk", 1024, 1024), ("su", 3080, 1024), ("g3", 5128, 3072), ("mg", 9240, 6144)]
TM_RANGES = [("fv", 2048, 1024), ("sv", 4104, 1024), ("gg", 8216, 1024)]
SM_COLS = list(range(3072, 3080)) + list(range(8200, 8216))
NFM = sum(r[2] for r in FM_RANGES) // 128

BC = {}
_o = 0
for _n, _w in [("btm", 3072), ("bsm", 24), ("ln1g", D), ("ln1b", D), ("ln2g", D), ("ln2b", D),
               ("slg", 1024), ("slb", 1024), ("gng", 128), ("alog", 8), ("dtb", 8), ("sgub", 1024)]:
    BC[_n] = (_o, _w)
    _o += _w
BCW = _o
PC = {}
_o = 0
for _n, _w in [("bfm", NFM), ("gcw", 24 * 4), ("fcw", 86 * 3), ("fcb", 86)]:
    PC[_n] = (_o, _w)
    _o += _w
PCW = _o


def layer_small(inp, l):
    bin_ = inp["b_in"][l]
    rows = [np.concatenate([bin_[c0:c0 + w] for _, c0, w in TM_RANGES]), bin_[SM_COLS],
            inp["ln1_g"][l], inp["ln1_b"][l], inp["ln2_g"][l], inp["ln2_b"][l],
            inp["sgu_ln_g"][l], inp["sgu_ln_b"][l], inp["gdn_norm_g"][l], inp["gdn_a_log"][l],
            inp["gdn_dt_bias"][l], inp["sgu_b"][l].reshape(-1)]
    row = np.concatenate(rows).astype(np.float32)
    assert row.shape[0] == BCW
    bc = np.ascontiguousarray(np.broadcast_to(row[None, :], (128, BCW)))
    bfm = np.concatenate([bin_[c0:c0 + w] for _, c0, w in FM_RANGES]).reshape(NFM, 128).T
    gcw = inp["gdn_conv_w"][l].reshape(4, 24, 128).transpose(2, 1, 0).reshape(128, 96)
    fcw = inp["ffn_conv_w"][l].reshape(3, 86, 128).transpose(2, 1, 0).reshape(128, 258)
    fcb = inp["ffn_conv_b"][l].reshape(86, 128).T
    pc = np.ascontiguousarray(np.concatenate([bfm, gcw, fcw, fcb], axis=1).astype(np.float32))
    assert pc.shape[1] == PCW
    sguwT = np.ascontiguousarray(inp["sgu_w"][l].transpose(2, 0, 1))
    return bc, pc, sguwT


class Ctx:
    pass


def build(nl, T, dbg_in=(), dbg_out=(), stages="0ABCDEF"):
    nc = bass.Bass("TRN2", target_bir_lowering=False)
    NB = T // 128
    C = Ctx()
    C.nc, C.T, C.NB = nc, T, NB

    def ext_in(name, shape, dt=F32):
        return nc.dram_tensor(name, list(shape), dt, kind="ExternalInput").ap()

    def scr(name, shape, dt):
        kind = "ExternalInput" if name in dbg_in else ("ExternalOutput" if name in dbg_out else "Internal")
        return nc.dram_tensor(name, list(shape), dt, kind=kind).ap()

    x_in = ext_in("x", [T, D])
    consts = ext_in("consts", [128, len(CN), 128])
    W = []
    for l in range(nl):
        w = Ctx()
        w.w_in = ext_in(f"w_in{l}", [D, NIN])
        w.bc = ext_in(f"bc{l}", [128, BCW])
        w.pc = ext_in(f"pc{l}", [128, PCW])
        w.sguwT = ext_in(f"sguwT{l}", [128, 8, 128])
        w.wpa = ext_in(f"wpa{l}", [1024, D])
        w.wpb = ext_in(f"wpb{l}", [1024, D])
        w.wpc = ext_in(f"wpc{l}", [1024, D])
        w.wout = ext_in(f"wout{l}", [D, D])
        w.wup = ext_in(f"wup{l}", [D, 2 * DFF])
        w.wdn = ext_in(f"wdn{l}", [DFF, D])
        W.append(w)
    y_out = nc.dram_tensor("y", [T, D], F32, kind="ExternalOutput").ap()

    S = Ctx()
    S.xr1 = scr("xr1", [T, D], F32)
    S.xr0 = scr("xr0", [T, D], F32)
    S.xT = scr("xT", [D, T], BF16)
    S.x1T = scr("x1T", [D, T], BF16)
    S.fqT = scr("fqT", [1024, T], BF16)
    S.fkT = scr("fkT", [1024, T], BF16)
    S.suT = scr("suT", [1024, T], BF16)
    S.g3T = scr("g3T", [3072, T], BF16)
    S.mgT = scr("mgT", [6144, T], BF16)
    S.fv = scr("fv", [T, 1024], BF16)
    S.sv = scr("sv", [T, 1024], BF16)
    S.gg = scr("gg", [T, 1024], BF16)
    S.sm = scr("sm", [T, 24], F32)
    S.yaT = scr("yaT", [1024, T], BF16)
    S.ybT = scr("ybT", [1024, T], BF16)
    S.ycT = scr("ycT", [1024, T], BF16)
    S.mT = scr("mT", [D, T], BF16)
    S.wupb = scr("wupb", [22, 128, 2, 16, 256], BF16)
    S.wdnb = scr("wdnb", [8, 128, 43, 256], BF16)
    S.wpab = scr("wpab", [128, 8, D], BF16)
    S.wpbb = scr("wpbb", [128, 8, D], BF16)
    S.wpcb = scr("wpcb", [128, 8, D], BF16)
    S.wob = scr("wob", [128, 16, D], BF16)
    SB = {k: Buf(k) for k in vars(S)}
    SB["y"] = Buf("y")
    SB["x"] = Buf("x")
    C.S, C.SB = S, SB

    with ExitStack() as es:
        P = Prog(nc, es)
        C.P = P
        C.ps = [Tl(es.enter_context(nc.psum_tensor(f"ps{i}", [128, 512], F32)), f"ps{i}") for i in range(6)]
        C.pb = [Tl(es.enter_context(nc.psum_tensor(f"pb{i}", [128, 1024], BF16)), f"pb{i}") for i in range(2)]
        for t_ in C.ps + C.pb:
            t_.b.excl = True
        C.c32 = P.sb("c32", [128, len(CN), 128], F32)
        C.cbf = P.sb("cbf", [128, len(CN), 128], BF16)
        P.dma("sp", C.c32[:], consts, writes=[C.c32], sem="ld_c")
        P.cp("dve", C.cbf[:], C.c32[:], [C.c32], [C.cbf])
        C.k32 = lambda n: C.c32[:, CN.index(n), :]
        C.kbf = lambda n: C.cbf[:, CN.index(n), :]

        for l in range(nl):
            w = W[l]
            xin, xin_b = (x_in, SB["x"]) if l == 0 else (S.xr0, SB["xr0"])
            last = (l == nl - 1)
            if l > 0:
                P.epoch(l)
            xout, xout_b = (y_out, SB["y"]) if last else (S.xr0, SB["xr0"])
            if "0" in stages:
                stage0(C, xin, xin_b)
            if "A" in stages:
                stageA(C, w)
            if "B" in stages:
                stageB(C, w)
            if "C" in stages:
                stageC(C, w)
            if "D" in stages:
                stageD(C, w)
            if "E" in stages:
                stageE1(C, w)
                stageE2(C, w, xin, xin_b)
            if "F" in stages:
                stageF(C, w, xout, xout_b)
        P.barrier()
        C.nops, C.nwait = P.nops, P.nwait
    return nc, C


class TStore:
    def __init__(self, C, dst, dst_buf, name):
        P = C.P
        self.C, self.dst, self.dst_buf = C, dst, dst_buf
        self.tiles = [P.sb(f"{name}_xTt{i}", [128, 16, 512], BF16) for i in range(2)]
        self.name = name
        self.n = 0

    def push(self, src, tb):
        C, P = self.C, self.C.P
        g, bi = tb // 4, tb % 4
        xt = self.tiles[g % 2]
        for q in range(4):
            pb = C.pb[self.n % 2]
            self.n += 1
            for j in range(4):
                kc = q * 4 + j
                P.tr(pb[:, j * 128:(j + 1) * 128], src[:, kc * 128:(kc + 1) * 128], C.kbf("ident"),
                     [src, C.cbf], [pb], signal=(j == 3))
            eng = "act" if (q % 2) else "dve"
            P.cp(eng, xt[:, q * 4:(q + 1) * 4, bi * 128:(bi + 1) * 128],
                 pb[:, 0:512].rearrange("p (j t) -> p j t", j=4), [pb], [xt])
        if bi == 3:
            P.dma("pool", self.dst.rearrange("(kc p) t -> p kc t", p=128)[:, :, g * 512:(g + 1) * 512], xt[:],
                  reads=[xt], writes=[self.dst_buf], sem=f"st_{self.name}{g % 2}")


def stage0(C, xin, xin_b):
    P, S, SB = C.P, C.S, C.SB
    with Scope(P):
        ts = TStore(C, S.xT, SB["xT"], "s0")
        x32 = [P.sb(f"x32_{i}", [128, D], F32) for i in range(2)]
        xbf = [P.sb(f"xbf_{i}", [128, D], BF16) for i in range(2)]
        for tb in range(C.NB):
            a, b = x32[tb % 2], xbf[tb % 2]
            P.dma("sp", a[:], xin[tb * 128:(tb + 1) * 128, :], reads=[xin_b], writes=[a], sem=f"ld_x{tb % 2}")
            P.cp("pool", b[:, 0:1024], a[:, 0:1024], [a], [b])
            P.cp("act", b[:, 1024:2048], a[:, 1024:2048], [a], [b])
            ts.push(b, tb)


def stageA(C, w):
    P, S, SB, T = C.P, C.S, C.SB, C.T
    TT = min(T, 2048)
    fm_dst = {"fq": (S.fqT, "fqT"), "fk": (S.fkT, "fkT"), "su": (S.suT, "suT"), "g3": (S.g3T, "g3T"), "mg": (S.mgT, "mgT")}
    tm_dst = {"fv": (S.fv, "fv"), "sv": (S.sv, "sv"), "gg": (S.gg, "gg")}
    with Scope(P):
        xs = P.sb("xs", [128, 16, TT], BF16)
        wt = [P.sb(f"wt{i}", [128, 16, 512], BF16) for i in range(2)]
        wsm = P.sb("wsm", [128, 16, 24], BF16)
        bfm = P.sb("bfm", [128, NFM], F32)
        btm = P.sb("btm", [128, 3072 + 24], F32)
        ofm = [P.sb(f"ofm{i}", [128, TT], BF16) for i in range(2)]
        otm = [P.sb(f"otm{i}", [128, 512], BF16) for i in range(2)]
        osm = [P.sb(f"osm{i}", [128, 24], F32) for i in range(2)]
        P.dma("sp", bfm[:], w.pc[:, PC["bfm"][0]:PC["bfm"][0] + NFM], writes=[bfm], sem="ld_s0")
        P.dma("sp", btm[:], w.bc[:, 0:3096], writes=[btm], sem="ld_s1")
        wi = 0
        pi = 0
        for ts_ in range(T // TT):
            t0 = ts_ * TT
            P.dma("sp", xs[:], S.xT.rearrange("(kc p) t -> p kc t", p=128)[:, :, t0:t0 + TT],
                  reads=[SB["xT"]], writes=[xs], sem="ld_xs")
            ci = 0
            oi = 0
            for name, c0, width in FM_RANGES:
                dst, dname = fm_dst[name]
                func = AF.Sigmoid if name == "mg" else AF.Identity
                for g in range(width // 512):
                    wb = wt[wi % 2]
                    P.dma("pool", wb[:], w.w_in.rearrange("(kc p) n -> p kc n", p=128)[:, :, c0 + g * 512:c0 + (g + 1) * 512],
                          writes=[wb], sem=f"ld_w{wi % 2}")
                    wi += 1
                    for c in range(4):
                        ob = ofm[oi % 2]
                        oi += 1
                        for tt in range(TT // 512):
                            ps = C.ps[pi % 4]
                            pi += 1
                            for kc in range(16):
                                P.mm(ps[:], wb[:, kc, c * 128:(c + 1) * 128], xs[:, kc, tt * 512:(tt + 1) * 512],
                                     kc == 0, kc == 15, [wb, xs], [ps])
                            P.act(ob[:, tt * 512:(tt + 1) * 512], ps[:], func, [ps, bfm], [ob],
                                  bias=bfm[:, ci:ci + 1])
                        r0 = g * 512 + c * 128
                        P.dma("sp", dst[r0:r0 + 128, t0:t0 + TT], ob[:], reads=[ob], writes=[SB[dname]],
                              sem=f"st_ofm{(oi - 1) % 2}")
                        ci += 1
            bo = 0
            oi = 0
            for name, c0, width in TM_RANGES:
                dst, dname = tm_dst[name]
                for g in range(width // 512):
                    wb = wt[wi % 2]
                    P.dma("pool", wb[:], w.w_in.rearrange("(kc p) n -> p kc n", p=128)[:, :, c0 + g * 512:c0 + (g + 1) * 512],
                          writes=[wb], sem=f"ld_w{wi % 2}")
                    wi += 1
                    for tb in range(TT // 128):
                        ps = C.ps[pi % 4]
                        pi += 1
                        for kc in range(16):
                            P.mm(ps[:], xs[:, kc, tb * 128:(tb + 1) * 128], wb[:, kc, :], kc == 0, kc == 15, [wb, xs], [ps])
                        ob = otm[oi % 2]
                        oi += 1
                        P.tt("dve", ob[:], ps[:], btm[:, bo:bo + 512], ALU.add, [ps, btm], [ob])
                        P.dma("sp", dst[t0 + tb * 128:t0 + (tb + 1) * 128, g * 512:(g + 1) * 512], ob[:], reads=[ob],
                              writes=[SB[dname]], sem=f"st_otm{(oi - 1) % 2}")
                    bo += 512
            if ts_ == 0:
                for i, c0 in enumerate((3072, 8200, 8208)):
                    P.dma("pool", wsm[:, :, i * 8:(i + 1) * 8], w.w_in.rearrange("(kc p) n -> p kc n", p=128)[:, :, c0:c0 + 8],
                          writes=[wsm], sem="ld_wsm")
            oi = 0
            for tb in range(TT // 128):
                ps = C.ps[pi % 4]
                pi += 1
                for kc in range(16):
                    P.mm(ps[:, 0:24], xs[:, kc, tb * 128:(tb + 1) * 128], wsm[:, kc, :], kc == 0, kc == 15, [wsm, xs], [ps])
                ob = osm[oi % 2]
                oi += 1
                P.tt("dve", ob[:], ps[:, 0:24], btm[:, 3072:3096], ALU.add, [ps, btm], [ob])
                P.dma("sp", S.sm[t0 + tb * 128:t0 + (tb + 1) * 128, :], ob[:], reads=[ob], writes=[SB["sm"]],
                      sem=f"st_osm{(oi - 1) % 2}")


def stageB(C, w):
    P, S, SB, T, NB = C.P, C.S, C.SB, C.T, C.NB
    scale = 128 ** -0.5
    with Scope(P):
        smf = P.sb("smf", [128, NB, 8], F32)
        Lp = P.sb("Lp", [128, NB, 8], F32)
        Lc = P.sb("Lc", [128, NB, 8], F32)
        tot = P.sb("tot", [128, NB, 8], F32)
        carry = P.sb("carry", [128, NB + 1, 8], F32)
        BT = [P.sb(f"BT{i}", [128, NB, NB], F32) for i in range(2)]
        kT = [P.sb(f"kT{i}", [128, T], BF16) for i in range(2)]
        qT = [P.sb(f"qT{i}", [128, T], BF16) for i in range(2)]
        vv = [P.sb(f"vv{i}", [128, NB, 128], BF16) for i in range(2)]
        pt = [P.sb(f"pt{i}", [128, 512], BF16) for i in range(3)]
        rl = P.sb("rl", [128, 512], F32)
        yat = [P.sb(f"yat{i}", [128, 512], BF16) for i in range(2)]
        P.dma("sp", smf[:], S.sm.rearrange("(b p) c -> p b c", p=128)[:, :, 0:8], reads=[SB["sm"]], writes=[smf], sem="ld_s0")
        P.act(Lp[:], smf[:], AF.Exp, [smf], [Lp], scale=-1.0)
        P.act(Lp[:], Lp[:], AF.Ln, [Lp], [Lp], bias=1.0)
        Lp2 = Lp[:].rearrange("p b h -> p (b h)")
        psW, psT = C.ps[4], C.ps[5]
        P.mm(psW[:, 0:NB * 8], C.k32("triU"), Lp2, True, True, [C.c32, Lp], [psW])
        P.mm(psT[:, 0:NB * 8], C.k32("ones"), Lp2, True, True, [C.c32, Lp], [psT])
        P.cp("dve", tot[:].rearrange("p b h -> p (b h)"), psT[:, 0:NB * 8], [psT], [tot])
        P.op("dve", lambda e: e.memset(carry[:, 0, :], 0.0), writes=[carry])
        for b in range(NB):
            P.tt("dve", carry[:, b + 1, :], carry[:, b, :], tot[:, b, :], ALU.add, [carry, tot], [carry])
        P.tt("dve", Lc[:].rearrange("p b h -> p (b h)"), psW[:, 0:NB * 8],
             carry[:, 0:NB, :].rearrange("p b h -> p (b h)"), ALU.add, [psW, carry], [Lc])
        iters = [(h, qt, i, 4 * qt + 4) for h in range(8) for qt in range(T // 512) for i in range(4 * qt + 4)]
        yi = [0]

        def qk_exp(n):
            h, qt, i, nk = iters[n]
            k_, q_, v_, bt = kT[h % 2], qT[h % 2], vv[h % 2], BT[h % 2]
            if qt == 0 and i == 0:
                P.dma("sp", k_[:], S.fkT[h * 128:(h + 1) * 128, :], reads=[SB["fkT"]], writes=[k_], sem=f"ld_k{h % 2}")
                P.dma("sp", q_[:], S.fqT[h * 128:(h + 1) * 128, :], reads=[SB["fqT"]], writes=[q_], sem=f"ld_q{h % 2}")
                P.dma("sp", v_[:], S.fv.rearrange("(b p) d -> p b d", p=128)[:, :, h * 128:(h + 1) * 128],
                      reads=[SB["fv"]], writes=[v_], sem=f"ld_v{h % 2}")
                P.tt("dve", bt[:], Lc[:, :, h].unsqueeze(2).to_broadcast([128, NB, NB]),
                     carry[:, 1:NB + 1, h].unsqueeze(1).to_broadcast([128, NB, NB]), ALU.subtract, [Lc, carry], [bt])
            jmin = max(0, i - 4 * qt)
            c0 = jmin * 128
            pss = C.ps[n % 2]
            P.mm(pss[:, c0:512], k_[:, i * 128:(i + 1) * 128], q_[:, qt * 512 + c0:(qt + 1) * 512], True, True,
                 [k_, q_], [pss])
            p_ = pt[n % 3]
            for j in range(jmin, 4):
                P.act(p_[:, j * 128:(j + 1) * 128], pss[:, j * 128:(j + 1) * 128], AF.Exp, [pss, bt], [p_],
                      bias=bt[:, i, 4 * qt + j:4 * qt + j + 1], scale=scale)
            if i >= 4 * qt:
                P.tt("pool", p_[:, c0:c0 + 128], p_[:, c0:c0 + 128], C.kbf("triKQ"), ALU.mult, [p_, C.cbf], [p_])

        def pv_l(n):
            h, qt, i, nk = iters[n]
            v_ = vv[h % 2]
            p_ = pt[n % 3]
            pso = C.ps[4 + (qt % 2)]
            psl = C.ps[2 + (qt % 2)]
            c0 = max(0, i - 4 * qt) * 128
            P.mm(pso[:, c0:512], v_[:, i, :], p_[:, c0:512], i == 0, i == nk - 1, [v_, p_], [pso])
            P.mm(psl[:, c0:512], C.kbf("ones"), p_[:, c0:512], i == 0, i == nk - 1, [C.cbf, p_], [psl])
            if i == nk - 1:
                P.op("dve", lambda e: e.reciprocal(out=rl[:], in_=psl[:]), reads=[psl], writes=[rl])
                y_ = yat[yi[0] % 2]
                sl = yi[0] % 2
                yi[0] += 1
                P.tt("dve", y_[:], pso[:], rl[:], ALU.mult, [pso, rl], [y_])
                P.dma("sp", S.yaT[h * 128:(h + 1) * 128, qt * 512:(qt + 1) * 512], y_[:], reads=[y_], writes=[SB["yaT"]],
                      sem=f"st_ya{sl}")

        for n in range(len(iters) + 1):
            if n < len(iters):
                qk_exp(n)
            if n >= 1:
                pv_l(n - 1)


def stageC(C, w):
    P, S, SB, T, NB = C.P, C.S, C.SB, C.T, C.NB
    with Scope(P):
        wT32 = P.sb("wT32", [128, 8, 128], F32)
        wTm = P.sb("wTm", [128, 8, 128], BF16)
        lg = P.sb("lg", [128, 1024], F32)
        lb = P.sb("lb", [128, 1024], F32)
        bsb = P.sb("bsb", [128, 8, 128], F32)
        sv = [P.sb(f"sv{i}", [128, 1024], BF16) for i in range(2)]
        su = [P.sb(f"su{i}", [128, 8, 512], BF16) for i in range(2)]
        yb = [P.sb(f"yb{i}", [128, 8, 512], BF16) for i in range(2)]
        sq = P.sb("sq", [128, 1024], F32)
        vn = P.sb("vn", [128, 1024], F32)
        vg = [P.sb(f"vg{i}", [128, 1024], BF16) for i in range(2)]
        tmp = P.sb("tmp", [128, 8, 128], F32)
        st = P.sb("st", [128, 4, 8], F32)
        P.dma("sp", wT32[:], w.sguwT, writes=[wT32], sem="ld_s0")
        P.dma("sp", lg[:], w.bc[:, BC["slg"][0]:BC["slg"][0] + 1024], writes=[lg], sem="ld_s1")
        P.dma("sp", lb[:], w.bc[:, BC["slb"][0]:BC["slb"][0] + 1024], writes=[lb], sem="ld_s2")
        P.dma("sp", bsb[:].rearrange("p g t -> p (g t)"), w.bc[:, BC["sgub"][0]:BC["sgub"][0] + 1024], writes=[bsb], sem="ld_s3")
        P.tt("dve", wTm[:], wT32[:], C.k32("sguT").unsqueeze(1).to_broadcast([128, 8, 128]), ALU.mult, [wT32, C.c32], [wTm])
        for g4 in range(NB // 4):
            u_, y_ = su[g4 % 2], yb[g4 % 2]
            P.dma("sp", u_[:], S.suT.rearrange("(g c) t -> c g t", c=128)[:, :, g4 * 512:(g4 + 1) * 512],
                  reads=[SB["suT"]], writes=[u_], sem=f"ld_su{g4 % 2}")
            for bi in range(4):
                sp_ = g4 * 4 + bi
                v_ = sv[sp_ % 2]
                P.dma("sp", v_[:], S.sv[sp_ * 128:(sp_ + 1) * 128, :], reads=[SB["sv"]], writes=[v_], sem=f"ld_sv{sp_ % 2}")
                v3 = v_[:].rearrange("p (g c) -> p g c", g=8)
                P.op("dve", lambda e: e.tensor_reduce(out=st[:, 0, :], in_=v3, axis=AX.X, op=ALU.add), reads=[v_], writes=[st])
                P.act(sq[:], v_[:], AF.Square, [v_], [sq])
                P.op("dve", lambda e: e.tensor_reduce(out=st[:, 1, :], in_=sq[:].rearrange("p (g c) -> p g c", g=8),
                                                      axis=AX.X, op=ALU.add), reads=[sq], writes=[st])
                P.ts("dve", st[:, 2, :], st[:, 0, :], 1.0 / 128, None, ALU.mult, None, [st], [st])
                P.tt("dve", st[:, 0, :], st[:, 2, :], st[:, 2, :], ALU.mult, [st], [st])
                P.stt("dve", st[:, 1, :], st[:, 1, :], 1.0 / 128, st[:, 0, :], ALU.mult, ALU.subtract, [st], [st])
                P.act(st[:, 3, :], st[:, 1, :], AF.Ln, [st], [st], bias=LN_EPS)
                P.act(st[:, 3, :], st[:, 3, :], AF.Exp, [st], [st], scale=-0.5)
                vn3 = vn[:].rearrange("p (g c) -> p g c", g=8)
                P.tt("dve", vn3, v3, st[:, 2, :].unsqueeze(2).to_broadcast([128, 8, 128]), ALU.subtract, [v_, st], [vn])
                P.tt("pool", vn3, vn3, st[:, 3, :].unsqueeze(2).to_broadcast([128, 8, 128]), ALU.mult, [vn, st], [vn])
                P.tt("pool", vn[:], vn[:], lg[:], ALU.mult, [vn, lg], [vn])
                vg_ = vg[sp_ % 2]
                P.tt("dve", vg_[:], vn[:], lb[:], ALU.add, [vn, lb], [vg_])
                for half in range(2):
                    ps = C.ps[(sp_ * 2 + half) % 4]
                    for gg in range(4):
                        g = half * 4 + gg
                        P.mm(ps[:, gg * 128:(gg + 1) * 128], vg_[:, g * 128:(g + 1) * 128], wTm[:, g, :], True, True,
                             [vg_, wTm], [ps], signal=(gg == 3))
                    t3 = tmp[:, half * 4:(half + 1) * 4, :]
                    P.tt("dve", t3, ps[:].rearrange("p (g t) -> p g t", g=4), bsb[:, half * 4:(half + 1) * 4, :], ALU.add,
                         [ps, bsb], [tmp])
                    P.tt("pool", y_[:, half * 4:(half + 1) * 4, bi * 128:(bi + 1) * 128], t3,
                         u_[:, half * 4:(half + 1) * 4, bi * 128:(bi + 1) * 128], ALU.mult, [tmp, u_], [y_])
            P.dma("sp", S.ybT.rearrange("(g c) t -> c g t", c=128)[:, :, g4 * 512:(g4 + 1) * 512], y_[:], reads=[y_],
                  writes=[SB["ybT"]], sem=f"st_yb{g4 % 2}")


def stageE1(C, w):
    P, S, SB, T = C.P, C.S, C.SB, C.T
    with Scope(P):
        wp = [P.sb(f"wp{i}", [128, 8, D], BF16) for i in range(3)]
        ys = [[P.sb(f"ys{i}_{j}", [128, 8, 512], BF16) for j in range(3)] for i in range(2)]
        gt = [P.sb(f"gt{i}", [128, 3, 512], BF16) for i in range(2)]
        acc = P.sb("acc", [128, 512], F32)
        t1 = P.sb("t1", [128, 512], F32)
        t2 = P.sb("t2", [128, 512], F32)
        mt = [P.sb(f"mt{i}", [128, 512], BF16) for i in range(2)]
        for i, (wsrc, wsc, nm) in enumerate(((w.wpa, S.wpab, "wpab"), (w.wpb, S.wpbb, "wpbb"), (w.wpc, S.wpcb, "wpcb"))):
            if PRECAST:
                P.dma("sp", wp[i][:], wsc, reads=[SB[nm]], writes=[wp[i]], sem=f"ld_wp{i}")
                continue
            for hh in range(2):
                P.dma("pool", wp[i][:, :, hh * 1024:(hh + 1) * 1024],
                      wsrc.rearrange("(kc p) n -> p kc n", p=128)[:, :, hh * 1024:(hh + 1) * 1024], writes=[wp[i]], sem=f"ld_wp{i}")
        srcs = [(S.yaT, "yaT"), (S.ybT, "ybT"), (S.ycT, "ycT")]
        gi = 0
        for tt in range(T // 512):
            yt = ys[tt % 2]
            for i, (src, nm) in enumerate(srcs):
                P.dma("sp", yt[i][:], src.rearrange("(kc p) t -> p kc t", p=128)[:, :, tt * 512:(tt + 1) * 512],
                      reads=[SB[nm]], writes=[yt[i]], sem=f"ld_ys{tt % 2}_{i}")
            for fc in range(16):
                g_ = gt[gi % 2]
                m_ = mt[gi % 2]
                P.dma("sp", g_[:], S.mgT.rearrange("(b fc p) t -> p b fc t", b=3, p=128)[:, :, fc, tt * 512:(tt + 1) * 512],
                      reads=[SB["mgT"]], writes=[g_], sem=f"ld_gt{gi % 2}")
                pss = [C.ps[(gi % 2) * 3 + i] for i in range(3)]
                gi += 1
                for i in range(3):
                    for kc in range(8):
                        P.mm(pss[i][:], wp[i][:, kc, fc * 128:(fc + 1) * 128], yt[i][:, kc, :], kc == 0, kc == 7,
                             [wp[i], yt[i]], [pss[i]])
                P.tt("dve", acc[:], pss[0][:], g_[:, 0, :], ALU.mult, [pss[0], g_], [acc])
                P.tt("dve", t1[:], pss[1][:], g_[:, 1, :], ALU.mult, [pss[1], g_], [t1])
                P.tt("dve", t2[:], pss[2][:], g_[:, 2, :], ALU.mult, [pss[2], g_], [t2])
                P.tt("pool", acc[:], acc[:], t1[:], ALU.add, [acc, t1], [acc])
                P.tt("pool", m_[:], acc[:], t2[:], ALU.add, [acc, t2], [m_])
                P.dma("sp", S.mT[fc * 128:(fc + 1) * 128, tt * 512:(tt + 1) * 512], m_[:], reads=[m_], writes=[SB["mT"]],
                      sem=f"st_mt{(gi - 1) % 2}")


def layer_norm_tile(C, y, st6, mv, sc, gt_, bt_):
    P = C.P
    for q in range(4):
        P.op("dve", lambda e, q=q: e.bn_stats(out=st6[:, q, :], in_=y[:, q * 512:(q + 1) * 512]), reads=[y], writes=[st6])
    P.op("dve", lambda e: e.bn_aggr(out=mv[:], in_=st6[:]), reads=[st6], writes=[mv])
    P.act(sc[:, 0:1], mv[:, 1:2], AF.Ln, [mv], [sc], bias=LN_EPS)
    P.act(sc[:, 0:1], sc[:, 0:1], AF.Exp, [sc], [sc], scale=-0.5)
    P.stt("dve", sc[:, 1:2], mv[:, 0:1], -1.0, sc[:, 0:1], ALU.mult, ALU.mult, [mv, sc], [sc])
    P.act(y[:], y[:], AF.Identity, [y, sc], [y], bias=sc[:, 1:2], scale=sc[:, 0:1])
    P.tt("pool", y[:], y[:], gt_[:], ALU.mult, [y, gt_], [y])
    P.tt("dve", y[:], y[:], bt_[:], ALU.add, [y, bt_], [y])


def stageE2(C, w, xin, xin_b):
    P, S, SB, T, NB = C.P, C.S, C.SB, C.T, C.NB
    with Scope(P):
        wo = P.sb("wo", [128, 16, D], BF16)
        mt = [P.sb(f"mtl{i}", [128, 16, 512], BF16) for i in range(2)]
        yt = [P.sb(f"yt{i}", [128, D], F32) for i in range(2)]
        xb = [P.sb(f"xb{i}", [128, D], BF16) for i in range(2)]
        lg = P.sb("lg", [128, D], F32)
        lb = P.sb("lb", [128, D], F32)
        st6 = P.sb("st6", [128, 4, 6], F32)
        mv = P.sb("mv", [128, 2], F32)
        sc = P.sb("sc", [128, 2], F32)
        ts = TStore(C, S.x1T, SB["x1T"], "e2")
        if PRECAST:
            P.dma("sp", wo[:], S.wob, reads=[SB["wob"]], writes=[wo], sem="ld_wo")
        else:
            for q in range(4):
                P.dma("pool", wo[:, :, q * 512:(q + 1) * 512], w.wout.rearrange("(kc p) n -> p kc n", p=128)[:, :, q * 512:(q + 1) * 512],
                      writes=[wo], sem="ld_wo")
        P.dma("sp", lg[:], w.bc[:, BC["ln1g"][0]:BC["ln1g"][0] + D], writes=[lg], sem="ld_s0")
        P.dma("sp", lb[:], w.bc[:, BC["ln1b"][0]:BC["ln1b"][0] + D], writes=[lb], sem="ld_s1")
        pend = None
        for tb in range(NB):
            g, bi = tb // 4, tb % 4
            m_ = mt[g % 2]
            if bi == 0:
                P.dma("sp", m_[:], S.mT.rearrange("(kc p) t -> p kc t", p=128)[:, :, g * 512:(g + 1) * 512],
                      reads=[SB["mT"]], writes=[m_], sem=f"ld_mt{g % 2}")
            y_ = yt[tb % 2]
            P.dma("sp", y_[:], xin[tb * 128:(tb + 1) * 128, :], reads=[xin_b], writes=[y_], sem=f"ld_y{tb % 2}")
            for cg in range(4):
                ps = C.ps[cg]
                for kc in range(16):
                    P.mm(ps[:], m_[:, kc, bi * 128:(bi + 1) * 128], wo[:, kc, cg * 512:(cg + 1) * 512], kc == 0, kc == 15,
                         [m_, wo], [ps])
                P.stt("dve", y_[:, cg * 512:(cg + 1) * 512], y_[:, cg * 512:(cg + 1) * 512], ALPHA, ps[:], ALU.mult, ALU.add,
                      [y_, ps], [y_])
            if pend is not None:
                ts.push(*pend)
            layer_norm_tile(C, y_, st6, mv, sc, lg, lb)
            P.dma("pool", S.xr1[tb * 128:(tb + 1) * 128, :], y_[:], reads=[y_], writes=[SB["xr1"]], sem=f"st_y{tb % 2}")
            b_ = xb[tb % 2]
            P.cp("act", b_[:], y_[:], [y_], [b_])
            pend = (b_, tb)
        ts.push(*pend)


def stageF(C, w, xout, xout_b):
    P, S, SB, T = C.P, C.S, C.SB, C.T
    NP = DFF // 128
    with Scope(P):
        x1 = P.sb("x1", [128, 16, 512], BF16)
        wu = [P.sb(f"wu{i}", [128, 2, 16, 256], BF16) for i in range(2)]
        aT = P.sb("aT", [128, NP, 512], BF16)
        wd = [P.sb(f"wd{i}", [128, NP, 256], BF16) for i in range(2)]
        yt = P.sb("ytf", [128, 4, D], F32)
        lg = P.sb("lg", [128, D], F32)
        lb = P.sb("lb", [128, D], F32)
        fcw = P.sb("fcw", [128, 86, 3], F32)
        fcb = P.sb("fcb", [128, 86], F32)
        hal = P.sb("hal", [128, 86, 2], F32)
        hb = [P.sb(f"hb{i}", [128, 2, 514], F32) for i in range(1)]
        cv = [P.sb(f"cv{i}", [128, 2, 512], F32) for i in range(1)]
        sg = P.sb("sg", [128, 512], F32)
        st6 = P.sb("st6", [128, 4, 6], F32)
        mv = P.sb("mv", [128, 2], F32)
        sc = P.sb("sc", [128, 2], F32)
        P.dma("sp", lg[:], w.bc[:, BC["ln2g"][0]:BC["ln2g"][0] + D], writes=[lg], sem="ld_s0")
        P.dma("sp", lb[:], w.bc[:, BC["ln2b"][0]:BC["ln2b"][0] + D], writes=[lb], sem="ld_s1")
        P.dma("sp", fcw[:].rearrange("p c k -> p (c k)"), w.pc[:, PC["fcw"][0]:PC["fcw"][0] + 258], writes=[fcw], sem="ld_s2")
        P.dma("sp", fcb[:], w.pc[:, PC["fcb"][0]:PC["fcb"][0] + 86], writes=[fcb], sem="ld_s3")
        P.op("pool", lambda e: e.memset(hal[:], 0.0), writes=[hal])
        wui = 0
        wdi = 0
        pi = 0
        hi = 0
        wupv = w.wup.rearrange("(kc p) (two n) -> p two kc n", p=128, two=2)
        for tt in range(T // 512):
            P.dma("sp", x1[:], S.x1T.rearrange("(kc p) t -> p kc t", p=128)[:, :, tt * 512:(tt + 1) * 512],
                  reads=[SB["x1T"]], writes=[x1], sem="ld_x1")
            for fc in range(NP):
                if fc == NP // 2:
                    for tb in range(4):
                        r0 = tt * 512 + tb * 128
                        P.dma("sp", yt[:, tb, :], S.xr1[r0:r0 + 128, :], reads=[SB["xr1"]], writes=[yt], sem="ld_ytf")
                if fc % 2 == 0:
                    wb = wu[wui % 2]
                    wui += 1
                    if PRECAST:
                        P.dma("sp", wb[:], S.wupb[fc // 2], reads=[SB["wupb"]], writes=[wb], sem=f"ld_wu{(wui - 1) % 2}")
                    else:
                        ncol = min(256, DFF - fc * 128)
                        for two in range(2):
                            P.dma("pool", wb[:, two, :, 0:ncol], wupv[:, two, :, fc * 128:fc * 128 + ncol], writes=[wb],
                                  sem=f"ld_wu{(wui - 1) % 2}")
                co = (fc % 2) * 128
                pg, pv = C.ps[pi % 4], C.ps[(pi + 1) % 4]
                pi += 2
                for two, ps in ((0, pg), (1, pv)):
                    for kc in range(16):
                        P.mm(ps[:], wb[:, two, kc, co:co + 128], x1[:, kc, :], kc == 0, kc == 15, [wb, x1], [ps])
                h_ = hb[0]
                c_ = cv[0]
                hi += 1
                for two, ps in ((0, pg), (1, pv)):
                    ch = two * NP + fc
                    P.cp("pool", h_[:, two, 0:2], hal[:, ch, :], [hal], [h_])
                    P.cp("act", h_[:, two, 2:514], ps[:], [ps], [h_])
                    P.cp("pool", hal[:, ch, :], h_[:, two, 512:514], [h_], [hal])
                    P.ts("dve", c_[:, two, :], h_[:, two, 2:514], fcw[:, ch, 2:3], fcb[:, ch:ch + 1], ALU.mult, ALU.add,
                         [h_, fcw, fcb], [c_])
                    P.stt("dve", c_[:, two, :], h_[:, two, 1:513], fcw[:, ch, 1:2], c_[:, two, :], ALU.mult, ALU.add,
                          [h_, fcw, c_], [c_])
                    P.stt("dve", c_[:, two, :], h_[:, two, 0:512], fcw[:, ch, 0:1], c_[:, two, :], ALU.mult, ALU.add,
                          [h_, fcw, c_], [c_])
                P.act(sg[:], c_[:, 0, :], AF.Silu, [c_], [sg])
                P.tt("pool", aT[:, fc, :], sg[:], c_[:, 1, :], ALU.mult, [sg, c_], [aT])
            for cg in range(D // 256):
                wb = wd[wdi % 2]
                wdi += 1
                if PRECAST:
                    P.dma("sp", wb[:], S.wdnb[cg], reads=[SB["wdnb"]], writes=[wb], sem=f"ld_wd{(wdi - 1) % 2}")
                else:
                    for q, (k0, k1) in enumerate(((0, 22), (22, NP))):
                        P.dma("pool", wb[:, k0:k1, :], w.wdn.rearrange("(kc p) n -> p kc n", p=128)[:, k0:k1, cg * 256:(cg + 1) * 256],
                              writes=[wb], sem=f"ld_wd{(wdi - 1) % 2}")
                for tb in range(4):
                    ps = C.ps[pi % 4]
                    pi += 1
                    for kc in range(NP):
                        P.mm(ps[:, 0:256], aT[:, kc, tb * 128:(tb + 1) * 128], wb[:, kc, :], kc == 0, kc == NP - 1, [aT, wb], [ps])
                    ysl = yt[:, tb, cg * 256:(cg + 1) * 256]
                    P.stt("dve", ysl, ysl, ALPHA, ps[:, 0:256], ALU.mult, ALU.add, [yt, ps], [yt])
            for tb in range(4):
                r0 = tt * 512 + tb * 128
                yv = _YV(yt, tb)
                layer_norm_tile(C, yv, st6, mv, sc, lg, lb)
                P.dma("pool", xout[r0:r0 + 128, :], yt[:, tb, :], reads=[yt], writes=[xout_b], sem="st_yf")


class _YV:
    def __init__(self, t, tb):
        self.t, self.tb, self.b = t, tb, t.b

    def __getitem__(self, k):
        v = self.t.t[:, self.tb, :]
        if isinstance(k, tuple):
            return v[k]
        return v


def precast_ffn(C, w, stg):
    P, S, SB = C.P, C.S, C.SB
    i = 0
    wupv = w.wup.rearrange("(kc p) n -> p kc n", p=128)
    for two in range(2):
        for c0 in range(0, DFF, 512):
            ncol = min(512, DFF - c0)
            st = stg[i % 2]
            sl = i % 2
            i += 1
            P.dma("pool", st[:, :, 0:ncol], wupv[:, :, two * DFF + c0:two * DFF + c0 + ncol], writes=[st], sem=f"ld_pc{sl}")
            yield
            for j in range(0, ncol, 256):
                n2 = min(256, ncol - j)
                pr = (c0 + j) // 256
                P.dma("sp", S.wupb[pr, :, two, :, 0:n2], st[:, :, j:j + n2], reads=[st], writes=[SB["wupb"]], sem=f"st_pc{sl}")
            yield
    wdnv = w.wdn.rearrange("(kc p) n -> p kc n", p=128)
    for c0 in range(0, D, 512):
        for k0 in range(0, 43, 16):
            nk = min(16, 43 - k0)
            st = stg[i % 2]
            sl = i % 2
            i += 1
            P.dma("pool", st[:, 0:nk, :], wdnv[:, k0:k0 + nk, c0:c0 + 512], writes=[st], sem=f"ld_pc{sl}")
            yield
            for j in range(2):
                cg = c0 // 256 + j
                P.dma("sp", S.wdnb[cg, :, k0:k0 + nk, :], st[:, 0:nk, j * 256:(j + 1) * 256], reads=[st], writes=[SB["wdnb"]],
                      sem=f"st_pc{sl}")
            yield
    for wsrc, dst, nm, nkc in ((w.wpa, S.wpab, "wpab", 8), (w.wpb, S.wpbb, "wpbb", 8), (w.wpc, S.wpcb, "wpcb", 8), (w.wout, S.wob, "wob", 16)):
        wv = wsrc.rearrange("(kc p) n -> p kc n", p=128)
        for c0 in range(0, D, 512):
            st = stg[i % 2]
            sl = i % 2
            i += 1
            P.dma("pool", st[:, 0:nkc, :], wv[:, :, c0:c0 + 512], writes=[st], sem=f"ld_pc{sl}")
            yield
            P.dma("sp", dst[:, :, c0:c0 + 512], st[:, 0:nkc, :], reads=[st], writes=[SB[nm]], sem=f"st_pc{sl}")
            yield


D_HEADS = 8
D_NB = None
D_LEVEL = 5
PRECAST = True


def stageD(C, w, npar=2):
    P, S, SB, T, NB = C.P, C.S, C.SB, C.T, C.NB
    NBH = NB * 8
    with Scope(P):
        sma = P.sb("sma", [128, NB, 8], F32)
        smb = P.sb("smb", [128, NB, 8], F32)
        hb8 = P.sb("hb8", [128, 2, 8], F32)
        nega = P.sb("nega", [128, 8], F32)
        gg_ = P.sb("g", [128, NB, 8], F32)
        gc = P.sb("gc", [128, NB, 8], F32)
        egc = P.sb("egc", [128, NB, 8], F32)
        kdec = P.sb("kdec", [128, NB, 8], F32)
        beta = P.sb("beta", [128, NB, 8], F32)
        nbeta = P.sb("nbeta", [128, NB, 8], F32)
        eglb = P.sb("eglb", [128, 2, NBH], F32)
        ngb = P.sb("ngb", [128, 128], F32)
        gcw = P.sb("gcw", [128, 24, 4], F32)
        f2 = lambda t: t[:].rearrange("p b h -> p (b h)")
        P.dma("sp", sma[:], S.sm.rearrange("(b p) c -> p b c", p=128)[:, :, 8:16], reads=[SB["sm"]], writes=[sma], sem="ld_s0")
        P.dma("sp", smb[:], S.sm.rearrange("(b p) c -> p b c", p=128)[:, :, 16:24], reads=[SB["sm"]], writes=[smb], sem="ld_s1")
        P.dma("sp", hb8[:].rearrange("p a h -> p (a h)"), w.bc[:, BC["alog"][0]:BC["alog"][0] + 16], writes=[hb8], sem="ld_s2")
        P.dma("sp", ngb[:], w.bc[:, BC["gng"][0]:BC["gng"][0] + 128], writes=[ngb], sem="ld_s3")
        P.dma("sp", gcw[:].rearrange("p c k -> p (c k)"), w.pc[:, PC["gcw"][0]:PC["gcw"][0] + 96], writes=[gcw], sem="ld_s4")
        P.act(nega[:], hb8[:, 0, :], AF.Exp, [hb8], [nega])
        P.ts("dve", nega[:], nega[:], -1.0, None, ALU.mult, None, [nega], [nega])
        P.tt("dve", sma[:], sma[:], hb8[:, 1, :].unsqueeze(1).to_broadcast([128, NB, 8]), ALU.add, [sma, hb8], [sma])
        P.act(sma[:], sma[:], AF.Exp, [sma], [sma])
        P.act(sma[:], sma[:], AF.Ln, [sma], [sma], bias=1.0)
        P.tt("dve", gg_[:], sma[:], nega[:].unsqueeze(1).to_broadcast([128, NB, 8]), ALU.mult, [sma, nega], [gg_])
        P.act(beta[:], smb[:], AF.Sigmoid, [smb], [beta])
        P.ts("dve", nbeta[:], beta[:], -1.0, None, ALU.mult, None, [beta], [nbeta])
        p0, p1 = C.ps[0], C.ps[1]
        P.mm(p0[:, 0:NBH], C.k32("triBD"), f2(gg_), True, True, [C.c32, gg_], [p0])
        P.mm(p0[:, NBH:2 * NBH], C.k32("blk"), f2(gg_), True, True, [C.c32, gg_], [p0])
        P.mm(p1[:, 0:NBH], C.k32("half0"), f2(gg_), True, True, [C.c32, gg_], [p1])
        P.mm(p1[:, NBH:2 * NBH], C.k32("half1"), f2(gg_), True, True, [C.c32, gg_], [p1])
        P.cp("dve", f2(gc), p0[:, 0:NBH], [p0], [gc])
        P.act(f2(egc), p0[:, 0:NBH], AF.Exp, [p0], [egc])
        P.tt("dve", f2(kdec), p0[:, NBH:2 * NBH], f2(gc), ALU.subtract, [p0, gc], [kdec])
        P.act(f2(kdec), f2(kdec), AF.Exp, [kdec], [kdec])
        P.act(eglb[:].rearrange("p c n -> p (c n)"), p1[:, 0:2 * NBH], AF.Exp, [p1], [eglb])
        P.barrier()

        chains = []
        for ci in range(npar):
            R = Ctx()
            R.ci = ci
            R.raw = P.sb("raw", [128, T + 3], BF16)
            R.acc = P.sb("acc", [128, T], F32)
            R.cvT = P.sb("cvT", [128, 3, T], BF16)
            R.ggh = P.sb("ggh", [128, NB, 128], BF16)
            R.ycT = P.sb("ycT", [128, T], BF16)
            R.S32 = P.sb("S32", [128, 128], F32)
            R.Sbf = P.sb("Sbf", [128, 128], BF16)
            R.jk = P.sb("jk", [128, 128], F32)
            R.qsq = P.sb("qsq", [128, 128], BF16)
            R.sc = P.sb("sc", [128, 8], F32)
            R.kn = P.sb("kn", [128, 128], BF16)
            R.kb = P.sb("kb", [128, 128], BF16)
            R.kg = P.sb("kg", [128, 128], BF16)
            R.kd = P.sb("kd", [128, 128], BF16)
            R.knT = P.sb("knT", [128, 128], BF16)
            R.kbT = P.sb("kbT", [128, 128], BF16)
            R.vtm = P.sb("vtm", [128, 128], BF16)
            R.gU = P.sb("gU", [128, 128], F32)
            R.E2 = P.sb("E2", [128, 256], F32)
            R.Em = P.sb("Em", [128, 3, 128], F32)
            R.N = [P.sb(f"N{i}", [128, 128], BF16) for i in range(2)]
            R.NT = [P.sb(f"NT{i}", [128, 128], BF16) for i in range(2)]
            R.qkT = P.sb("qkT", [128, 128], BF16)
            R.X = [P.sb(f"X{i}", [128, 128], BF16) for i in range(2)]
            R.ub = P.sb("ub", [128, 128], F32)
            R.wT = P.sb("wT", [128, 128], BF16)
            R.vnew = P.sb("vnew", [128, 128], BF16)
            R.t1 = P.sb("t1", [128, 128], F32)
            R.o = P.sb("o", [128, 128], F32)
            R.y1 = P.sb("y1", [128, 128], BF16)
            bA, bB, bC = C.ps[ci * 3], C.ps[ci * 3 + 1], C.ps[ci * 3 + 2]
            pb = C.pb[ci]
            R.pA, R.pB, R.pC, R.pb = bA, bB, bC, pb
            R.rg = {"kk": bA.b, "ssq": bA.b, "dd": bB.b, "dbl": bA.b, "x": bC.b, "ws": bC.b, "qs": bC.b, "qv": bC.b,
                    "tk": pb.b, "tn": pb.b, "tv": pb.b, "ty": pb.b}
            chains.append(R)

        def chain(R, heads):
            ci = R.ci
            rg = R.rg
            pA, pB, pC, pb = R.pA, R.pB, R.pC, R.pb
            for h in heads:
                P.op("pool", lambda e: e.memset(R.raw[:, 0:3], 0.0), writes=[R.raw])
                P.dma("sp", R.ggh[:], S.gg.rearrange("(b p) e -> p b e", p=128)[:, :, h * 128:(h + 1) * 128],
                      reads=[SB["gg"]], writes=[R.ggh], sem=f"ld_ggh{ci}")
                yield
                for i in range(3):
                    r0 = i * 1024 + h * 128
                    P.dma("sp", R.raw[:, 3:3 + T], S.g3T[r0:r0 + 128, :], reads=[SB["g3T"]], writes=[R.raw], sem=f"ld_raw{ci}")
                    chn = i * 8 + h
                    P.ts("dve", R.acc[:], R.raw[:, 3:3 + T], gcw[:, chn, 3:4], None, ALU.mult, None, [R.raw, gcw], [R.acc])
                    yield
                    for tap in (2, 1, 0):
                        eng = "dve"
                        P.stt(eng, R.acc[:], R.raw[:, tap:tap + T], gcw[:, chn, tap:tap + 1], R.acc[:], ALU.mult, ALU.add,
                              [R.raw, gcw, R.acc], [R.acc])
                        yield
                    P.act(R.cvT[:, i, :], R.acc[:], AF.Silu, [R.acc], [R.cvT])
                    yield
                P.act(R.ggh[:], R.ggh[:], AF.Silu, [R.ggh], [R.ggh])
                P.tt("pool", R.ggh[:], R.ggh[:], ngb[:].unsqueeze(1).to_broadcast([128, NB, 128]), ALU.mult, [R.ggh, ngb], [R.ggh])
                P.op("pool", lambda e: e.memset(R.S32[:], 0.0), writes=[R.S32])
                P.op("pool", lambda e: e.memset(R.Sbf[:], 0.0), writes=[R.Sbf])
                yield
                qcT, kcT, vcT = R.cvT[:, 0, :], R.cvT[:, 1, :], R.cvT[:, 2, :]
                for b in range(NB if D_NB is None else D_NB):
                    if D_LEVEL < 2.1:
                        break
                    cols = slice(b * 128, (b + 1) * 128)
                    bh = b * 8 + h
                    col = lambda t: t[:, b, h:h + 1]
                    P.tr(pb[:, 0:128], kcT[:, cols], C.kbf("ident"), [R.cvT, C.cbf], [rg["tk"]])
                    P.tr(pb[:, 384:512], vcT[:, cols], C.kbf("ident"), [R.cvT, C.cbf], [rg["tv"]])
                    yield
                    P.act(R.jk[:], pb[:, 0:128], AF.Square, [rg["tk"]], [R.jk, R.sc], accum_out=R.sc[:, 0:1])
                    P.act(R.sc[:, 1:2], R.sc[:, 0:1], AF.Ln, [R.sc], [R.sc], bias=RMS_EPS)
                    P.act(R.sc[:, 1:2], R.sc[:, 1:2], AF.Exp, [R.sc], [R.sc], scale=-0.5)
                    yield
                    P.ts("dve", R.kn[:], pb[:, 0:128], R.sc[:, 1:2], None, ALU.mult, None, [rg["tk"], R.sc], [R.kn])
                    P.cp("act", R.vtm[:], pb[:, 384:512], [rg["tv"]], [R.vtm])
                    yield
                    if D_LEVEL < 2.3:
                        continue
                    P.act(R.kb[:], R.kn[:], AF.Identity, [R.kn, beta], [R.kb], scale=col(beta))
                    P.act(R.kg[:], R.kn[:], AF.Identity, [R.kn, egc], [R.kg], scale=col(egc))
                    P.act(R.kd[:], R.kn[:], AF.Identity, [R.kn, kdec], [R.kd], scale=col(kdec))
                    P.ts("dve", R.gU[:], C.k32("triBD"), col(gg_), None, ALU.mult, None, [C.c32, gg_], [R.gU])
                    yield
                    if D_LEVEL < 2.35:
                        continue
                    P.tr(pb[:, 128:256], R.kn[:], C.kbf("ident"), [R.kn, C.cbf], [rg["tn"]])
                    P.tr(pb[:, 256:384], R.kb[:], C.kbf("ident"), [R.kb, C.cbf], [rg["tn"]])
                    yield
                    P.cp("act", R.knT[:], pb[:, 128:256], [rg["tn"]], [R.knT])
                    P.cp("dve", R.kbT[:], pb[:, 256:384], [rg["tn"]], [R.kbT])
                    if D_LEVEL < 2.5:
                        continue
                    P.act(R.qsq[:], qcT[:, cols], AF.Square, [R.cvT], [R.qsq])
                    yield
                    P.mm(pA[:, 384:385], R.qsq[:], C.kbf("ones")[:, 0:1], True, True, [R.qsq, C.cbf], [rg["ssq"]])
                    if D_LEVEL < 2.6:
                        continue
                    P.mm(pB[:, 0:128], R.gU[:], C.k32("lm"), True, True, [R.gU, C.c32], [rg["dd"]], signal=False)
                    P.mm(pB[:, 128:256], C.k32("lm"), R.gU[:], True, True, [R.gU, C.c32], [rg["dd"]])
                    P.mm(pA[:, 0:128], R.knT[:], R.kbT[:], True, True, [R.knT, R.kbT], [rg["kk"]], signal=False)
                    P.mm(pA[:, 128:256], R.kbT[:], R.knT[:], True, True, [R.knT, R.kbT], [rg["kk"]], signal=False)
                    P.mm(pA[:, 256:384], R.knT[:], qcT[:, cols], True, True, [R.knT, R.cvT], [rg["kk"]])
                    yield
                    if D_LEVEL < 2.8:
                        continue
                    P.act(R.sc[:, 2:3], pA[:, 384:385], AF.Ln, [rg["ssq"]], [R.sc], bias=RMS_EPS)
                    P.act(R.sc[:, 2:3], R.sc[:, 2:3], AF.Exp, [R.sc], [R.sc], scale=-0.5)
                    P.act(R.E2[:], pB[:, 0:256], AF.Exp, [rg["dd"]], [R.E2])
                    yield
                    P.tt("pool", R.Em[:, 0, :], R.E2[:, 0:128], C.k32("sm"), ALU.mult, [R.E2, C.c32], [R.Em])
                    P.tt("pool", R.Em[:, 1, :], R.E2[:, 128:256], C.k32("smT"), ALU.mult, [R.E2, C.c32], [R.Em])
                    P.tt("pool", R.Em[:, 2, :], R.E2[:, 128:256], C.k32("inclT"), ALU.mult, [R.E2, C.c32], [R.Em])
                    yield
                    P.stt("dve", R.NT[0][:], pA[:, 0:128], -1.0, R.Em[:, 0, :], ALU.mult, ALU.mult, [rg["kk"], R.Em], [R.NT[0]])
                    P.stt("dve", R.N[0][:], pA[:, 128:256], -1.0, R.Em[:, 1, :], ALU.mult, ALU.mult, [rg["kk"], R.Em], [R.N[0]])
                    P.tt("dve", R.qkT[:], pA[:, 256:384], R.Em[:, 2, :], ALU.mult, [rg["kk"], R.Em], [R.qkT])
                    P.tt("pool", R.X[0][:], R.N[0][:], C.kbf("ident"), ALU.add, [R.N[0], C.cbf], [R.X[0]])
                    yield
                    if D_LEVEL < 3.5:
                        continue
                    cur = 0
                    for s in range(5):
                        Nc, NTc = R.N[cur], R.NT[cur]
                        Nn, NTn = R.N[1 - cur], R.NT[1 - cur]
                        if s < 4:
                            P.mm(pA[:, 0:128], NTc[:], Nc[:], True, True, [Nc, NTc], [rg["dbl"]], signal=False)
                        P.mm(pA[:, 128:256], Nc[:], NTc[:], True, True, [Nc, NTc], [rg["dbl"]])
                        yield
                        if D_LEVEL < 3.58:
                            continue
                        if s < 4:
                            P.cp("act", Nn[:], pA[:, 0:128], [rg["dbl"]], [Nn])
                        P.cp("dve", NTn[:], pA[:, 128:256], [rg["dbl"]], [NTn])
                        yield
                        if D_LEVEL < 3.65:
                            continue
                        Xc, Xn = R.X[s % 2], R.X[(s + 1) % 2]
                        P.mm(pC[:, 0:128], NTn[:], Xc[:], True, True, [NTn, Xc], [rg["x"]])
                        yield
                        P.tt("dve", Xn[:], pC[:, 0:128], Xc[:], ALU.add, [rg["x"], Xc], [Xn])
                        yield
                        cur = 1 - cur
                    if D_LEVEL < 3.8:
                        continue
                    Xf = R.X[5 % 2]
                    P.mm(pC[:, 0:128], Xf[:], R.vtm[:], True, True, [Xf, R.vtm], [rg["x"]])
                    yield
                    P.ts("dve", R.ub[:], pC[:, 0:128], col(beta), None, ALU.mult, None, [rg["x"], beta], [R.ub])
                    yield
                    P.mm(pC[:, 0:128], R.kg[:], Xf[:], True, True, [R.kg, Xf], [rg["x"]])
                    yield
                    P.cp("act", R.wT[:], pC[:, 0:128], [rg["x"]], [R.wT])
                    yield
                    if D_LEVEL < 5:
                        continue
                    for c in range(2):
                        r = slice(c * 64, (c + 1) * 64)
                        P.mm(pC[:, 128:256], R.wT[:], R.Sbf[:], True, True, [R.wT, R.Sbf], [rg["ws"]])
                        P.mm(pC[:, 256:384], qcT[:, cols], R.Sbf[:], True, True, [R.cvT, R.Sbf], [rg["qs"]])
                        yield
                        P.stt("dve", R.vnew[r, :], pC[r, 128:256], nbeta[r, b, h:h + 1], R.ub[r, :], ALU.mult, ALU.add,
                              [rg["ws"], nbeta, R.ub], [R.vnew])
                        P.act(R.t1[r, :], pC[r, 256:384], AF.Identity, [rg["qs"], egc], [R.t1], scale=egc[r, b, h:h + 1])
                        yield
                        P.mm(pC[:, 384:512], R.qkT[r, :], R.vnew[r, :], True, True, [R.qkT, R.vnew], [rg["qv"]])
                        yield
                        P.tt("dve", R.o[r, :], pC[r, 384:512], R.t1[r, :], ALU.add, [rg["qv"], R.t1], [R.o])
                        yield
                        P.mm(pC[:, 384:512], R.kd[r, :], R.vnew[r, :], True, True, [R.kd, R.vnew], [rg["qv"]])
                        yield
                        P.stt("dve", R.S32[:], R.S32[:], eglb[:, c, bh:bh + 1], pC[:, 384:512], ALU.mult, ALU.add,
                              [R.S32, eglb, rg["qv"]], [R.S32])
                        P.cp("act", R.Sbf[:], R.S32[:], [R.S32], [R.Sbf])
                        yield
                    P.ts("dve", R.sc[:, 3:4], R.sc[:, 2:3], 128 ** -0.5, None, ALU.mult, None, [R.sc], [R.sc])
                    P.act(R.jk[:], R.o[:], AF.Square, [R.o, R.sc], [R.jk, R.sc], scale=R.sc[:, 3:4], accum_out=R.sc[:, 4:5])
                    P.act(R.sc[:, 5:6], R.sc[:, 4:5], AF.Ln, [R.sc], [R.sc], bias=RMS_EPS, scale=1.0 / 128)
                    P.act(R.sc[:, 5:6], R.sc[:, 5:6], AF.Exp, [R.sc], [R.sc], scale=-0.5)
                    P.tt("dve", R.sc[:, 6:7], R.sc[:, 5:6], R.sc[:, 3:4], ALU.mult, [R.sc], [R.sc])
                    yield
                    P.stt("dve", R.y1[:], R.o[:], R.sc[:, 6:7], R.ggh[:, b, :], ALU.mult, ALU.mult, [R.o, R.sc, R.ggh], [R.y1])
                    yield
                    P.tr(pb[:, 0:128], R.y1[:], C.kbf("ident"), [R.y1, C.cbf], [rg["ty"]])
                    yield
                    P.cp("act", R.ycT[:, cols], pb[:, 0:128], [rg["ty"]], [R.ycT])
                    yield
                P.dma("sp", S.ycT[h * 128:(h + 1) * 128, :], R.ycT[:], reads=[R.ycT], writes=[SB["ycT"]], sem=f"st_yc{ci}")
                yield

        gens = [chain(chains[ci], list(range(ci, D_HEADS, npar))) for ci in range(npar)] if D_LEVEL >= 2 else []
        if PRECAST:
            stg = [P.sb(f"pcs{i}", [128, 16, 512], BF16) for i in range(2)]
            gens.append(precast_ffn(C, w, stg))
        alive = list(gens)
        while alive:
            for g in list(alive):
                try:
                    next(g)
                except StopIteration:
                    alive.remove(g)


_CACHE = {}
FUSED = True


def _prog(nl):
    if nl not in _CACHE:
        _CACHE[nl] = build(nl, SEQ)[0]
    return _CACHE[nl]


def _layer_map(inp, l, slot):
    bc, pc, sguwT = layer_small(inp, l)
    f = lambda a: np.ascontiguousarray(a, dtype=np.float32)
    return {f"w_in{slot}": f(inp["w_in"][l]), f"bc{slot}": bc, f"pc{slot}": pc, f"sguwT{slot}": sguwT,
            f"wpa{slot}": f(inp["w_proj_a"][l]), f"wpb{slot}": f(inp["w_proj_b"][l]), f"wpc{slot}": f(inp["w_proj_c"][l]),
            f"wout{slot}": f(inp["w_out"][l]), f"wup{slot}": f(inp["ffn_w_up"][l]), f"wdn{slot}": f(inp["ffn_w_down"][l])}


def kernel(**inp):
    inp = {k: np.asarray(v) for k, v in inp.items()}
    x = inp["x"].astype(np.float32, copy=False)
    B = x.shape[0]
    consts = make_consts()
    cur = [np.ascontiguousarray(x[b]) for b in range(B)]
    if FUSED:
        nc = _prog(DEPTH)
        wm = {}
        for l in range(DEPTH):
            wm.update(_layer_map(inp, l, l))
        in_maps = [dict(wm, x=cur[b], consts=consts) for b in range(B)]
        res = run_bass_kernel_spmd(nc, in_maps, core_ids=list(range(B)))
        cur = [np.asarray(res.results[b]["y"]) for b in range(B)]
    else:
        nc = _prog(1)
        for l in range(DEPTH):
            wm = _layer_map(inp, l, 0)
            in_maps = [dict(wm, x=cur[b], consts=consts) for b in range(B)]
            res = run_bass_kernel_spmd(nc, in_maps, core_ids=list(range(B)))
            cur = [np.asarray(res.results[b]["y"]) for b in range(B)]
    return np.stack(cur).astype(np.float32)
```
